# Optimizing a Trainium2 kernel written in Bass

```python
import math
import jax, jax.numpy as jnp
from jax import lax
import numpy as np

D_MODEL = 1024
BATCH = 4
SEQ = 4096
DEPTH = 1

MEM_LEN = 256
DA_HEADS = 8
DA_HEAD_DIM = 64
DA_QK = DA_HEADS * 2 * DA_HEAD_DIM
DA_V = DA_HEADS * 2 * DA_HEAD_DIM
Q_BLOCK = 128
SSD_EXPAND = 2
SSD_INNER = SSD_EXPAND * D_MODEL
SSD_HEAD_DIM = 64
SSD_HEADS = SSD_INNER // SSD_HEAD_DIM
SSD_GROUPS = 4
SSD_HEADS_PER_GROUP = SSD_HEADS // SSD_GROUPS
SSD_STATE = 128
SSD_CONV = 4
SSD_CHUNK = 128
SSD_CONV_DIM = SSD_INNER + 2 * SSD_GROUPS * SSD_STATE
XA_HEADS = 4
XA_HEAD_DIM = D_MODEL // XA_HEADS
D_FF = 2816
N_BRANCH = 2
IN_SIZES = (DA_QK, DA_QK, DA_V, SSD_INNER, SSD_CONV_DIM, SSD_HEADS, N_BRANCH * D_MODEL)
IN_WIDTH = DA_QK + DA_QK + DA_V + SSD_INNER + SSD_CONV_DIM + SSD_HEADS + N_BRANCH * D_MODEL
NORM_EPS = 1e-6
SUBLN_EPS = 1e-5

kernel_name = "hybrid_diffattn_ssd_gated_macaron"


def _rmsnorm(x, g, eps=NORM_EPS):
    xf = x.astype(jnp.float32)
    y = xf * lax.rsqrt(jnp.mean(xf * xf, axis=-1, keepdims=True) + eps)
    return (y * g.astype(jnp.float32)).astype(x.dtype)


def _swiglu(h, w_gu, w_down):
    g, u = jnp.split(h @ w_gu, 2, axis=-1)
    return (jax.nn.silu(g) * u) @ w_down


def _diff_attention(q, k, v, lam):
    b, s = q.shape[0], q.shape[1]
    q1, q2 = q[:, :, :, 0].transpose(0, 2, 1, 3), q[:, :, :, 1].transpose(0, 2, 1, 3)
    k1, k2 = k[:, :, :, 0].transpose(0, 2, 1, 3), k[:, :, :, 1].transpose(0, 2, 1, 3)
    vh = v.transpose(0, 2, 1, 3)
    nblk = s // Q_BLOCK
    scale = DA_HEAD_DIM ** -0.5
    key_pos = jnp.arange(s)

    def blocks(t):
        return t.reshape(b, DA_HEADS, nblk, Q_BLOCK, DA_HEAD_DIM).transpose(2, 0, 1, 3, 4)

    def one_block(args):
        q1b, q2b, start = args
        mask = (start + jnp.arange(Q_BLOCK))[:, None] >= key_pos[None, :]

        def probs(qb, kk):
            sc = jnp.einsum('bhqd,bhkd->bhqk', qb, kk).astype(jnp.float32) * scale
            return jax.nn.softmax(jnp.where(mask, sc, -jnp.inf), axis=-1)

        p = probs(q1b, k1) - lam * probs(q2b, k2)
        return jnp.einsum('bhqk,bhke->bhqe', p.astype(vh.dtype), vh)

    out = lax.map(one_block, (blocks(q1), blocks(q2), jnp.arange(nblk, dtype=jnp.int32) * Q_BLOCK))
    return out.transpose(1, 2, 0, 3, 4).reshape(b, DA_HEADS, s, 2 * DA_HEAD_DIM)


def _segsum(a):
    t = a.shape[-1]
    cs = jnp.cumsum(a, axis=-1)
    seg = cs[..., :, None] - cs[..., None, :]
    return jnp.where(jnp.tril(jnp.ones((t, t), dtype=bool)), seg, -jnp.inf)


def _ssd_chunked(xdt, adt, bm, cm):
    b, s = xdt.shape[0], xdt.shape[1]
    c = s // SSD_CHUNK
    G, R, P, N, Lc = SSD_GROUPS, SSD_HEADS_PER_GROUP, SSD_HEAD_DIM, SSD_STATE, SSD_CHUNK
    X = xdt.astype(jnp.float32).reshape(b, c, Lc, G, R, P)
    A = adt.astype(jnp.float32).reshape(b, c, Lc, G, R).transpose(0, 3, 4, 1, 2)
    Bc = bm.astype(jnp.float32).reshape(b, c, Lc, G, N)
    Cc = cm.astype(jnp.float32).reshape(b, c, Lc, G, N)
    a_cs = jnp.cumsum(A, axis=-1)
    Lmat = jnp.exp(_segsum(A))
    cb = jnp.einsum('bclgn,bcsgn->bgcls', Cc, Bc)
    y_diag = jnp.einsum('bgrcls,bcsgrp->bclgrp', cb[:, :, None] * Lmat, X)
    decay_states = jnp.exp(a_cs[..., -1:] - a_cs)
    states = jnp.einsum('bclgn,bgrcl,bclgrp->bcgrpn', Bc, decay_states, X)
    chunk_end = jnp.pad(a_cs[..., -1], ((0, 0), (0, 0), (0, 0), (1, 0)))
    decay_chunk = jnp.exp(_segsum(chunk_end))
    states = jnp.concatenate([jnp.zeros_like(states[:, :1]), states], axis=1)
    prev_states = jnp.einsum('bgrzc,bcgrpn->bzgrpn', decay_chunk, states)[:, :-1]
    y_off = jnp.einsum('bclgn,bcgrpn,bgrcl->bclgrp', Cc, prev_states, jnp.exp(a_cs))
    return (y_diag + y_off).reshape(b, s, SSD_HEADS, P)


def _depthwise_causal_conv(u, w, bias):
    y = lax.conv_general_dilated(u, w.astype(u.dtype)[:, None, :], window_strides=(1,),
                                 padding=[(SSD_CONV - 1, 0)],
                                 dimension_numbers=('NWC', 'WIO', 'NWC'),
                                 feature_group_count=u.shape[-1])
    return y + bias.astype(u.dtype)


def _layer(x, mem, layer_idx,
           ffn1_pre_g, ffn1_post_g, ffn1_w_gu, ffn1_w_down,
           mix_pre_g, mix_post_g, w_in, b_gate,
           da_lambda_q1, da_lambda_k1, da_lambda_q2, da_lambda_k2, da_subln_g,
           ssd_conv_w, ssd_conv_b, ssd_dt_bias, ssd_A_log, ssd_D, ssd_norm_g,
           w_branch_attn, w_branch_ssd, w_mix_out,
           xa_pre_g, xa_post_g, mem_norm_g, xa_w_q, xa_w_kv, xa_w_o,
           ffn2_pre_g, ffn2_post_g, ffn2_w_gu, ffn2_w_down):
    b, s, _ = x.shape
    x = x + 0.5 * _rmsnorm(_swiglu(_rmsnorm(x, ffn1_pre_g), ffn1_w_gu, ffn1_w_down), ffn1_post_g)

    h = _rmsnorm(x, mix_pre_g)
    cuts = [sum(IN_SIZES[:i + 1]) for i in range(len(IN_SIZES) - 1)]
    q, k, v, z, xbc, dt_raw, gate_logits = jnp.split(h @ w_in, cuts, axis=-1)

    lam_init = 0.8 - 0.6 * math.exp(-0.3 * layer_idx)
    lam = (jnp.exp(jnp.sum(da_lambda_q1 * da_lambda_k1).astype(jnp.float32))
           - jnp.exp(jnp.sum(da_lambda_q2 * da_lambda_k2).astype(jnp.float32)) + lam_init)
    o = _diff_attention(q.reshape(b, s, DA_HEADS, 2, DA_HEAD_DIM),
                        k.reshape(b, s, DA_HEADS, 2, DA_HEAD_DIM),
                        v.reshape(b, s, DA_HEADS, 2 * DA_HEAD_DIM), lam)
    o = _rmsnorm(o, da_subln_g, SUBLN_EPS) * (1.0 - lam_init)
    attn_out = o.transpose(0, 2, 1, 3).reshape(b, s, DA_V) @ w_branch_attn

    xbc = jax.nn.silu(_depthwise_causal_conv(xbc, ssd_conv_w, ssd_conv_b))
    xs, bm, cm = jnp.split(xbc, [SSD_INNER, SSD_INNER + SSD_GROUPS * SSD_STATE], axis=-1)
    dt = jax.nn.softplus(dt_raw.astype(jnp.float32) + ssd_dt_bias.astype(jnp.float32))
    a = -jnp.exp(ssd_A_log.astype(jnp.float32))
    xh = xs.reshape(b, s, SSD_HEADS, SSD_HEAD_DIM)
    y = _ssd_chunked(xh * dt[..., None], a * dt,
                     bm.reshape(b, s, SSD_GROUPS, SSD_STATE), cm.reshape(b, s, SSD_GROUPS, SSD_STATE))
    y = (y + ssd_D.astype(jnp.float32)[:, None] * xh).astype(x.dtype).reshape(b, s, SSD_INNER)
    yg = (y * jax.nn.silu(z)).reshape(b, s, SSD_GROUPS, SSD_INNER // SSD_GROUPS)
    y = _rmsnorm(yg, ssd_norm_g.reshape(SSD_GROUPS, SSD_INNER // SSD_GROUPS), SUBLN_EPS)
    ssd_out = y.reshape(b, s, SSD_INNER) @ w_branch_ssd

    g_attn, g_ssd = jnp.split(jax.nn.sigmoid(gate_logits + b_gate), N_BRANCH, axis=-1)
    mixed = (g_attn * attn_out + g_ssd * ssd_out) @ w_mix_out
    x = x + _rmsnorm(mixed, mix_post_g)

    hq = _rmsnorm(x, xa_pre_g)
    qx = (hq @ xa_w_q).reshape(b, s, XA_HEADS, XA_HEAD_DIM)
    kx, vx = jnp.split(_rmsnorm(mem, mem_norm_g) @ xa_w_kv, 2, axis=-1)
    m = mem.shape[1]
    kx = kx.reshape(b, m, XA_HEADS, XA_HEAD_DIM)
    vx = vx.reshape(b, m, XA_HEADS, XA_HEAD_DIM)
    sc = jnp.einsum('bqhd,bkhd->bhqk', qx, kx).astype(jnp.float32) * (XA_HEAD_DIM ** -0.5)
    p = jax.nn.softmax(sc, axis=-1).astype(vx.dtype)
    xo = jnp.einsum('bhqk,bkhd->bqhd', p, vx).reshape(b, s, D_MODEL) @ xa_w_o
    x = x + _rmsnorm(xo, xa_post_g)

    x = x + 0.5 * _rmsnorm(_swiglu(_rmsnorm(x, ffn2_pre_g), ffn2_w_gu, ffn2_w_down), ffn2_post_g)
    return x


def setup_inputs(seed: int = 0) -> dict:
    key = jax.random.key(seed)
    ks = iter(jax.random.split(key, 48))
    L = DEPTH

    def nrm(shape, scale):
        return scale * jax.random.normal(next(ks), shape, jnp.float32)

    def gain(n):
        return 1.0 + 0.02 * jax.random.normal(next(ks), (L, n), jnp.float32)

    inp = {}
    inp["x"] = nrm((BATCH, SEQ, D_MODEL), 1.0)
    inp["mem"] = nrm((BATCH, MEM_LEN, D_MODEL), 1.0)
    inp["ffn1_pre_g"] = gain(D_MODEL)
    inp["ffn1_post_g"] = gain(D_MODEL)
    inp["ffn1_w_gu"] = nrm((L, D_MODEL, 2 * D_FF), D_MODEL ** -0.5)
    inp["ffn1_w_down"] = nrm((L, D_FF, D_MODEL), D_FF ** -0.5)
    inp["mix_pre_g"] = gain(D_MODEL)
    inp["mix_post_g"] = gain(D_MODEL)
    inp["w_in"] = nrm((L, D_MODEL, IN_WIDTH), D_MODEL ** -0.5)
    inp["b_gate"] = nrm((L, N_BRANCH * D_MODEL), 0.02)
    inp["da_lambda_q1"] = nrm((L, DA_HEAD_DIM), 0.1)
    inp["da_lambda_k1"] = nrm((L, DA_HEAD_DIM), 0.1)
    inp["da_lambda_q2"] = nrm((L, DA_HEAD_DIM), 0.1)
    inp["da_lambda_k2"] = nrm((L, DA_HEAD_DIM), 0.1)
    inp["da_subln_g"] = gain(2 * DA_HEAD_DIM)
    inp["ssd_conv_w"] = nrm((L, SSD_CONV, SSD_CONV_DIM), SSD_CONV ** -0.5)
    inp["ssd_conv_b"] = nrm((L, SSD_CONV_DIM), 0.02)
    u = jax.random.uniform(next(ks), (L, SSD_HEADS), jnp.float32)
    dt0 = jnp.exp(u * (math.log(0.1) - math.log(0.001)) + math.log(0.001))
    inp["ssd_dt_bias"] = dt0 + jnp.log(-jnp.expm1(-dt0))
    inp["ssd_A_log"] = jnp.log(jax.random.uniform(next(ks), (L, SSD_HEADS), jnp.float32, 1.0, 16.0))
    inp["ssd_D"] = 1.0 + nrm((L, SSD_HEADS), 0.1)
    inp["ssd_norm_g"] = gain(SSD_INNER)
    inp["w_branch_attn"] = nrm((L, DA_V, D_MODEL), DA_V ** -0.5)
    inp["w_branch_ssd"] = nrm((L, SSD_INNER, D_MODEL), SSD_INNER ** -0.5)
    inp["w_mix_out"] = nrm((L, D_MODEL, D_MODEL), D_MODEL ** -0.5)
    inp["xa_pre_g"] = gain(D_MODEL)
    inp["xa_post_g"] = gain(D_MODEL)
    inp["mem_norm_g"] = gain(D_MODEL)
    inp["xa_w_q"] = nrm((L, D_MODEL, D_MODEL), D_MODEL ** -0.5)
    inp["xa_w_kv"] = nrm((L, D_MODEL, 2 * D_MODEL), D_MODEL ** -0.5)
    inp["xa_w_o"] = nrm((L, D_MODEL, D_MODEL), D_MODEL ** -0.5)
    inp["ffn2_pre_g"] = gain(D_MODEL)
    inp["ffn2_post_g"] = gain(D_MODEL)
    inp["ffn2_w_gu"] = nrm((L, D_MODEL, 2 * D_FF), D_MODEL ** -0.5)
    inp["ffn2_w_down"] = nrm((L, D_FF, D_MODEL), D_FF ** -0.5)
    return inp


def reference(x, mem, ffn1_pre_g, ffn1_post_g, ffn1_w_gu, ffn1_w_down,
              mix_pre_g, mix_post_g, w_in, b_gate,
              da_lambda_q1, da_lambda_k1, da_lambda_q2, da_lambda_k2, da_subln_g,
              ssd_conv_w, ssd_conv_b, ssd_dt_bias, ssd_A_log, ssd_D, ssd_norm_g,
              w_branch_attn, w_branch_ssd, w_mix_out,
              xa_pre_g, xa_post_g, mem_norm_g, xa_w_q, xa_w_kv, xa_w_o,
              ffn2_pre_g, ffn2_post_g, ffn2_w_gu, ffn2_w_down):
    for l in range(DEPTH):
        x = _layer(x, mem, l,
                   ffn1_pre_g[l], ffn1_post_g[l], ffn1_w_gu[l], ffn1_w_down[l],
                   mix_pre_g[l], mix_post_g[l], w_in[l], b_gate[l],
                   da_lambda_q1[l], da_lambda_k1[l], da_lambda_q2[l], da_lambda_k2[l], da_subln_g[l],
                   ssd_conv_w[l], ssd_conv_b[l], ssd_dt_bias[l], ssd_A_log[l], ssd_D[l], ssd_norm_g[l],
                   w_branch_attn[l], w_branch_ssd[l], w_mix_out[l],
                   xa_pre_g[l], xa_post_g[l], mem_norm_g[l], xa_w_q[l], xa_w_kv[l], xa_w_o[l],
                   ffn2_pre_g[l], ffn2_post_g[l], ffn2_w_gu[l], ffn2_w_down[l])
    return x
```

```python
import numpy as np
import ml_dtypes
import concourse.bass as bass
import concourse.mybir as mybir
from concourse.bass_utils import run_bass_kernel_spmd

F32 = mybir.dt.float32
BF16 = mybir.dt.bfloat16
AF = mybir.ActivationFunctionType
ALU = mybir.AluOpType

D = 1024
DFF = 2816
NFF = DFF // 128
SEQ = 4096
G = 512
NH = 8
SH = 32
SG = 4
MEM = 256
NEG = -30000.0

PV = {}
_o = 0
for _n, _w in [("ffn1_pre", 8), ("ffn1_post", 8), ("mix_pre", 8), ("mix_post", 8), ("xa_pre", 8),
               ("xa_post", 8), ("mem_g", 8), ("ffn2_pre", 8), ("ffn2_post", 8), ("b_gate", 16),
               ("subln", 1), ("conv_w", 96), ("conv_b", 24), ("ssd_g", 16)]:
    PV[_n] = _o
    _o += _w
NPV = _o
NROW = 96 + 256
NCONST = 256 + 2048


class Buf:
    __slots__ = ("name", "w", "r", "dsem", "dcnt")

    def __init__(self, name, prior=None):
        self.name = name
        self.w = None
        self.r = list(prior) if prior else []
        self.dsem = None
        self.dcnt = 0

    def tokens(self):
        t = list(self.r)
        if self.w is not None:
            t.append(self.w)
        return t


class BL(list):
    pass


class Op:
    __slots__ = ("eng", "fn", "deps", "dwaits", "need", "ticket", "dinc")

    def __init__(self, eng, fn):
        self.eng = eng
        self.fn = fn
        self.deps = []
        self.dwaits = []
        self.need = False
        self.ticket = None
        self.dinc = None


class Sched:
    ENGS = ("pe", "act", "dve", "pool", "sp")

    def __init__(self, nc):
        self.nc = nc
        self.ops = {e: [] for e in self.ENGS}
        self.nsem = 0

    def _dep(self, op, tok, kind):
        if isinstance(tok, Op):
            if tok.eng == op.eng:
                if op.eng == "pe" or kind != "raw":
                    return
            tok.need = True
            op.deps.append(tok)
        else:
            sb = tok[1]
            op.dwaits.append((sb, sb.dcnt))

    @staticmethod
    def _flat(bs):
        out = []
        for b in bs:
            if isinstance(b, (list, tuple)):
                out.extend(Sched._flat(b))
            else:
                out.append(b)
        return out

    def _track(self, op, tok, reads, writes):
        reads = self._flat(reads)
        writes = self._flat(writes)
        for b in reads:
            if b.w is not None:
                self._dep(op, b.w, "raw")
        for b in writes:
            if b.w is not None:
                self._dep(op, b.w, "waw")
            for t in b.r:
                if t is not op:
                    self._dep(op, t, "war")
        key = tok.eng if isinstance(tok, Op) else id(tok[1])
        for b in reads:
            b.r = [t for t in b.r if (t.eng if isinstance(t, Op) else id(t[1])) != key]
            b.r.append(tok)
        for b in writes:
            b.w = tok
            b.r = []

    def op(self, eng, fn, reads=(), writes=()):
        o = Op(eng, fn)
        for b in self._flat(reads):
            if b.name.startswith("psb"):
                for t in b.r:
                    assert not isinstance(t, Op) or t.eng == eng, ("two engines read one PSUM bank", b.name, eng, t.eng)
        self._track(o, o, reads, writes)
        self.ops[eng].append(o)
        return o

    def dma(self, q, out, in_, sb, reads=(), writes=()):
        o = Op(q, lambda e: e.dma_start(out=out, in_=in_))
        if sb.dsem is None:
            sb.dsem = self.nc.alloc_semaphore(name="d%d_%s" % (self.nsem, sb.name))
            self.nsem += 1
        tok = ("d", sb)
        self._track(o, tok, reads, writes)
        sb.dcnt += 16
        o.dinc = sb
        self.ops[q].append(o)
        return o

    def emit(self, final_waits):
        nc = self.nc
        sems = {e: nc.alloc_semaphore(name="eng_" + e) for e in self.ENGS}
        for e in self.ENGS:
            t = 0
            for o in self.ops[e]:
                if o.need:
                    t += 1
                    o.ticket = t
        handles = {"pe": "tensor", "act": "scalar", "dve": "vector", "pool": "gpsimd", "sp": "sync"}
        with nc.Block() as block:
            for e in self.ENGS:
                ops = self.ops[e]
                fw = final_waits if e == "sp" else []

                def body(eng, ops=ops, e=e, fw=fw):
                    waited = {}
                    for o in ops:
                        need = {}
                        for d in o.deps:
                            k = ("e", d.eng)
                            if d.ticket > need.get(k, (None, 0))[1]:
                                need[k] = (sems[d.eng], d.ticket)
                        for sb, v in o.dwaits:
                            k = ("d", id(sb))
                            if v > need.get(k, (None, 0))[1]:
                                need[k] = (sb.dsem, v)
                        for k, (s, v) in need.items():
                            if waited.get(k, 0) < v:
                                eng.wait_ge(s, v)
                                waited[k] = v
                        ins = o.fn(eng)
                        if o.dinc is not None:
                            ins.then_inc(o.dinc.dsem, 16)
                        elif o.ticket is not None:
                            ins.then_inc(sems[e], 1)
                    for sb in fw:
                        eng.wait_ge(sb.dsem, sb.dcnt)

                getattr(block, handles[e])(body)


def build_program(T, TP=0, stages=("ab", "c", "d", "e"), debug=False):
    NG_ = T // G
    NPG = TP // G
    KVT = TP + T
    NKG = KVT // G
    NCH = KVT // 128
    nc = bass.Bass("TRN2", target_bir_lowering=False)
    S = Sched(nc)
    okind = "ExternalOutput" if debug else "Internal"

    def din(name, shape, dt=F32):
        return nc.dram_tensor(name, list(shape), dt, kind="ExternalInput").ap()

    def dscr(name, shape, dt):
        return nc.dram_tensor(name, list(shape), dt, kind=okind).ap()

    xT_d = din("xT", [D, T])
    xTp_d = din("xTp", [D, max(TP, G)])
    flags_d = din("flags", [128, 2])
    memT_d = din("memT", [D, MEM])
    pvec_d = din("pvec", [128, NPV])
    row_d = din("row", [1, NROW])
    const_d = din("consts", [128, NCONST])
    wgu_d = [din("wgu%d" % i, [NFF, 128, 2048]) for i in (1, 2)]
    wdn_d = [din("wdn%d" % i, [8, 128, DFF]) for i in (1, 2)]
    winf_d = din("winf", [56, 128, 1024])
    wint_d = din("wint", [6, 128, 4096])
    wdt_d = din("wdt", [128, 256])
    wa_d = din("wa", [8, 128, 1024])
    ws_d = din("ws", [8, 128, 2048])
    wmix_d = din("wmix", [8, 128, 1024])
    wxq_d = din("wxq", [8, 128, 1024])
    wxk_d = din("wxk", [8, 128, 1024])
    wxv_d = din("wxv", [2, 128, 4096])
    wxo_d = din("wxo", [8, 128, 1024])
    out_d = nc.dram_tensor("outT", [D, T], F32, kind="ExternalOutput").ap()

    XS_d = dscr("XS", [D, T], F32)
    _W = {}
    QT_d = dscr("QT", [D, T], BF16)
    KT_d = dscr("KT", [D, KVT], BF16)
    V_d = dscr("V", [KVT, D], BF16)
    Z_d = dscr("Z", [T, 2048], BF16)
    XBC_d = dscr("XBC", [3072, KVT], BF16)
    DT_d = dscr("DT", [KVT, 32], F32)
    GT_d = dscr("GT", [2048, T], BF16)
    ON_d = dscr("ON", [D, T], BF16)
    YN_d = dscr("YN", [2048, T], BF16)

    def dbufs(name, n):
        return [Buf("%s%d" % (name, i)) for i in range(n)]
    XS_b, Q_b, Z_b, GT_b, ON_b, YN_b = [dbufs(n, NG_) for n in ("XS", "Q", "Z", "GT", "ON", "YN")]
    K_b, V_b, XBC_b, DT_b = [dbufs(n, NKG) for n in ("K", "V", "XBC", "DT")]

    A32 = nc.alloc_sbuf_tensor("A32", [128, 12288], F32)
    A16 = nc.alloc_sbuf_tensor("A16", [128, 57344], BF16)
    CST = nc.alloc_sbuf_tensor("CST", [128, NPV + NROW + NCONST + 64], F32)
    C16 = nc.alloc_sbuf_tensor("C16", [128, 128 * 4 + 2048 + 32 * 128], BF16)
    PS = [nc.alloc_psum_tensor("ps%d" % i, [128, 512], F32) for i in range(8)]

    class Arena:
        def __init__(self, t, prior):
            self.t = t
            self.off = 0
            self.prior = prior
            self.bufs = []

        def take(self, name, n):
            ap = self.t[:, self.off:self.off + n]
            self.off += n
            assert self.off <= self.t.shape[1], (name, self.off)
            b = Buf(name, self.prior)
            self.bufs.append(b)
            return ap, b

        def take_chunks(self, name, nch, width):
            ap, b0 = self.take(name, nch * width)
            bl = BL([b0] + [Buf("%s_%d" % (name, i), self.prior) for i in range(1, nch)])
            self.bufs.extend(bl[1:])
            return ap, bl

    state = {"prior32": [], "prior16": [], "priorps": []}
    psb = [Buf("psb%d" % i) for i in range(8)]

    def new_phase():
        toks = []
        for b in state.get("bufs", []):
            toks.extend(b.tokens())
        seen = set()
        pr = []
        for t in toks:
            k = id(t) if isinstance(t, Op) else ("d", id(t[1]))
            if k not in seen:
                seen.add(k)
                pr.append(t)
        a32 = Arena(A32, pr)
        a16 = Arena(A16, pr)
        state["a32"], state["a16"] = a32, a16
        return a32, a16

    def end_phase(a32, a16):
        state["bufs"] = a32.bufs + a16.bufs

    pv = CST[:, 0:NPV]
    rowb = CST[:, NPV:NPV + NROW]
    cst = CST[:, NPV + NROW:NPV + NROW + NCONST]
    lamc = CST[:, NPV + NROW + NCONST:NPV + NROW + NCONST + 64]
    cbuf = Buf("consts")
    c16buf = Buf("c16")
    ident32 = cst[:, 0:128]
    tri32 = cst[:, 128:256]
    identb = C16[:, 0:128]
    ones_d = C16[:, 128:256]
    ones_1 = C16[:, 256:384]
    ones_h = C16[:, 384:512]
    maskb = C16[:, 512:512 + 2048]
    Ddiag = C16[:, 2560:2560 + 4096]

    S.dma("sp", CST[:, 0:NPV], pvec_d[:, :], cbuf, writes=[cbuf])
    S.dma("sp", rowb, row_d[0:1, :].partition_broadcast(128), cbuf, writes=[cbuf])
    S.dma("sp", cst, const_d[:, :], cbuf, writes=[cbuf])
    S.dma("sp", lamc[:, 16:18], flags_d[:, :], cbuf, writes=[cbuf])
    fl_valid = lamc[:, 16:17]
    fl_bias = lamc[:, 17:18]
    S.op("dve", lambda e: e.tensor_copy(out=identb, in_=ident32), reads=[cbuf], writes=[c16buf])
    S.op("dve", lambda e: e.memset(ones_d, 1.0 / 1024.0), writes=[c16buf])
    S.op("dve", lambda e: e.memset(ones_1, 1.0), writes=[c16buf])
    S.op("dve", lambda e: e.memset(ones_h, 1.0 / 128.0), writes=[c16buf])
    S.op("dve", lambda e: e.tensor_copy(out=maskb, in_=cst[:, 256:256 + 2048]), reads=[cbuf], writes=[c16buf])
    for h in range(SH):
        S.op("dve", lambda e, h=h: e.tensor_scalar(out=Ddiag[:, h * 128:(h + 1) * 128], in0=ident32,
                                                     scalar1=rowb[:, 64 + h:65 + h], scalar2=None, op0=ALU.mult),
             reads=[cbuf], writes=[c16buf])
    LAM_INIT = 0.8 - 0.6 * 1.0
    lbuf = Buf("lam")
    tmpl = lamc[:, 0:64]
    s1 = lamc[:, 0:1]

    def lam_ops():
        a32, a16 = new_phase()
        t1, t1b = a32.take("lt1", 64)
        t2, t2b = a32.take("lt2", 64)
        acc, accb = a32.take("lacc", 4)
        S.op("dve", lambda e: e.tensor_tensor(out=t1, in0=rowb[:, 96:160], in1=rowb[:, 160:224], op=ALU.mult),
             reads=[cbuf], writes=[t1b])
        S.op("dve", lambda e: e.reduce_sum(out=acc[:, 0:1], in_=t1, axis=mybir.AxisListType.X),
             reads=[t1b], writes=[accb])
        S.op("dve", lambda e: e.tensor_tensor(out=t2, in0=rowb[:, 224:288], in1=rowb[:, 288:352], op=ALU.mult),
             reads=[cbuf], writes=[t2b])
        S.op("dve", lambda e: e.reduce_sum(out=acc[:, 1:2], in_=t2, axis=mybir.AxisListType.X),
             reads=[t2b, accb], writes=[accb])
        S.op("act", lambda e: e.activation(out=acc[:, 2:4], in_=acc[:, 0:2], func=AF.Exp), reads=[accb], writes=[accb])
        S.op("dve", lambda e: e.tensor_tensor(out=lamc[:, 0:1], in0=acc[:, 2:3], in1=acc[:, 3:4], op=ALU.subtract),
             reads=[accb], writes=[lbuf])
        S.op("dve", lambda e: e.tensor_scalar(out=lamc[:, 0:1], in0=lamc[:, 0:1], scalar1=LAM_INIT, scalar2=None,
                                                op0=ALU.add), reads=[lbuf], writes=[lbuf])
        S.op("dve", lambda e: e.tensor_scalar(out=lamc[:, 1:2], in0=pv[:, PV["subln"]:PV["subln"] + 1],
                                                scalar1=1.0 - LAM_INIT, scalar2=None, op0=ALU.mult),
             reads=[cbuf, lbuf], writes=[lbuf])
        end_phase(a32, a16)
    lam_ops()
    lam_col = lamc[:, 0:1]
    gsub_col = lamc[:, 1:2]
    import math
    kbuf = Buf("kconst")
    k_eps6, k_eps5, k_lnhalf, k_one, k_zero = [lamc[:, 8 + i:9 + i] for i in range(5)]
    for ap_, v_ in ((k_eps6, 1e-6), (k_eps5, 1e-5), (k_lnhalf, math.log(0.5)), (k_one, 1.0), (k_zero, 0.0)):
        S.op("dve", lambda e, a=ap_, v=v_: e.memset(a, v), writes=[kbuf])

    class WRef:
        __slots__ = ("f32", "scr", "buf")

        def __init__(self, f32, scr):
            self.f32, self.scr, self.buf = f32, scr, Buf("wscr")

    def wrefs(name, d_ap):
        if d_ap.ndim == 3:
            scr = nc.dram_tensor(name + "_bf", list(d_ap.shape), BF16, kind="Internal").ap()
            return [WRef(d_ap[i], scr[i]) for i in range(d_ap.shape[0])]
        return [WRef(d_ap, None)]

    class WStream:
        def __init__(self, a32, a16, nstage, nslot, cap, scap=2048):
            self.sl = [a16.take("wbf%d" % i, cap) for i in range(nslot)]
            self.stb = [Buf("wst%d" % i) for i in range(nslot)]
            self.j = 0

        def load(self, wr, n, cast_eng=None):
            sl_ap, sl_b = self.sl[self.j % len(self.sl)]
            st_b = self.stb[self.j % len(self.sl)]
            self.j += 1
            if not isinstance(wr, WRef):
                S.dma("pool", sl_ap[:, 0:n], wr, sl_b, writes=[sl_b])
            elif wr.buf.w is None:
                S.dma("pool", sl_ap[:, 0:n], wr.f32, sl_b, writes=[sl_b])
                if wr.scr is not None:
                    S.dma("sp", wr.scr, sl_ap[:, 0:n], st_b, reads=[sl_b], writes=[wr.buf])
            else:
                S.dma("pool", sl_ap[:, 0:n], wr.scr, sl_b, reads=[wr.buf], writes=[sl_b])
            return sl_ap[:, 0:n], sl_b

    wgu_r = [wrefs("wgu%d" % (i + 1), wgu_d[i]) for i in range(2)]
    wdn_r = [wrefs("wdn%d" % (i + 1), wdn_d[i]) for i in range(2)]
    winf_r = wrefs("winf", winf_d)
    wint_r = wrefs("wint", wint_d)
    wa_r, ws_r, wmix_r, wxq_r, wxo_r = [wrefs(n, d) for n, d in
                                        (("wa", wa_d), ("ws", ws_d), ("wmix", wmix_d), ("wxq", wxq_d), ("wxo", wxo_d))]
    evac_rr = [0]

    def evac_copy(out_ap, in_ap, reads, writes):
        evac_rr[0] += 1
        if evac_rr[0] % 2:
            return S.op("act", lambda e: e.activation(out=out_ap, in_=in_ap, func=AF.Copy), reads=reads, writes=writes)
        return S.op("dve", lambda e: e.tensor_copy(out=out_ap, in_=in_ap), reads=reads, writes=writes)

    def mm(ps_i, out_ap, lhsT, rhs, start, stop, reads):
        return S.op("pe", lambda e: e.matmul(out_ap, lhsT, rhs, start=start, stop=stop), reads=reads, writes=[psb[ps_i]])

    def rsqrt_ps(ps_i, width, out_ap, out_b, eps, mul=1.0):
        eb = {1e-6: k_eps6, 1e-5: k_eps5}[eps]
        mb = {1.0: k_zero, 0.5: k_lnhalf}[mul]
        S.op("act", lambda e: e.activation(out=out_ap, in_=PS[ps_i][:, 0:width], func=AF.Ln, bias=eb),
             reads=[psb[ps_i], kbuf], writes=[out_b])
        S.op("act", lambda e: e.activation(out=out_ap, in_=out_ap, func=AF.Exp, scale=-0.5, bias=mb),
             reads=[out_b, kbuf], writes=[out_b])

    def rms_stats(xsrc, xb, nchunk, width, sq_pair, ones_ap, ps_i, rstd_ap, rstd_b, eps):
        for c in range(nchunk):
            sq_ap, sq_b = sq_pair[c % 2]
            S.op("act", lambda e, o=sq_ap[:, 0:width], i_=xsrc(c): e.activation(out=o, in_=i_, func=AF.Square),
                 reads=[xb[c] if isinstance(xb, BL) else xb], writes=[sq_b])
            mm(ps_i, PS[ps_i][:, 0:width], ones_ap, sq_ap[:, 0:width], c == 0, c == nchunk - 1, [sq_b, c16buf])
        rsqrt_ps(ps_i, width, rstd_ap, rstd_b, eps)

    def norm_cast(xsrc, xb, gcol0, rstd_ap, rstd_b, dst, dst_b, nchunk=8):
        for c in range(nchunk):
            S.op("dve", lambda e, c=c: e.scalar_tensor_tensor(out=dst(c), in0=xsrc(c), scalar=pv[:, gcol0 + c:gcol0 + c + 1],
                                                               in1=rstd_ap, op0=ALU.mult, op1=ALU.mult),
                 reads=[xb[c] if isinstance(xb, BL) else xb, rstd_b, cbuf],
                 writes=[dst_b[c] if isinstance(dst_b, BL) else dst_b])

    def ffn(a32, a16, ws, xT, xTb, xn, xnb, hid, hidb, f1, f1b, sq_pair, rstd, rstdb, tmp32, wgu, wdn, gpre, gpost):
        xc = lambda c: xT[:, c * G:(c + 1) * G]
        rms_stats(xc, xTb, 8, G, sq_pair, ones_d, 7, rstd, rstdb, 1e-6)
        norm_cast(xc, xTb, gpre, rstd, rstdb, lambda c: xn[:, c * G:(c + 1) * G], xnb)
        for i in range(NFF):
            w, wb = ws.load(wgu[i], 2048)
            pg, pu = (0, 1) if i % 2 == 0 else (2, 3)
            for c in range(8):
                mm(pg, PS[pg][:, :], w[:, c * 128:(c + 1) * 128], xn[:, c * G:(c + 1) * G], c == 0, c == 7, [wb, xnb[c]])
            for c in range(8):
                mm(pu, PS[pu][:, :], w[:, 1024 + c * 128:1024 + (c + 1) * 128], xn[:, c * G:(c + 1) * G],
                   c == 0, c == 7, [wb, xnb[c]])
            t_ap, t_b = tmp32[i % 2]
            S.op("act", lambda e, o=t_ap, i_=PS[pg][:, :]: e.activation(out=o, in_=i_, func=AF.Silu),
                 reads=[psb[pg]], writes=[t_b])
            S.op("dve", lambda e, o=hid[:, i * G:(i + 1) * G], a=t_ap, b=PS[pu][:, :]:
                 e.tensor_tensor(out=o, in0=b, in1=a, op=ALU.mult), reads=[t_b, psb[pu]], writes=[hidb])
        for d in range(8):
            w, wb = ws.load(wdn[d], DFF)
            p = 4 + d % 2
            for i in range(NFF):
                mm(p, PS[p][:, :], w[:, i * 128:(i + 1) * 128], hid[:, i * G:(i + 1) * G], i == 0, i == NFF - 1, [wb, hidb])
            sq_ap, sq_b = sq_pair[d % 2]
            S.op("dve", lambda e, o=f1[:, d * G:(d + 1) * G], i_=PS[p][:, :]: e.tensor_copy(out=o, in_=i_),
                 reads=[psb[p]], writes=[f1b[d]])
            S.op("act", lambda e, o=sq_ap, i_=f1[:, d * G:(d + 1) * G]: e.activation(out=o, in_=i_, func=AF.Square),
                 reads=[f1b[d]], writes=[sq_b])
            mm(6, PS[6][:, :], ones_d, sq_ap, d == 0, d == 7, [sq_b, c16buf])
        rsqrt_ps(6, G, rstd, rstdb, 1e-6, 0.5)
        resid_add(xT, xTb, f1, f1b, rstd, rstdb, gpost)

    def resid_add(xT, xTb, f1, f1b, rstd, rstdb, gcol0):
        for d in range(8):
            eng = "dve"
            S.op("dve", lambda e, d=d: e.scalar_tensor_tensor(out=f1[:, d * G:(d + 1) * G], in0=f1[:, d * G:(d + 1) * G],
                                                               scalar=pv[:, gcol0 + d:gcol0 + d + 1], in1=rstd,
                                                               op0=ALU.mult, op1=ALU.mult),
                 reads=[f1b[d], rstdb, cbuf], writes=[f1b[d]])
            S.op(eng, lambda e, d=d: e.tensor_tensor(out=xT[:, d * G:(d + 1) * G], in0=f1[:, d * G:(d + 1) * G],
                                                      in1=xT[:, d * G:(d + 1) * G], op=ALU.add),
                 reads=[f1b[d], xTb[d]], writes=[xTb[d]])

    def phase_ab():
        NSLOT = 8
        a32, a16 = new_phase()
        xT, xTb = a32.take_chunks("xT", 8, G)
        f1, f1b = a32.take_chunks("f1", 8, G)
        rstd, rstdb = a32.take("rstd", G)
        tmp32 = [a32.take("tmp%d" % i, G) for i in range(2)]
        xn, xnb = a16.take_chunks("xn", 8, G)
        hid, hidb = a16.take("hid", NFF * G)
        sq_pair = [a16.take("sq%d" % i, G) for i in range(2)]
        stg = [a16.take("stg%d" % i, G) for i in range(4)]
        dtst = [a32.take("dtst%d" % i, 32) for i in range(2)]
        ws = WStream(a32, a16, 3, NSLOT, 4096)
        wdt32, wdt32b = a32.take("wdt32", 256)
        wdtb, wdtbb = a16.take("wdtb", 256)
        S.dma("sp", wdt32, wdt_d[:, :], wdt32b, writes=[wdt32b])
        S.op("dve", lambda e: e.tensor_copy(out=wdtb, in_=wdt32), reads=[wdt32b], writes=[wdtbb])
        sti = [0]

        def stage():
            sti[0] += 1
            return stg[sti[0] % 4]

        for gg in range(NKG):
            pre = gg < NPG
            g = gg - NPG
            t0 = g * G
            k0 = gg * G
            src = xTp_d[:, gg * G:(gg + 1) * G] if pre else xT_d[:, t0:t0 + G]
            for c in range(8):
                S.dma("sp", xT[:, c * G:(c + 1) * G], src[c * 128:(c + 1) * 128, :], xTb[c], writes=[xTb[c]])
            ffn(a32, a16, ws, xT, xTb, xn, xnb, hid, hidb, f1, f1b, sq_pair, rstd, rstdb, tmp32, wgu_r[0], wdn_r[0],
                PV["ffn1_pre"], PV["ffn1_post"])
            if not pre:
                for c in range(8):
                    S.dma("sp", XS_d[c * 128:(c + 1) * 128, t0:t0 + G], xT[:, c * G:(c + 1) * G], xTb[c],
                          reads=[xTb[c]], writes=[XS_b[g]])
            xc = lambda c: xT[:, c * G:(c + 1) * G]
            rms_stats(xc, xTb, 8, G, sq_pair, ones_d, 7, rstd, rstdb, 1e-6)
            norm_cast(xc, xTb, PV["mix_pre"], rstd, rstdb, lambda c: xn[:, c * G:(c + 1) * G], xnb)
            for ft in range(56):
                if pre and (ft < 8 or ft >= 40):
                    continue
                w, wb = ws.load(winf_r[ft], 1024)
                p = ft % 4
                for c in range(8):
                    mm(p, PS[p][:, :], w[:, c * 128:(c + 1) * 128], xn[:, c * G:(c + 1) * G], c == 0, c == 7, [wb, xnb[c]])
                st_ap, st_b = stage()
                if ft < 8:
                    dst, db = QT_d[ft * 128:(ft + 1) * 128, t0:t0 + G], Q_b[g]
                elif ft < 16:
                    dst, db = KT_d[(ft - 8) * 128:(ft - 7) * 128, k0:k0 + G], K_b[gg]
                elif ft < 40:
                    dst, db = XBC_d[(ft - 16) * 128:(ft - 15) * 128, k0:k0 + G], XBC_b[gg]
                else:
                    dst, db = GT_d[(ft - 40) * 128:(ft - 39) * 128, t0:t0 + G], GT_b[g]
                if ft >= 40:
                    bcol = pv[:, PV["b_gate"] + ft - 40:PV["b_gate"] + ft - 39]
                    S.op("act", lambda e, o=st_ap, i_=PS[p][:, :], b=bcol: e.activation(out=o, in_=i_, func=AF.Sigmoid, bias=b),
                         reads=[psb[p], cbuf], writes=[st_b])
                else:
                    evac_copy(st_ap, PS[p][:, :], [psb[p]], [st_b])
                S.dma("sp", dst, st_ap, st_b, reads=[st_b], writes=[db])
            for blk in range(6):
                if pre and blk >= 2:
                    continue
                w, wb = ws.load(wint_r[blk], 4096)
                for tt in range(G // 128):
                    p = 4 + (blk * 4 + tt) % 2
                    for c in range(8):
                        mm(p, PS[p][:, :], xn[:, c * G + tt * 128:c * G + (tt + 1) * 128], w[:, c * 512:(c + 1) * 512],
                           c == 0, c == 7, [wb, xnb[c]])
                    st_ap, st_b = stage()
                    evac_copy(st_ap, PS[p][:, :], [psb[p]], [st_b])
                    if blk < 2:
                        r0 = k0 + tt * 128
                        S.dma("sp", V_d[r0:r0 + 128, blk * 512:(blk + 1) * 512], st_ap, st_b, reads=[st_b], writes=[V_b[gg]])
                    else:
                        r0 = t0 + tt * 128
                        S.dma("sp", Z_d[r0:r0 + 128, (blk - 2) * 512:(blk - 1) * 512], st_ap, st_b, reads=[st_b],
                              writes=[Z_b[g]])
            for tt in range(G // 128):
                for c in range(8):
                    mm(6, PS[6][:, 0:32], xn[:, c * G + tt * 128:c * G + (tt + 1) * 128], wdtb[:, c * 32:(c + 1) * 32],
                       c == 0, c == 7, [wdtbb, xnb[c]])
                d_ap, d_b = dtst[tt % 2]
                S.op("dve", lambda e, o=d_ap, i_=PS[6][:, 0:32]: e.tensor_copy(out=o, in_=i_), reads=[psb[6]], writes=[d_b])
                r0 = k0 + tt * 128
                S.dma("sp", DT_d[r0:r0 + 128, :], d_ap, d_b, reads=[d_b], writes=[DT_b[gg]])
        end_phase(a32, a16)

    def phase_c():
        a32, a16 = new_phase()
        kT = [a16.take("kT%d" % i, KVT) for i in range(2)]
        vv = [a16.take("vv%d" % i, KVT) for i in range(2)]
        qT = [a16.take("qT%d" % i, G) for i in range(2)]
        pp = [a16.take("pp%d" % i, G) for i in range(4)]
        sq_pair = [a16.take("sqc%d" % i, G) for i in range(2)]
        ost = [a16.take("ost%d" % i, G) for i in range(2)]
        r1, r1b = a32.take("r1", G)
        r2, r2b = a32.take("r2", G)
        o1, o1b = a32.take("o1", G)
        o2, o2b = a32.take("o2", G)
        rs, rsb = a32.take("rsc", G)
        pi = [0]

        def load_kv(h):
            k_ap, k_b = kT[h % 2]
            v_ap, v_b = vv[h % 2]
            S.dma("sp", k_ap, KT_d[h * 128:(h + 1) * 128, :], k_b, reads=K_b, writes=[k_b])
            for q4 in range(0, NCH, 8):
                S.dma("sp", v_ap[:, q4 * 128:(q4 + 8) * 128].rearrange("p (k e) -> p k e", e=128),
                      V_d[q4 * 128:(q4 + 8) * 128, h * 128:(h + 1) * 128].rearrange("(k p) e -> p k e", p=128), v_b,
                      reads=V_b, writes=[v_b])

        def load_q(h, j):
            q_ap, q_b = qT[(h * NG_ + j) % 2]
            S.dma("sp", q_ap, QT_d[h * 128:(h + 1) * 128, j * G:(j + 1) * G], q_b, reads=[Q_b[j]], writes=[q_b])

        load_kv(0)
        load_q(0, 0)
        for h in range(NH):
            k_ap, k_b = kT[h % 2]
            v_ap, v_b = vv[h % 2]
            if h + 1 < NH:
                load_kv(h + 1)
            for j in range(NG_):
                q_ap, q_b = qT[(h * NG_ + j) % 2]
                if j + 1 < NG_:
                    load_q(h, j + 1)
                elif h + 1 < NH:
                    load_q(h + 1, 0)
                npk = NPG * 4
                nkt = npk + 4 * j + 4

                def emit_s(kt, k_ap=k_ap, k_b=k_b, q_ap=q_ap, q_b=q_b, j=j):
                    sa, sb_ = (4, 5) if kt % 2 == 0 else (6, 7)
                    mm(sa, PS[sa][:, :], k_ap[0:64, kt * 128:(kt + 1) * 128], q_ap[0:64, :], True, True, [k_b, q_b])
                    mm(sb_, PS[sb_][:, :], k_ap[64:128, kt * 128:(kt + 1) * 128], q_ap[64:128, :], True, True, [k_b, q_b])
                    p1, p1b = pp[pi[0] % 4]
                    p2, p2b = pp[(pi[0] + 1) % 4]
                    pi[0] += 2
                    bias_ = fl_bias if kt < NPG * 4 else k_zero
                    S.op("act", lambda e, o=p1, i_=PS[sa][:, :], b_=bias_: e.activation(out=o, in_=i_, func=AF.Exp, scale=0.125,
                                                                                     bias=b_),
                         reads=[psb[sa], cbuf, kbuf], writes=[p1b])
                    S.op("act", lambda e, o=p2, i_=PS[sb_][:, :], b_=bias_: e.activation(out=o, in_=i_, func=AF.Exp, scale=0.125,
                                                                                      bias=b_),
                         reads=[psb[sb_], cbuf, kbuf], writes=[p2b])
                    m = kt - NPG * 4 - 4 * j
                    if m >= 0:
                        mk = maskb[:, m * 512:(m + 1) * 512]
                        S.op("dve", lambda e, o=p1, mk=mk: e.tensor_tensor(out=o, in0=o, in1=mk, op=ALU.mult),
                             reads=[p1b, c16buf], writes=[p1b])
                        S.op("dve", lambda e, o=p2, mk=mk: e.tensor_tensor(out=o, in0=o, in1=mk, op=ALU.mult),
                             reads=[p2b, c16buf], writes=[p2b])
                    return p1, p1b, p2, p2b

                pend = emit_s(0)
                for kt in range(nkt):
                    p1, p1b, p2, p2b = pend
                    if kt + 1 < nkt:
                        pend = emit_s(kt + 1)
                    first, last = kt == 0, kt == nkt - 1
                    vt = v_ap[:, kt * 128:(kt + 1) * 128]
                    mm(0, PS[0][:, :], vt, p1, first, last, [v_b, p1b])
                    mm(1, PS[1][:, :], vt, p2, first, last, [v_b, p2b])
                    mm(2, PS[2][:, :], ones_1, p1, first, last, [c16buf, p1b])
                    mm(3, PS[3][:, :], ones_1, p2, first, last, [c16buf, p2b])
                S.op("dve", lambda e: e.reciprocal(out=r1, in_=PS[2][:, :]), reads=[psb[2]], writes=[r1b])
                S.op("dve", lambda e: e.reciprocal(out=r2, in_=PS[3][:, :]), reads=[psb[3]], writes=[r2b])
                S.op("dve", lambda e: e.tensor_tensor(out=o1, in0=PS[0][:, :], in1=r1, op=ALU.mult),
                     reads=[psb[0], r1b], writes=[o1b])
                S.op("dve", lambda e: e.scalar_tensor_tensor(out=o2, in0=PS[1][:, :], scalar=lam_col, in1=r2,
                                                             op0=ALU.mult, op1=ALU.mult),
                     reads=[psb[1], r2b, lbuf], writes=[o2b])
                S.op("dve", lambda e: e.tensor_tensor(out=o1, in0=o1, in1=o2, op=ALU.subtract),
                     reads=[o1b, o2b], writes=[o1b])
                sq_ap, sq_b = sq_pair[j % 2]
                S.op("act", lambda e, o=sq_ap: e.activation(out=o, in_=o1, func=AF.Square), reads=[o1b], writes=[sq_b])
                mm(4, PS[4][:, :], ones_h, sq_ap, True, True, [sq_b, c16buf])
                rsqrt_ps(4, G, rs, rsb, 1e-5)
                os_ap, os_b = ost[j % 2]
                S.op("dve", lambda e, o=os_ap: e.scalar_tensor_tensor(out=o, in0=o1, scalar=gsub_col, in1=rs,
                                                                      op0=ALU.mult, op1=ALU.mult),
                     reads=[o1b, rsb, lbuf], writes=[os_b])
                S.dma("sp", ON_d[h * 128:(h + 1) * 128, j * G:(j + 1) * G], os_ap, os_b, reads=[os_b], writes=[ON_b[j]])
        end_phase(a32, a16)


    def phase_d():
        a32, a16 = new_phase()
        dtr, dtrb = a32.take("dtr", 128)
        dtv, dtvb = a32.take("dtv", 128)
        adt, adtb = a32.take("adt", 128)
        nega, negab = a32.take("nega", 32)
        onesf, onesfb = a32.take("onesf", 128)
        acs, acsb = a32.take("acs", 32)
        tot, totb = a32.take("tot", 32)
        dsd, dsdb = a32.take("dsd", 32)
        dab, dabb = a32.take("dab", 32)
        tmps, tmpsb = a32.take("tmps", 32)
        rhs4all, _r40 = a32.take("rhs4all", 4096)
        rhs4 = [(rhs4all[:, i * 512:(i + 1) * 512], _r40 if i == 0 else Buf("rhs4_%d" % i, a32.prior)) for i in range(8)]
        a32.bufs.extend([b_ for _, b_ in rhs4[1:]])
        acsbc = [a32.take("acsbc%d" % i, 512) for i in range(2)]
        dif = [a32.take("dif%d" % i, 512) for i in range(2)]
        St, _stb0 = a32.take("St", 2048)
        Stb = [_stb0] + [Buf("St%d" % i, a32.prior) for i in range(1, 4)]
        a32.bufs.extend(Stb[1:])
        yz, yzb = a32.take("yz", 2048)
        ss, ssb = a32.take("ss", 8)
        xin = [a16.take("xin%d" % i, 520) for i in range(4)]
        xc, xcb = a16.take("xc", 24 * G)
        zt = [a16.take("zt%d" % i, 2048) for i in range(2)]
        xst, xstb = a16.take("xst", 2048)
        btk, btkb = a16.take("btk", 512)
        MT = [a16.take("MT%d" % i, 512) for i in range(2)]
        Ce = [a16.take("Ce%d" % i, 512) for i in range(2)]
        LT = [a16.take("LT%d" % i, 512) for i in range(2)]
        ea = [a16.take("ea%d" % i, 512) for i in range(2)]
        cbm, cbmb = a16.take("cbm", 512)
        trib4, trib4b = a16.take("trib4", 512)
        xd, xdb = a16.take("xd", 2048)
        Sbf, _sbf0 = a16.take("Sbf", 2048)
        Sbfb = [_sbf0] + [Buf("Sbf%d" % i, a16.prior) for i in range(1, 4)]
        a16.bufs.extend(Sbfb[1:])
        yzn, yznb = a16.take("yzn", 2048)
        junk, junkb = yzn[:, 0:512], yznb
        sz, szb = a16.take("sz", 2048)
        ynT = [a16.take("ynT%d" % i, 16 * G) for i in range(1)]
        cdiag, cdiagb = a16.take("cdiag", 96 * 128)
        xdt, xdtb = a16.take("xdt", 2048)
        for fj in range(96):
            S.op("dve", lambda e, fj=fj: e.tensor_scalar(out=cdiag[:, fj * 128:(fj + 1) * 128], in0=ident32,
                                                          scalar1=pv[:, PV["conv_w"] + fj:PV["conv_w"] + fj + 1],
                                                          scalar2=None, op0=ALU.mult), reads=[cbuf], writes=[cdiagb])

        S.op("act", lambda e: e.activation(out=nega, in_=rowb[:, 32:64], func=AF.Exp), reads=[cbuf], writes=[negab])
        S.op("dve", lambda e: e.tensor_scalar(out=nega, in0=nega, scalar1=-1.0, scalar2=None, op0=ALU.mult),
             reads=[negab], writes=[negab])
        S.op("dve", lambda e: e.memset(onesf, 1.0), writes=[onesfb])
        S.op("dve", lambda e: e.memset(St, 0.0), writes=Stb)
        S.op("dve", lambda e: e.memset(Sbf, 0.0), writes=Sbfb)
        for a in range(4):
            S.op("dve", lambda e, a=a: e.tensor_copy(out=trib4[:, a * 128:(a + 1) * 128], in_=tri32), reads=[cbuf],
                 writes=[trib4b])
        psT = [PS[4][:, :].bitcast(BF16), PS[5][:, :].bitcast(BF16)]
        psB = PS[6][:, :].bitcast(BF16)
        xi = [0]
        pending = [None]
        for gg in range(NKG):
            pre = gg < NPG
            g = gg - NPG
            t0 = g * G
            k0 = gg * G
            if NPG > 0 and gg == NPG:
                S.op("dve", lambda e: e.tensor_scalar(out=St, in0=St, scalar1=fl_valid, scalar2=None, op0=ALU.mult),
                     reads=Stb + [cbuf], writes=Stb)
                S.op("dve", lambda e: e.tensor_scalar(out=Sbf, in0=Sbf, scalar1=fl_valid, scalar2=None, op0=ALU.mult),
                     reads=Sbfb + [cbuf], writes=Sbfb)
            for f in range(24):
                x_ap, x_b = xin[xi[0] % 4]
                xi[0] += 1
                if gg == 0:
                    S.op("dve", lambda e, o=x_ap[:, 0:3]: e.memset(o, 0.0), writes=[x_b])
                    S.dma("sp", x_ap[:, 3:3 + G], XBC_d[f * 128:(f + 1) * 128, 0:G], x_b, reads=[XBC_b[0]], writes=[x_b])
                else:
                    S.dma("sp", x_ap[:, 0:3 + G], XBC_d[f * 128:(f + 1) * 128, k0 - 3:k0 + G], x_b,
                          reads=[XBC_b[gg - 1], XBC_b[gg]], writes=[x_b])
                    if gg == NPG:
                        S.op("dve", lambda e, o=x_ap[:, 0:3]: e.tensor_scalar(out=o, in0=o, scalar1=fl_valid, scalar2=None,
                                                                              op0=ALU.mult), reads=[x_b, cbuf], writes=[x_b])
                cp = f % 4
                for j in range(4):
                    mm(cp, PS[cp][:, :], cdiag[:, (f * 4 + j) * 128:(f * 4 + j + 1) * 128], x_ap[:, j:j + G], j == 0, j == 3,
                       [cdiagb, x_b])
                S.op("act", lambda e, o=xc[:, f * G:(f + 1) * G], i_=PS[cp][:, :], b=pv[:, PV["conv_b"] + f:PV["conv_b"] + f + 1]:
                     e.activation(out=o, in_=i_, func=AF.Silu, bias=b), reads=[psb[cp], cbuf], writes=[xcb])
            S.dma("sp", dtr.rearrange("p (k h) -> p k h", h=32),
                  DT_d[k0:k0 + G, :].rearrange("(k p) h -> p k h", p=128), dtrb, reads=[DT_b[gg]], writes=[dtrb])
            for k in range(4):
                S.op("dve", lambda e, k=k: e.tensor_tensor(out=dtr[:, k * 32:(k + 1) * 32], in0=dtr[:, k * 32:(k + 1) * 32],
                                                           in1=rowb[:, 0:32], op=ALU.add), reads=[dtrb, cbuf], writes=[dtrb])
            S.op("act", lambda e: e.activation(out=dtv, in_=dtr, func=AF.Exp), reads=[dtrb], writes=[dtvb])
            S.op("act", lambda e: e.activation(out=dtv, in_=dtv, func=AF.Ln, bias=k_one), reads=[dtvb, kbuf], writes=[dtvb])
            for k in range(4):
                S.op("dve", lambda e, k=k: e.tensor_tensor(out=adt[:, k * 32:(k + 1) * 32], in0=dtv[:, k * 32:(k + 1) * 32],
                                                           in1=nega, op=ALU.mult), reads=[dtvb, negab], writes=[adtb])
            yn_ap, yn_b = ynT[0]
            for k in range(4):
                c0 = k * 128
                r0 = t0 + c0
                z_ap, z_b = zt[k % 2]
                if not pre:
                    S.dma("sp", z_ap, Z_d[r0:r0 + 128, :], z_b, reads=[Z_b[g]], writes=[z_b])
                    S.op("act", lambda e, z_ap=z_ap: e.activation(out=sz, in_=z_ap, func=AF.Silu), reads=[z_b], writes=[szb])
                adk = adt[:, k * 32:(k + 1) * 32]
                dtk = dtv[:, k * 32:(k + 1) * 32]
                if not pre:
                    for q in range(8):
                        r4, r4b = rhs4[q]
                        S.op("dve", lambda e, o=r4, q=q, adk=adk: e.tensor_tensor(
                            out=o.rearrange("p (a b) -> p a b", a=4), in0=tri32.unsqueeze(1).to_broadcast([128, 4, 128]),
                            in1=adk[:, 4 * q:4 * q + 4].unsqueeze(2).to_broadcast([128, 4, 128]), op=ALU.mult),
                            reads=[cbuf, adtb], writes=[r4b])
                for f in range(16):
                    S.op("pe", lambda e, o=psT[f // 8][:, (f % 8) * 128:(f % 8 + 1) * 128], i_=xc[:, f * G + c0:f * G + c0 + 128]:
                         e.transpose(o, i_, identb), reads=[xcb, c16buf], writes=[psb[4 + f // 8]])
                for f in range(4):
                    S.op("pe", lambda e, o=psB[:, f * 128:(f + 1) * 128], i_=xc[:, (16 + f) * G + c0:(16 + f) * G + c0 + 128]:
                         e.transpose(o, i_, identb), reads=[xcb, c16buf], writes=[psb[6]])
                S.op("act", lambda e: e.activation(out=xst[:, 0:1024], in_=psT[0], func=AF.Copy), reads=[psb[4]], writes=[xstb])
                S.op("dve", lambda e: e.tensor_copy(out=xst[:, 1024:2048], in_=psT[1]), reads=[psb[5]], writes=[xstb])
                S.op("dve", lambda e: e.tensor_copy(out=btk, in_=psB[:, 0:512]), reads=[psb[6]], writes=[btkb])
                mm(7, PS[7][:, 0:32], tri32, adk, True, True, [cbuf, adtb])
                mm(7, PS[7][:, 32:64], onesf, adk, True, True, [onesfb, adtb])
                S.op("dve", lambda e: e.tensor_copy(out=acs, in_=PS[7][:, 0:32]), reads=[psb[7]], writes=[acsb])
                S.op("dve", lambda e: e.tensor_copy(out=tot, in_=PS[7][:, 32:64]), reads=[psb[7]], writes=[totb])
                S.op("dve", lambda e: e.tensor_tensor(out=tmps, in0=tot, in1=acs, op=ALU.subtract), reads=[totb, acsb],
                     writes=[tmpsb])
                S.op("act", lambda e: e.activation(out=tmps, in_=tmps, func=AF.Exp), reads=[tmpsb], writes=[tmpsb])
                S.op("dve", lambda e, dtk=dtk: e.tensor_tensor(out=dsd, in0=tmps, in1=dtk, op=ALU.mult),
                     reads=[tmpsb, dtvb], writes=[dsdb])
                S.op("act", lambda e: e.activation(out=dab, in_=tot, func=AF.Exp), reads=[totb], writes=[dabb])
                if not pre:
                    for gr in range(4):
                        mm(6, PS[6][:, gr * 128:(gr + 1) * 128], xc[:, (16 + gr) * G + c0:(16 + gr) * G + c0 + 128],
                           xc[:, (20 + gr) * G + c0:(20 + gr) * G + c0 + 128], True, True, [xcb, btkb])
                    S.op("dve", lambda e: e.tensor_tensor(out=cbm, in0=PS[6][:, :], in1=trib4, op=ALU.mult),
                         reads=[psb[6], trib4b], writes=[cbmb])
                    S.op("dve", lambda e, dtk=dtk: e.tensor_tensor(out=xdt.rearrange("p (h d) -> p h d", d=64),
                                                                   in0=xst.rearrange("p (h d) -> p h d", d=64),
                                                                   in1=dtk.unsqueeze(2).to_broadcast([128, 32, 64]), op=ALU.mult),
                         reads=[xstb, dtvb], writes=[xdtb])
                    def front(q, adk=adk, c0=c0):
                        gr = q // 2
                        r4, r4b = rhs4[q]
                        abk = 7 if q % 2 == 0 else 6
                        mm(abk, PS[abk][:, :], onesf, r4, True, True, [onesfb, r4b])
                        ab_ap, ab_b = acsbc[q % 2]
                        S.op("act", lambda e, o=ab_ap, abk=abk: e.activation(out=o, in_=PS[abk][:, :], func=AF.Copy),
                             reads=[psb[abk]], writes=[ab_b])
                        d_ap, d_b = dif[q % 2]
                        for hh in range(4):
                            h = 4 * q + hh
                            S.op("dve", lambda e, o=d_ap[:, hh * 128:(hh + 1) * 128], i_=ab_ap[:, hh * 128:(hh + 1) * 128], h=h:
                                 e.tensor_scalar(out=o, in0=i_, scalar1=acs[:, h:h + 1], scalar2=0.0, op0=ALU.subtract, op1=ALU.min),
                                 reads=[ab_b, acsb], writes=[d_b])
                        l_ap, l_b = LT[q % 2]
                        e_ap, e_b = ea[q % 2]
                        S.op("act", lambda e, o=l_ap, i_=d_ap: e.activation(out=o, in_=i_, func=AF.Exp), reads=[d_b], writes=[l_b])
                        S.op("act", lambda e, o=e_ap, i_=ab_ap: e.activation(out=o, in_=i_, func=AF.Exp), reads=[ab_b], writes=[e_b])
                        m_ap, m_b = MT[q % 2]
                        c_ap, c_b = Ce[q % 2]
                        S.op("dve", lambda e, o=m_ap, i_=l_ap, cb_=cbm[:, gr * 128:(gr + 1) * 128]: e.tensor_tensor(
                            out=o.rearrange("p (a b) -> p a b", a=4), in0=i_.rearrange("p (a b) -> p a b", a=4),
                            in1=cb_.unsqueeze(1).to_broadcast([128, 4, 128]), op=ALU.mult),
                            reads=[l_b, cbmb], writes=[m_b])
                        S.op("dve", lambda e, o=c_ap, i_=e_ap, cc=xc[:, (20 + gr) * G + c0:(20 + gr) * G + c0 + 128]: e.tensor_tensor(
                            out=o.rearrange("p (a b) -> p a b", a=4), in0=i_.rearrange("p (a b) -> p a b", a=4),
                            in1=cc.unsqueeze(1).to_broadcast([128, 4, 128]),
                            op=ALU.mult), reads=[e_b, xcb], writes=[c_b])
                        return m_ap, m_b, c_ap, c_b

                    def back(q, m_ap, m_b, c_ap, c_b):
                        for hh in range(4):
                            h = 4 * q + hh
                            bk = h // 8
                            o = PS[bk][:, (h % 8) * 64:(h % 8 + 1) * 64]
                            mm(bk, o, m_ap[:, hh * 128:(hh + 1) * 128], xdt[:, h * 64:(h + 1) * 64], True, False, [m_b, xdtb])
                            mm(bk, o, c_ap[:, hh * 128:(hh + 1) * 128], Sbf[:, h * 64:(h + 1) * 64], False, False, [c_b, Sbfb[bk]])
                            mm(bk, o, Ddiag[:, h * 128:(h + 1) * 128], xst[:, h * 64:(h + 1) * 64], False, True, [c16buf, xstb])

                def st_part1():
                    S.op("dve", lambda e: e.tensor_tensor(out=xd.rearrange("p (h d) -> p h d", d=64),
                                                          in0=xst.rearrange("p (h d) -> p h d", d=64),
                                                          in1=dsd.unsqueeze(2).to_broadcast([128, 32, 64]), op=ALU.mult),
                         reads=[xstb, dsdb], writes=[xdb])
                    for gr in range(4):
                        S.op("dve", lambda e, gr=gr: e.tensor_tensor(
                            out=St[:, gr * 512:(gr + 1) * 512].rearrange("p (h d) -> p h d", d=64),
                            in0=St[:, gr * 512:(gr + 1) * 512].rearrange("p (h d) -> p h d", d=64),
                            in1=dab[:, gr * 8:(gr + 1) * 8].unsqueeze(2).to_broadcast([128, 8, 64]), op=ALU.mult),
                            reads=[Stb[gr], dabb], writes=[Stb[gr]])

                def st_part2(grs):
                    for gr in grs:
                        sbk = 7 if gr % 2 == 0 else 6
                        mm(sbk, PS[sbk][:, :], btk[:, gr * 128:(gr + 1) * 128], xd[:, gr * 512:(gr + 1) * 512], True, True,
                           [btkb, xdb])
                        S.op("dve", lambda e, gr=gr, sbk=sbk: e.tensor_tensor(out=St[:, gr * 512:(gr + 1) * 512],
                                                                              in0=St[:, gr * 512:(gr + 1) * 512],
                                                                              in1=PS[sbk][:, :], op=ALU.add),
                             reads=[Stb[gr], psb[sbk]], writes=[Stb[gr]])

                def sbf_refresh():
                    for gr in range(4):
                        S.op("act", lambda e, gr=gr: e.activation(out=Sbf[:, gr * 512:(gr + 1) * 512],
                                                                  in_=St[:, gr * 512:(gr + 1) * 512], func=AF.Copy),
                             reads=[Stb[gr]], writes=[Sbfb[gr]])

                if pre:
                    st_part1()
                    st_part2((0, 1, 2, 3))
                    sbf_refresh()
                else:
                    steps = pending[0] or []
                    pending[0] = None
                    extra = {4: [st_part1], 5: [lambda: st_part2((0, 1))], 6: [lambda: st_part2((2, 3))]}
                    for i_, st_ in enumerate(steps):
                        extra.setdefault(i_, []).append(st_)
                    fr = front(0)
                    for q in range(8):
                        cur = fr
                        if q + 1 < 8:
                            fr = front(q + 1)
                        back(q, *cur)
                        for fn_ in extra.get(q, []):
                            fn_()
                    sbf_refresh()
                    for gr in range(4):
                        S.op("dve", lambda e, gr=gr: e.tensor_tensor(out=yz[:, gr * 512:(gr + 1) * 512], in0=PS[gr][:, :],
                                                                     in1=sz[:, gr * 512:(gr + 1) * 512], op=ALU.mult),
                             reads=[psb[gr], szb], writes=[yzb])

                    def t1():
                        S.op("dve", lambda e: e.memset(ss[:, 0:4], 0.0), writes=[ssb])
                        for gr in range(4):
                            S.op("act", lambda e, gr=gr: e.activation(out=junk, in_=yz[:, gr * 512:(gr + 1) * 512], func=AF.Square,
                                                                      accum_out=ss[:, gr:gr + 1]), reads=[yzb], writes=[junkb, ssb])
                        S.op("act", lambda e: e.activation(out=ss[:, 4:8], in_=ss[:, 0:4], func=AF.Ln, scale=1.0 / 512.0,
                                                           bias=k_eps5), reads=[ssb, kbuf], writes=[ssb])
                        S.op("act", lambda e: e.activation(out=ss[:, 4:8], in_=ss[:, 4:8], func=AF.Exp, scale=-0.5),
                             reads=[ssb], writes=[ssb])

                    def t2():
                        for gr in range(4):
                            S.op("dve", lambda e, gr=gr: e.tensor_scalar(out=yzn[:, gr * 512:(gr + 1) * 512],
                                                                         in0=yz[:, gr * 512:(gr + 1) * 512],
                                                                         scalar1=ss[:, 4 + gr:5 + gr], scalar2=None, op0=ALU.mult),
                                 reads=[yzb, ssb], writes=[yznb])

                    def t3():
                        for f in range(16):
                            S.op("pe", lambda e, o=psT[f // 8][:, (f % 8) * 128:(f % 8 + 1) * 128], i_=yzn[:, f * 128:(f + 1) * 128]:
                                 e.transpose(o, i_, identb), reads=[yznb, c16buf], writes=[psb[4 + f // 8]])

                    def t4(c0=c0, yn_ap=yn_ap, yn_b=yn_b):
                        for f in range(16):
                            gcol = pv[:, PV["ssd_g"] + f:PV["ssd_g"] + f + 1]
                            src = psT[f // 8][:, (f % 8) * 128:(f % 8 + 1) * 128]
                            dst = yn_ap[:, f * G + c0:f * G + c0 + 128]
                            if f // 8 == 0:
                                S.op("act", lambda e, o=dst, i_=src, gc=gcol: e.activation(out=o, in_=i_, func=AF.Copy, scale=gc),
                                     reads=[psb[4], cbuf], writes=[yn_b])
                            else:
                                S.op("dve", lambda e, o=dst, i_=src, gc=gcol: e.tensor_scalar(out=o, in0=i_, scalar1=gc,
                                                                                             scalar2=None, op0=ALU.mult),
                                     reads=[psb[5], cbuf], writes=[yn_b])
                    pending[0] = [t1, t2, t3, t4]
            if pending[0] is not None:
                for st_ in pending[0]:
                    st_()
                pending[0] = None
            for f in range(16):
                if pre:
                    break
                S.dma("sp", YN_d[f * 128:(f + 1) * 128, t0:t0 + G], yn_ap[:, f * G:(f + 1) * G], yn_b, reads=[yn_b],
                      writes=[YN_b[g]])
        end_phase(a32, a16)


    def phase_e():
        NSLOT = 5
        a32, a16 = new_phase()
        xT, xTb = a32.take_chunks("xT", 8, G)
        f1, f1b = a32.take_chunks("f1", 8, G)
        rstd, rstdb = a32.take("rstd", G)
        tmp32 = [a32.take("tmp%d" % i, G) for i in range(2)]
        memx, memxb = f1[:, 0:8 * MEM], tuple(f1b)
        ws = WStream(a32, a16, 3, NSLOT, 4096)
        xn, xnb = a16.take_chunks("xn", 8, G)
        hid, hidb = a16.take("hid", NFF * G)
        sq_pair = [a16.take("sq%d" % i, G) for i in range(2)]
        aux, auxb = a16.take("aux", 8 * G)
        qx, qxb = a16.take("qx", 8 * G)
        gts = [a16.take("gts%d" % i, 2 * G) for i in range(2)]
        pp = [a16.take("ppx%d" % i, G) for i in range(2)]
        kxT, kxTb = a16.take("kxT", 8 * MEM)
        vx, vxb = a16.take("vx", 2 * D)
        memn, memnb = a16.take("memn", 8 * MEM)
        ons = xn
        onsb = xnb
        yns, ynsb = hid, hidb

        for c in range(8):
            S.dma("sp", memx[:, c * MEM:(c + 1) * MEM], memT_d[c * 128:(c + 1) * 128, :], f1b[0], writes=[memxb])
        mc = lambda c: memx[:, c * MEM:(c + 1) * MEM]
        rms_stats(mc, memxb, 8, MEM, sq_pair, ones_d, 7, rstd[:, 0:MEM], rstdb, 1e-6)
        norm_cast(mc, memxb, PV["mem_g"], rstd[:, 0:MEM], rstdb, lambda c: memn[:, c * MEM:(c + 1) * MEM], memnb)
        for dt_ in range(8):
            w, wb = ws.load(wxk_d[dt_], 1024)
            p = dt_ % 2
            for c in range(8):
                mm(p, PS[p][:, 0:MEM], w[:, c * 128:(c + 1) * 128], memn[:, c * MEM:(c + 1) * MEM], c == 0, c == 7, [wb, memnb])
            evac_copy(kxT[:, dt_ * MEM:(dt_ + 1) * MEM], PS[p][:, 0:MEM], [psb[p]], [kxTb])
        for blk in range(2):
            w, wb = ws.load(wxv_d[blk], 4096)
            for kt in range(2):
                p = 2 + kt
                for c in range(8):
                    mm(p, PS[p][:, :], memn[:, c * MEM + kt * 128:c * MEM + (kt + 1) * 128], w[:, c * 512:(c + 1) * 512],
                       c == 0, c == 7, [wb, memnb])
                evac_copy(vx[:, kt * D + blk * 512:kt * D + (blk + 1) * 512], PS[p][:, :], [psb[p]], [vxb])

        def proj_norm_resid(w_d, src, srcb, gpost, nck=8):
            for d in range(8):
                w, wb = ws.load(w_d[d], nck * 128)
                p = 4 + d % 2
                for c in range(nck):
                    mm(p, PS[p][:, :], w[:, c * 128:(c + 1) * 128], src[:, c * G:(c + 1) * G], c == 0, c == nck - 1, [wb, srcb])
                sq_ap, sq_b = sq_pair[d % 2]
                S.op("dve", lambda e, o=f1[:, d * G:(d + 1) * G], i_=PS[p][:, :]: e.tensor_copy(out=o, in_=i_),
                     reads=[psb[p]], writes=[f1b[d]])
                S.op("act", lambda e, o=sq_ap, i_=f1[:, d * G:(d + 1) * G]: e.activation(out=o, in_=i_, func=AF.Square),
                     reads=[f1b[d]], writes=[sq_b])
                mm(6, PS[6][:, :], ones_d, sq_ap, d == 0, d == 7, [sq_b, c16buf])
            rsqrt_ps(6, G, rstd, rstdb, 1e-6, 1.0)
            resid_add(xT, xTb, f1, f1b, rstd, rstdb, gpost)

        gi = [0]
        for g in range(NG_):
            t0 = g * G
            for c in range(8):
                S.dma("sp", ons[:, c * G:(c + 1) * G], ON_d[c * 128:(c + 1) * 128, t0:t0 + G], onsb[c], reads=[ON_b[g]], writes=[onsb[c]])
            for c in range(16):
                S.dma("sp", yns[:, c * G:(c + 1) * G], YN_d[c * 128:(c + 1) * 128, t0:t0 + G], ynsb, reads=[YN_b[g]], writes=[ynsb])
            for c in range(8):
                S.dma("sp", xT[:, c * G:(c + 1) * G], XS_d[c * 128:(c + 1) * 128, t0:t0 + G], xTb[c], reads=[XS_b[g]], writes=[xTb[c]])
            for d in range(8):
                wa, wab = ws.load(wa_r[d], 1024)
                wsd, wsb = ws.load(ws_r[d], 2048)
                pa, ps_ = (0, 1) if d % 2 == 0 else (2, 3)
                for c in range(8):
                    mm(pa, PS[pa][:, :], wa[:, c * 128:(c + 1) * 128], ons[:, c * G:(c + 1) * G], c == 0, c == 7, [wab, onsb[c]])
                for c in range(16):
                    mm(ps_, PS[ps_][:, :], wsd[:, c * 128:(c + 1) * 128], yns[:, c * G:(c + 1) * G], c == 0, c == 15, [wsb, ynsb])
                gt_ap, gt_b = gts[gi[0] % 2]
                gi[0] += 1
                S.dma("sp", gt_ap[:, 0:G], GT_d[d * 128:(d + 1) * 128, t0:t0 + G], gt_b, reads=[GT_b[g]], writes=[gt_b])
                S.dma("sp", gt_ap[:, G:2 * G], GT_d[D + d * 128:D + (d + 1) * 128, t0:t0 + G], gt_b, reads=[GT_b[g]], writes=[gt_b])
                ta, tab = tmp32[0]
                tb, tbb = tmp32[1]
                S.op("dve", lambda e, pa=pa, g_=gt_ap[:, 0:G]: e.tensor_tensor(out=ta, in0=PS[pa][:, :], in1=g_, op=ALU.mult),
                     reads=[psb[pa], gt_b], writes=[tab])
                S.op("dve", lambda e, ps_=ps_, g_=gt_ap[:, G:2 * G]: e.tensor_tensor(out=tb, in0=PS[ps_][:, :], in1=g_, op=ALU.mult),
                     reads=[psb[ps_], gt_b], writes=[tbb])
                S.op("dve", lambda e, o=aux[:, d * G:(d + 1) * G]: e.tensor_tensor(out=o, in0=ta, in1=tb, op=ALU.add),
                     reads=[tab, tbb], writes=[auxb])
            proj_norm_resid(wmix_r, aux, auxb, PV["mix_post"])
            xc_ = lambda c: xT[:, c * G:(c + 1) * G]
            rms_stats(xc_, xTb, 8, G, sq_pair, ones_d, 7, rstd, rstdb, 1e-6)
            norm_cast(xc_, xTb, PV["xa_pre"], rstd, rstdb, lambda c: xn[:, c * G:(c + 1) * G], xnb)
            for d in range(8):
                w, wb = ws.load(wxq_r[d], 1024)
                p = d % 2
                for c in range(8):
                    mm(p, PS[p][:, :], w[:, c * 128:(c + 1) * 128], xn[:, c * G:(c + 1) * G], c == 0, c == 7, [wb, xnb[c]])
                evac_copy(qx[:, d * G:(d + 1) * G], PS[p][:, :], [psb[p]], [qxb])
            for hd in range(4):
                for kt in range(2):
                    p = 4 + kt
                    for dd in range(2):
                        dt_ = hd * 2 + dd
                        mm(p, PS[p][:, :], kxT[:, dt_ * MEM + kt * 128:dt_ * MEM + (kt + 1) * 128], qx[:, dt_ * G:(dt_ + 1) * G],
                           dd == 0, dd == 1, [kxTb, qxb])
                    p_ap, p_b = pp[kt]
                    S.op("act", lambda e, o=p_ap, p=p: e.activation(out=o, in_=PS[p][:, :], func=AF.Exp, scale=1.0 / 16.0),
                         reads=[psb[p]], writes=[p_b])
                for kt in range(2):
                    p_ap, p_b = pp[kt]
                    for e_ in range(2):
                        mm(e_, PS[e_][:, :], vx[:, kt * D + hd * 256 + e_ * 128:kt * D + hd * 256 + (e_ + 1) * 128], p_ap,
                           kt == 0, kt == 1, [vxb, p_b])
                    mm(2, PS[2][:, :], ones_1, p_ap, kt == 0, kt == 1, [c16buf, p_b])
                ta, tab = tmp32[0]
                S.op("dve", lambda e: e.reciprocal(out=ta, in_=PS[2][:, :]), reads=[psb[2]], writes=[tab])
                for e_ in range(2):
                    S.op("dve", lambda e, e_=e_, o=aux[:, (hd * 2 + e_) * G:(hd * 2 + e_ + 1) * G]:
                         e.tensor_tensor(out=o, in0=PS[e_][:, :], in1=ta, op=ALU.mult), reads=[psb[e_], tab], writes=[auxb])
            proj_norm_resid(wxo_r, aux, auxb, PV["xa_post"])
            ffn(a32, a16, ws, xT, xTb, xn, xnb, hid, hidb, f1, f1b, sq_pair, rstd, rstdb, tmp32, wgu_r[1], wdn_r[1],
                PV["ffn2_pre"], PV["ffn2_post"])
            for c in range(8):
                S.dma("sp", out_d[c * 128:(c + 1) * 128, t0:t0 + G], xT[:, c * G:(c + 1) * G], xTb[c], reads=[xTb[c]], writes=[])
        end_phase(a32, a16)

    if "ab" in stages:
        phase_ab()
    if "d" in stages:
        phase_d()
    if "c" in stages:
        phase_c()
    if "e" in stages:
        phase_e()

    finals = []
    for lst in (XS_b, Q_b, K_b, V_b, Z_b, XBC_b, DT_b, GT_b, ON_b, YN_b):
        pass
    allb = []
    seen = set()
    for e in S.ENGS:
        for o in S.ops[e]:
            if o.dinc is not None and id(o.dinc) not in seen:
                seen.add(id(o.dinc))
                allb.append(o.dinc)
    S.emit(allb)
    return nc


def _tile_w(W):
    K, N = W.shape
    return np.ascontiguousarray(W.reshape(K // 128, 128, N // 128, 128).transpose(2, 1, 0, 3).reshape(N // 128, 128, K))


def _blk_w(W, nb=512):
    K, N = W.shape
    return np.ascontiguousarray(
        W.reshape(K // 128, 128, N // nb, nb).transpose(2, 1, 0, 3).reshape(N // nb, 128, (K // 128) * nb))


def _col(v):
    return np.ascontiguousarray(np.asarray(v).reshape(-1, 128).T)


def _consts():
    c = np.zeros((128, NCONST), np.float32)
    c[:, 0:128] = np.eye(128, dtype=np.float32)
    s = np.arange(128)
    c[:, 128:256] = (s[:, None] <= s[None, :]).astype(np.float32)
    q = np.arange(512)
    for m in range(4):
        c[:, 256 + m * 512:256 + (m + 1) * 512] = (q[None, :] >= 128 * m + s[:, None]).astype(np.float32)
    return c


def prep_shared(inp):
    f = lambda k: np.asarray(inp[k], np.float32)[0]
    sh = {}
    for i, n in ((1, "ffn1"), (2, "ffn2")):
        wgu = f(n + "_w_gu")
        sh["wgu%d" % i] = np.ascontiguousarray(
            wgu.reshape(8, 128, 2, NFF, 128).transpose(3, 1, 2, 0, 4).reshape(NFF, 128, 2048))
        sh["wdn%d" % i] = _tile_w(f(n + "_w_down"))
    win = f("w_in")
    q, k, v, z, xbc, dt, gt = np.split(win, np.cumsum([1024, 1024, 1024, 2048, 3072, 32])[:6], axis=1)
    sh["winf"] = np.concatenate([_tile_w(q), _tile_w(k), _tile_w(xbc), _tile_w(gt)], axis=0)
    sh["wint"] = np.concatenate([_blk_w(v), _blk_w(z)], axis=0)
    sh["wdt"] = np.ascontiguousarray(dt.reshape(8, 128, 32).transpose(1, 0, 2).reshape(128, 256))
    sh["wa"] = _tile_w(f("w_branch_attn"))
    sh["ws"] = _tile_w(f("w_branch_ssd"))
    sh["wmix"] = _tile_w(f("w_mix_out"))
    sh["wxq"] = _tile_w(f("xa_w_q"))
    kv = f("xa_w_kv")
    sh["wxk"] = _tile_w(kv[:, :1024])
    sh["wxv"] = _blk_w(kv[:, 1024:])
    sh["wxo"] = _tile_w(f("xa_w_o"))
    cw = f("ssd_conv_w")
    convw = np.ascontiguousarray(cw.reshape(4, 24, 128).transpose(2, 1, 0).reshape(128, 96))
    pvec = np.concatenate([_col(f("ffn1_pre_g")), _col(f("ffn1_post_g")), _col(f("mix_pre_g")), _col(f("mix_post_g")),
                           _col(f("xa_pre_g")), _col(f("xa_post_g")), _col(f("mem_norm_g")), _col(f("ffn2_pre_g")),
                           _col(f("ffn2_post_g")), _col(f("b_gate")), _col(f("da_subln_g")), convw,
                           _col(f("ssd_conv_b")), _col(f("ssd_norm_g"))], axis=1).astype(np.float32)
    assert pvec.shape == (128, NPV), pvec.shape
    sh["pvec"] = np.ascontiguousarray(pvec)
    row = np.concatenate([f("ssd_dt_bias"), f("ssd_A_log"), f("ssd_D"), f("da_lambda_q1"), f("da_lambda_k1"),
                          f("da_lambda_q2"), f("da_lambda_k2")])[None, :].astype(np.float32)
    assert row.shape == (1, NROW)
    sh["row"] = np.ascontiguousarray(row)
    sh["consts"] = _consts()
    return sh


def core_inputs(x, mem, b, r, T, TP):
    m = {}
    m["xT"] = np.ascontiguousarray(x[b, r * T:(r + 1) * T].T)
    m["xTp"] = np.ascontiguousarray(x[b, 0:TP].T)
    fl = np.zeros((128, 2), np.float32)
    fl[:, 0] = 1.0 if r else 0.0
    fl[:, 1] = 0.0 if r else NEG
    m["flags"] = fl
    m["memT"] = np.ascontiguousarray(mem[b].T)
    return m


def kernel(**inputs):
    x = np.asarray(inputs["x"], np.float32)
    mem = np.asarray(inputs["mem"], np.float32)
    B, SEQ_, _ = x.shape
    T = SEQ_ // 2
    sh = prep_shared(inputs)
    nc = build_program(T, T)
    in_maps = []
    for c in range(8):
        m = dict(sh)
        m.update(core_inputs(x, mem, c // 2, c % 2, T, T))
        in_maps.append(m)
    res = run_bass_kernel_spmd(nc, in_maps, core_ids=list(range(8)))
    out = np.empty((B, SEQ_, D), np.float32)
    for c in range(8):
        out[c // 2, (c % 2) * T:(c % 2 + 1) * T] = res.results[c]["outT"].T
    return out
```

```python
import numpy as np
import ml_dtypes
import concourse.bass as bass
import concourse.mybir as mybir
from concourse.bass_utils import run_bass_kernel_spmd

F32 = mybir.dt.float32
BF16 = mybir.dt.bfloat16
AF = mybir.ActivationFunctionType
ALU = mybir.AluOpType

D = 1024
DFF = 2816
NFF = DFF // 128
SEQ = 4096
G = 512
NH = 8
SH = 32
SG = 4
MEM = 256
NEG = -30000.0

PV = {}
_o = 0
for _n, _w in [("ffn1_pre", 8), ("ffn1_post", 8), ("mix_pre", 8), ("mix_post", 8), ("xa_pre", 8),
               ("xa_post", 8), ("mem_g", 8), ("ffn2_pre", 8), ("ffn2_post", 8), ("b_gate", 16),
               ("subln", 1), ("conv_w", 96), ("conv_b", 24), ("ssd_g", 16)]:
    PV[_n] = _o
    _o += _w
NPV = _o
NROW = 96 + 256
NCONST = 256 + 2048


class Buf:
    __slots__ = ("name", "w", "r", "dsem", "dcnt")

    def __init__(self, name, prior=None):
        self.name = name
        self.w = None
        self.r = list(prior) if prior else []
        self.dsem = None
        self.dcnt = 0

    def tokens(self):
        t = list(self.r)
        if self.w is not None:
            t.append(self.w)
        return t


class BL(list):
    pass


class Op:
    __slots__ = ("eng", "fn", "deps", "dwaits", "need", "ticket", "dinc")

    def __init__(self, eng, fn):
        self.eng = eng
        self.fn = fn
        self.deps = []
        self.dwaits = []
        self.need = False
        self.ticket = None
        self.dinc = None


class Sched:
    ENGS = ("pe", "act", "dve", "pool", "sp")

    def __init__(self, nc):
        self.nc = nc
        self.ops = {e: [] for e in self.ENGS}
        self.nsem = 0

    def _dep(self, op, tok, kind):
        if isinstance(tok, Op):
            if tok.eng == op.eng:
                if op.eng == "pe" or kind != "raw":
                    return
            tok.need = True
            op.deps.append(tok)
        else:
            sb = tok[1]
            op.dwaits.append((sb, sb.dcnt))

    @staticmethod
    def _flat(bs):
        out = []
        for b in bs:
            if isinstance(b, (list, tuple)):
                out.extend(Sched._flat(b))
            else:
                out.append(b)
        return out

    def _track(self, op, tok, reads, writes):
        reads = self._flat(reads)
        writes = self._flat(writes)
        for b in reads:
            if b.w is not None:
                self._dep(op, b.w, "raw")
        for b in writes:
            if b.w is not None:
                self._dep(op, b.w, "waw")
            for t in b.r:
                if t is not op:
                    self._dep(op, t, "war")
        key = tok.eng if isinstance(tok, Op) else id(tok[1])
        for b in reads:
            b.r = [t for t in b.r if (t.eng if isinstance(t, Op) else id(t[1])) != key]
            b.r.append(tok)
        for b in writes:
            b.w = tok
            b.r = []

    def op(self, eng, fn, reads=(), writes=()):
        o = Op(eng, fn)
        for b in self._flat(reads):
            if b.name.startswith("psb"):
                for t in b.r:
                    assert not isinstance(t, Op) or t.eng == eng, ("two engines read one PSUM bank", b.name, eng, t.eng)
        self._track(o, o, reads, writes)
        self.ops[eng].append(o)
        return o

    def dma(self, q, out, in_, sb, reads=(), writes=()):
        o = Op(q, lambda e: e.dma_start(out=out, in_=in_))
        if sb.dsem is None:
            sb.dsem = self.nc.alloc_semaphore(name="d%d_%s" % (self.nsem, sb.name))
            self.nsem += 1
        tok = ("d", sb)
        self._track(o, tok, reads, writes)
        sb.dcnt += 16
        o.dinc = sb
        self.ops[q].append(o)
        return o

    def emit(self, final_waits):
        nc = self.nc
        sems = {e: nc.alloc_semaphore(name="eng_" + e) for e in self.ENGS}
        for e in self.ENGS:
            t = 0
            for o in self.ops[e]:
                if o.need:
                    t += 1
                    o.ticket = t
        handles = {"pe": "tensor", "act": "scalar", "dve": "vector", "pool": "gpsimd", "sp": "sync"}
        with nc.Block() as block:
            for e in self.ENGS:
                ops = self.ops[e]
                fw = final_waits if e == "sp" else []

                def body(eng, ops=ops, e=e, fw=fw):
                    waited = {}
                    for o in ops:
                        need = {}
                        for d in o.deps:
                            k = ("e", d.eng)
                            if d.ticket > need.get(k, (None, 0))[1]:
                                need[k] = (sems[d.eng], d.ticket)
                        for sb, v in o.dwaits:
                            k = ("d", id(sb))
                            if v > need.get(k, (None, 0))[1]:
                                need[k] = (sb.dsem, v)
                        for k, (s, v) in need.items():
                            if waited.get(k, 0) < v:
                                eng.wait_ge(s, v)
                                waited[k] = v
                        ins = o.fn(eng)
                        if o.dinc is not None:
                            ins.then_inc(o.dinc.dsem, 16)
                        elif o.ticket is not None:
                            ins.then_inc(sems[e], 1)
                    for sb in fw:
                        eng.wait_ge(sb.dsem, sb.dcnt)

                getattr(block, handles[e])(body)


def build_program(T, TP=0, stages=("ab", "c", "d", "e"), debug=False):
    NG_ = T // G
    NPG = TP // G
    KVT = TP + T
    NKG = KVT // G
    NCH = KVT // 128
    nc = bass.Bass("TRN2", target_bir_lowering=False)
    S = Sched(nc)
    okind = "ExternalOutput" if debug else "Internal"

    def din(name, shape, dt=F32):
        return nc.dram_tensor(name, list(shape), dt, kind="ExternalInput").ap()

    def dscr(name, shape, dt):
        return nc.dram_tensor(name, list(shape), dt, kind=okind).ap()

    xT_d = din("xT", [D, T])
    xTp_d = din("xTp", [D, max(TP, G)])
    flags_d = din("flags", [128, 2])
    memT_d = din("memT", [D, MEM])
    pvec_d = din("pvec", [128, NPV])
    row_d = din("row", [1, NROW])
    const_d = din("consts", [128, NCONST])
    wgu_d = [din("wgu%d" % i, [NFF, 128, 2048]) for i in (1, 2)]
    wdn_d = [din("wdn%d" % i, [8, 128, DFF]) for i in (1, 2)]
    winf_d = din("winf", [56, 128, 1024])
    wint_d = din("wint", [6, 128, 4096])
    wdt_d = din("wdt", [128, 256])
    wa_d = din("wa", [8, 128, 1024])
    ws_d = din("ws", [8, 128, 2048])
    wmix_d = din("wmix", [8, 128, 1024])
    wxq_d = din("wxq", [8, 128, 1024])
    wxk_d = din("wxk", [8, 128, 1024])
    wxv_d = din("wxv", [2, 128, 4096])
    wxo_d = din("wxo", [8, 128, 1024])
    out_d = nc.dram_tensor("outT", [D, T], F32, kind="ExternalOutput").ap()

    XS_d = dscr("XS", [D, T], F32)
    _W = {}
    QT_d = dscr("QT", [D, T], BF16)
    KT_d = dscr("KT", [D, KVT], BF16)
    V_d = dscr("V", [KVT, D], BF16)
    Z_d = dscr("Z", [T, 2048], BF16)
    XBC_d = dscr("XBC", [3072, KVT], BF16)
    DT_d = dscr("DT", [KVT, 32], F32)
    GT_d = dscr("GT", [2048, T], BF16)
    ON_d = dscr("ON", [D, T], BF16)
    YN_d = dscr("YN", [2048, T], BF16)

    def dbufs(name, n):
        return [Buf("%s%d" % (name, i)) for i in range(n)]
    XS_b, Q_b, Z_b, GT_b, ON_b, YN_b = [dbufs(n, NG_) for n in ("XS", "Q", "Z", "GT", "ON", "YN")]
    K_b, V_b, XBC_b, DT_b = [dbufs(n, NKG) for n in ("K", "V", "XBC", "DT")]

    A32 = nc.alloc_sbuf_tensor("A32", [128, 12288], F32)
    A16 = nc.alloc_sbuf_tensor("A16", [128, 57344], BF16)
    CST = nc.alloc_sbuf_tensor("CST", [128, NPV + NROW + NCONST + 64], F32)
    C16 = nc.alloc_sbuf_tensor("C16", [128, 128 * 4 + 2048 + 32 * 128], BF16)
    PS = [nc.alloc_psum_tensor("ps%d" % i, [128, 512], F32) for i in range(8)]

    class Arena:
        def __init__(self, t, prior):
            self.t = t
            self.off = 0
            self.prior = prior
            self.bufs = []

        def take(self, name, n):
            ap = self.t[:, self.off:self.off + n]
            self.off += n
            assert self.off <= self.t.shape[1], (name, self.off)
            b = Buf(name, self.prior)
            self.bufs.append(b)
            return ap, b

        def take_chunks(self, name, nch, width):
            ap, b0 = self.take(name, nch * width)
            bl = BL([b0] + [Buf("%s_%d" % (name, i), self.prior) for i in range(1, nch)])
            self.bufs.extend(bl[1:])
            return ap, bl

    state = {"prior32": [], "prior16": [], "priorps": []}
    psb = [Buf("psb%d" % i) for i in range(8)]

    def new_phase():
        toks = []
        for b in state.get("bufs", []):
            toks.extend(b.tokens())
        seen = set()
        pr = []
        for t in toks:
            k = id(t) if isinstance(t, Op) else ("d", id(t[1]))
            if k not in seen:
                seen.add(k)
                pr.append(t)
        a32 = Arena(A32, pr)
        a16 = Arena(A16, pr)
        state["a32"], state["a16"] = a32, a16
        return a32, a16

    def end_phase(a32, a16):
        state["bufs"] = a32.bufs + a16.bufs

    pv = CST[:, 0:NPV]
    rowb = CST[:, NPV:NPV + NROW]
    cst = CST[:, NPV + NROW:NPV + NROW + NCONST]
    lamc = CST[:, NPV + NROW + NCONST:NPV + NROW + NCONST + 64]
    cbuf = Buf("consts")
    c16buf = Buf("c16")
    ident32 = cst[:, 0:128]
    tri32 = cst[:, 128:256]
    identb = C16[:, 0:128]
    ones_d = C16[:, 128:256]
    ones_1 = C16[:, 256:384]
    ones_h = C16[:, 384:512]
    maskb = C16[:, 512:512 + 2048]
    Ddiag = C16[:, 2560:2560 + 4096]

    S.dma("sp", CST[:, 0:NPV], pvec_d[:, :], cbuf, writes=[cbuf])
    S.dma("sp", rowb, row_d[0:1, :].partition_broadcast(128), cbuf, writes=[cbuf])
    S.dma("sp", cst, const_d[:, :], cbuf, writes=[cbuf])
    S.dma("sp", lamc[:, 16:18], flags_d[:, :], cbuf, writes=[cbuf])
    fl_valid = lamc[:, 16:17]
    fl_bias = lamc[:, 17:18]
    S.op("dve", lambda e: e.tensor_copy(out=identb, in_=ident32), reads=[cbuf], writes=[c16buf])
    S.op("dve", lambda e: e.memset(ones_d, 1.0 / 1024.0), writes=[c16buf])
    S.op("dve", lambda e: e.memset(ones_1, 1.0), writes=[c16buf])
    S.op("dve", lambda e: e.memset(ones_h, 1.0 / 128.0), writes=[c16buf])
    S.op("dve", lambda e: e.tensor_copy(out=maskb, in_=cst[:, 256:256 + 2048]), reads=[cbuf], writes=[c16buf])
    for h in range(SH):
        S.op("dve", lambda e, h=h: e.tensor_scalar(out=Ddiag[:, h * 128:(h + 1) * 128], in0=ident32,
                                                     scalar1=rowb[:, 64 + h:65 + h], scalar2=None, op0=ALU.mult),
             reads=[cbuf], writes=[c16buf])
    LAM_INIT = 0.8 - 0.6 * 1.0
    lbuf = Buf("lam")
    tmpl = lamc[:, 0:64]
    s1 = lamc[:, 0:1]

    def lam_ops():
        a32, a16 = new_phase()
        t1, t1b = a32.take("lt1", 64)
        t2, t2b = a32.take("lt2", 64)
        acc, accb = a32.take("lacc", 4)
        S.op("dve", lambda e: e.tensor_tensor(out=t1, in0=rowb[:, 96:160], in1=rowb[:, 160:224], op=ALU.mult),
             reads=[cbuf], writes=[t1b])
        S.op("dve", lambda e: e.reduce_sum(out=acc[:, 0:1], in_=t1, axis=mybir.AxisListType.X),
             reads=[t1b], writes=[accb])
        S.op("dve", lambda e: e.tensor_tensor(out=t2, in0=rowb[:, 224:288], in1=rowb[:, 288:352], op=ALU.mult),
             reads=[cbuf], writes=[t2b])
        S.op("dve", lambda e: e.reduce_sum(out=acc[:, 1:2], in_=t2, axis=mybir.AxisListType.X),
             reads=[t2b, accb], writes=[accb])
        S.op("act", lambda e: e.activation(out=acc[:, 2:4], in_=acc[:, 0:2], func=AF.Exp), reads=[accb], writes=[accb])
        S.op("dve", lambda e: e.tensor_tensor(out=lamc[:, 0:1], in0=acc[:, 2:3], in1=acc[:, 3:4], op=ALU.subtract),
             reads=[accb], writes=[lbuf])
        S.op("dve", lambda e: e.tensor_scalar(out=lamc[:, 0:1], in0=lamc[:, 0:1], scalar1=LAM_INIT, scalar2=None,
                                                op0=ALU.add), reads=[lbuf], writes=[lbuf])
        S.op("dve", lambda e: e.tensor_scalar(out=lamc[:, 1:2], in0=pv[:, PV["subln"]:PV["subln"] + 1],
                                                scalar1=1.0 - LAM_INIT, scalar2=None, op0=ALU.mult),
             reads=[cbuf, lbuf], writes=[lbuf])
        end_phase(a32, a16)
    lam_ops()
    lam_col = lamc[:, 0:1]
    gsub_col = lamc[:, 1:2]
    import math
    kbuf = Buf("kconst")
    k_eps6, k_eps5, k_lnhalf, k_one, k_zero = [lamc[:, 8 + i:9 + i] for i in range(5)]
    for ap_, v_ in ((k_eps6, 1e-6), (k_eps5, 1e-5), (k_lnhalf, math.log(0.5)), (k_one, 1.0), (k_zero, 0.0)):
        S.op("dve", lambda e, a=ap_, v=v_: e.memset(a, v), writes=[kbuf])

    class WRef:
        __slots__ = ("f32", "scr", "buf")

        def __init__(self, f32, scr):
            self.f32, self.scr, self.buf = f32, scr, Buf("wscr")

    def wrefs(name, d_ap):
        if d_ap.ndim == 3:
            scr = nc.dram_tensor(name + "_bf", list(d_ap.shape), BF16, kind="Internal").ap()
            return [WRef(d_ap[i], scr[i]) for i in range(d_ap.shape[0])]
        return [WRef(d_ap, None)]

    class WStream:
        def __init__(self, a32, a16, nstage, nslot, cap, scap=2048):
            self.sl = [a16.take("wbf%d" % i, cap) for i in range(nslot)]
            self.stb = [Buf("wst%d" % i) for i in range(nslot)]
            self.j = 0

        def load(self, wr, n, cast_eng=None):
            sl_ap, sl_b = self.sl[self.j % len(self.sl)]
            st_b = self.stb[self.j % len(self.sl)]
            self.j += 1
            if not isinstance(wr, WRef):
                S.dma("pool", sl_ap[:, 0:n], wr, sl_b, writes=[sl_b])
            elif wr.buf.w is None:
                S.dma("pool", sl_ap[:, 0:n], wr.f32, sl_b, writes=[sl_b])
                if wr.scr is not None:
                    S.dma("sp", wr.scr, sl_ap[:, 0:n], st_b, reads=[sl_b], writes=[wr.buf])
            else:
                S.dma("pool", sl_ap[:, 0:n], wr.scr, sl_b, reads=[wr.buf], writes=[sl_b])
            return sl_ap[:, 0:n], sl_b

    wgu_r = [wrefs("wgu%d" % (i + 1), wgu_d[i]) for i in range(2)]
    wdn_r = [wrefs("wdn%d" % (i + 1), wdn_d[i]) for i in range(2)]
    winf_r = wrefs("winf", winf_d)
    wint_r = wrefs("wint", wint_d)
    wa_r, ws_r, wmix_r, wxq_r, wxo_r = [wrefs(n, d) for n, d in
                                        (("wa", wa_d), ("ws", ws_d), ("wmix", wmix_d), ("wxq", wxq_d), ("wxo", wxo_d))]
    evac_rr = [0]

    def evac_copy(out_ap, in_ap, reads, writes):
        evac_rr[0] += 1
        if evac_rr[0] % 2:
            return S.op("act", lambda e: e.activation(out=out_ap, in_=in_ap, func=AF.Copy), reads=reads, writes=writes)
        return S.op("dve", lambda e: e.tensor_copy(out=out_ap, in_=in_ap), reads=reads, writes=writes)

    def mm(ps_i, out_ap, lhsT, rhs, start, stop, reads):
        return S.op("pe", lambda e: e.matmul(out_ap, lhsT, rhs, start=start, stop=stop), reads=reads, writes=[psb[ps_i]])

    def rsqrt_ps(ps_i, width, out_ap, out_b, eps, mul=1.0):
        eb = {1e-6: k_eps6, 1e-5: k_eps5}[eps]
        mb = {1.0: k_zero, 0.5: k_lnhalf}[mul]
        S.op("act", lambda e: e.activation(out=out_ap, in_=PS[ps_i][:, 0:width], func=AF.Ln, bias=eb),
             reads=[psb[ps_i], kbuf], writes=[out_b])
        S.op("act", lambda e: e.activation(out=out_ap, in_=out_ap, func=AF.Exp, scale=-0.5, bias=mb),
             reads=[out_b, kbuf], writes=[out_b])

    def rms_stats(xsrc, xb, nchunk, width, sq_pair, ones_ap, ps_i, rstd_ap, rstd_b, eps):
        for c in range(nchunk):
            sq_ap, sq_b = sq_pair[c % 2]
            S.op("act", lambda e, o=sq_ap[:, 0:width], i_=xsrc(c): e.activation(out=o, in_=i_, func=AF.Square),
                 reads=[xb[c] if isinstance(xb, BL) else xb], writes=[sq_b])
            mm(ps_i, PS[ps_i][:, 0:width], ones_ap, sq_ap[:, 0:width], c == 0, c == nchunk - 1, [sq_b, c16buf])
        rsqrt_ps(ps_i, width, rstd_ap, rstd_b, eps)

    def norm_cast(xsrc, xb, gcol0, rstd_ap, rstd_b, dst, dst_b, nchunk=8):
        for c in range(nchunk):
            S.op("dve", lambda e, c=c: e.scalar_tensor_tensor(out=dst(c), in0=xsrc(c), scalar=pv[:, gcol0 + c:gcol0 + c + 1],
                                                               in1=rstd_ap, op0=ALU.mult, op1=ALU.mult),
                 reads=[xb[c] if isinstance(xb, BL) else xb, rstd_b, cbuf],
                 writes=[dst_b[c] if isinstance(dst_b, BL) else dst_b])

    def ffn(a32, a16, ws, xT, xTb, xn, xnb, hid, hidb, f1, f1b, sq_pair, rstd, rstdb, tmp32, wgu, wdn, gpre, gpost):
        xc = lambda c: xT[:, c * G:(c + 1) * G]
        rms_stats(xc, xTb, 8, G, sq_pair, ones_d, 7, rstd, rstdb, 1e-6)
        norm_cast(xc, xTb, gpre, rstd, rstdb, lambda c: xn[:, c * G:(c + 1) * G], xnb)
        for i in range(NFF):
            w, wb = ws.load(wgu[i], 2048)
            pg, pu = (0, 1) if i % 2 == 0 else (2, 3)
            for c in range(8):
                mm(pg, PS[pg][:, :], w[:, c * 128:(c + 1) * 128], xn[:, c * G:(c + 1) * G], c == 0, c == 7, [wb, xnb[c]])
            for c in range(8):
                mm(pu, PS[pu][:, :], w[:, 1024 + c * 128:1024 + (c + 1) * 128], xn[:, c * G:(c + 1) * G],
                   c == 0, c == 7, [wb, xnb[c]])
            t_ap, t_b = tmp32[i % 2]
            S.op("act", lambda e, o=t_ap, i_=PS[pg][:, :]: e.activation(out=o, in_=i_, func=AF.Silu),
                 reads=[psb[pg]], writes=[t_b])
            S.op("dve", lambda e, o=hid[:, i * G:(i + 1) * G], a=t_ap, b=PS[pu][:, :]:
                 e.tensor_tensor(out=o, in0=b, in1=a, op=ALU.mult), reads=[t_b, psb[pu]], writes=[hidb])
        for d in range(8):
            w, wb = ws.load(wdn[d], DFF)
            p = 4 + d % 2
            for i in range(NFF):
                mm(p, PS[p][:, :], w[:, i * 128:(i + 1) * 128], hid[:, i * G:(i + 1) * G], i == 0, i == NFF - 1, [wb, hidb])
            sq_ap, sq_b = sq_pair[d % 2]
            S.op("dve", lambda e, o=f1[:, d * G:(d + 1) * G], i_=PS[p][:, :]: e.tensor_copy(out=o, in_=i_),
                 reads=[psb[p]], writes=[f1b[d]])
            S.op("act", lambda e, o=sq_ap, i_=f1[:, d * G:(d + 1) * G]: e.activation(out=o, in_=i_, func=AF.Square),
                 reads=[f1b[d]], writes=[sq_b])
            mm(6, PS[6][:, :], ones_d, sq_ap, d == 0, d == 7, [sq_b, c16buf])
        rsqrt_ps(6, G, rstd, rstdb, 1e-6, 0.5)
        resid_add(xT, xTb, f1, f1b, rstd, rstdb, gpost)

    def resid_add(xT, xTb, f1, f1b, rstd, rstdb, gcol0):
        for d in range(8):
            eng = "dve"
            S.op("dve", lambda e, d=d: e.scalar_tensor_tensor(out=f1[:, d * G:(d + 1) * G], in0=f1[:, d * G:(d + 1) * G],
                                                               scalar=pv[:, gcol0 + d:gcol0 + d + 1], in1=rstd,
                                                               op0=ALU.mult, op1=ALU.mult),
                 reads=[f1b[d], rstdb, cbuf], writes=[f1b[d]])
            S.op(eng, lambda e, d=d: e.tensor_tensor(out=xT[:, d * G:(d + 1) * G], in0=f1[:, d * G:(d + 1) * G],
                                                      in1=xT[:, d * G:(d + 1) * G], op=ALU.add),
                 reads=[f1b[d], xTb[d]], writes=[xTb[d]])

    def phase_ab():
        NSLOT = 8
        a32, a16 = new_phase()
        xT, xTb = a32.take_chunks("xT", 8, G)
        f1, f1b = a32.take_chunks("f1", 8, G)
        rstd, rstdb = a32.take("rstd", G)
        tmp32 = [a32.take("tmp%d" % i, G) for i in range(2)]
        xn, xnb = a16.take_chunks("xn", 8, G)
        hid, hidb = a16.take("hid", NFF * G)
        sq_pair = [a16.take("sq%d" % i, G) for i in range(2)]
        stg = [a16.take("stg%d" % i, G) for i in range(4)]
        dtst = [a32.take("dtst%d" % i, 32) for i in range(2)]
        ws = WStream(a32, a16, 3, NSLOT, 4096)
        wdt32, wdt32b = a32.take("wdt32", 256)
        wdtb, wdtbb = a16.take("wdtb", 256)
        S.dma("sp", wdt32, wdt_d[:, :], wdt32b, writes=[wdt32b])
        S.op("dve", lambda e: e.tensor_copy(out=wdtb, in_=wdt32), reads=[wdt32b], writes=[wdtbb])
        sti = [0]

        def stage():
            sti[0] += 1
            return stg[sti[0] % 4]

        for gg in range(NKG):
            pre = gg < NPG
            g = gg - NPG
            t0 = g * G
            k0 = gg * G
            src = xTp_d[:, gg * G:(gg + 1) * G] if pre else xT_d[:, t0:t0 + G]
            for c in range(8):
                S.dma("sp", xT[:, c * G:(c + 1) * G], src[c * 128:(c + 1) * 128, :], xTb[c], writes=[xTb[c]])
            ffn(a32, a16, ws, xT, xTb, xn, xnb, hid, hidb, f1, f1b, sq_pair, rstd, rstdb, tmp32, wgu_r[0], wdn_r[0],
                PV["ffn1_pre"], PV["ffn1_post"])
            if not pre:
                for c in range(8):
                    S.dma("sp", XS_d[c * 128:(c + 1) * 128, t0:t0 + G], xT[:, c * G:(c + 1) * G], xTb[c],
                          reads=[xTb[c]], writes=[XS_b[g]])
            xc = lambda c: xT[:, c * G:(c + 1) * G]
            rms_stats(xc, xTb, 8, G, sq_pair, ones_d, 7, rstd, rstdb, 1e-6)
            norm_cast(xc, xTb, PV["mix_pre"], rstd, rstdb, lambda c: xn[:, c * G:(c + 1) * G], xnb)
            for ft in range(56):
                if pre and (ft < 8 or ft >= 40):
                    continue
                w, wb = ws.load(winf_r[ft], 1024)
                p = ft % 4
                for c in range(8):
                    mm(p, PS[p][:, :], w[:, c * 128:(c + 1) * 128], xn[:, c * G:(c + 1) * G], c == 0, c == 7, [wb, xnb[c]])
                st_ap, st_b = stage()
                if ft < 8:
                    dst, db = QT_d[ft * 128:(ft + 1) * 128, t0:t0 + G], Q_b[g]
                elif ft < 16:
                    dst, db = KT_d[(ft - 8) * 128:(ft - 7) * 128, k0:k0 + G], K_b[gg]
                elif ft < 40:
                    dst, db = XBC_d[(ft - 16) * 128:(ft - 15) * 128, k0:k0 + G], XBC_b[gg]
                else:
                    dst, db = GT_d[(ft - 40) * 128:(ft - 39) * 128, t0:t0 + G], GT_b[g]
                if ft >= 40:
                    bcol = pv[:, PV["b_gate"] + ft - 40:PV["b_gate"] + ft - 39]
                    S.op("act", lambda e, o=st_ap, i_=PS[p][:, :], b=bcol: e.activation(out=o, in_=i_, func=AF.Sigmoid, bias=b),
                         reads=[psb[p], cbuf], writes=[st_b])
                else:
                    evac_copy(st_ap, PS[p][:, :], [psb[p]], [st_b])
                S.dma("sp", dst, st_ap, st_b, reads=[st_b], writes=[db])
            for blk in range(6):
                if pre and blk >= 2:
                    continue
                w, wb = ws.load(wint_r[blk], 4096)
                for tt in range(G // 128):
                    p = 4 + (blk * 4 + tt) % 2
                    for c in range(8):
                        mm(p, PS[p][:, :], xn[:, c * G + tt * 128:c * G + (tt + 1) * 128], w[:, c * 512:(c + 1) * 512],
                           c == 0, c == 7, [wb, xnb[c]])
                    st_ap, st_b = stage()
                    evac_copy(st_ap, PS[p][:, :], [psb[p]], [st_b])
                    if blk < 2:
                        r0 = k0 + tt * 128
                        S.dma("sp", V_d[r0:r0 + 128, blk * 512:(blk + 1) * 512], st_ap, st_b, reads=[st_b], writes=[V_b[gg]])
                    else:
                        r0 = t0 + tt * 128
                        S.dma("sp", Z_d[r0:r0 + 128, (blk - 2) * 512:(blk - 1) * 512], st_ap, st_b, reads=[st_b],
                              writes=[Z_b[g]])
            for tt in range(G // 128):
                for c in range(8):
                    mm(6, PS[6][:, 0:32], xn[:, c * G + tt * 128:c * G + (tt + 1) * 128], wdtb[:, c * 32:(c + 1) * 32],
                       c == 0, c == 7, [wdtbb, xnb[c]])
                d_ap, d_b = dtst[tt % 2]
                S.op("dve", lambda e, o=d_ap, i_=PS[6][:, 0:32]: e.tensor_copy(out=o, in_=i_), reads=[psb[6]], writes=[d_b])
                r0 = k0 + tt * 128
                S.dma("sp", DT_d[r0:r0 + 128, :], d_ap, d_b, reads=[d_b], writes=[DT_b[gg]])
        end_phase(a32, a16)

    def phase_c():
        a32, a16 = new_phase()
        kT = [a16.take("kT%d" % i, KVT) for i in range(2)]
        vv = [a16.take("vv%d" % i, KVT) for i in range(2)]
        qT = [a16.take("qT%d" % i, G) for i in range(2)]
        pp = [a16.take("pp%d" % i, G) for i in range(4)]
        sq_pair = [a16.take("sqc%d" % i, G) for i in range(2)]
        ost = [a16.take("ost%d" % i, G) for i in range(2)]
        r1, r1b = a32.take("r1", G)
        r2, r2b = a32.take("r2", G)
        o1, o1b = a32.take("o1", G)
        o2, o2b = a32.take("o2", G)
        rs, rsb = a32.take("rsc", G)
        pi = [0]

        def load_kv(h):
            k_ap, k_b = kT[h % 2]
            v_ap, v_b = vv[h % 2]
            S.dma("sp", k_ap, KT_d[h * 128:(h + 1) * 128, :], k_b, reads=K_b, writes=[k_b])
            for q4 in range(0, NCH, 8):
                S.dma("sp", v_ap[:, q4 * 128:(q4 + 8) * 128].rearrange("p (k e) -> p k e", e=128),
                      V_d[q4 * 128:(q4 + 8) * 128, h * 128:(h + 1) * 128].rearrange("(k p) e -> p k e", p=128), v_b,
                      reads=V_b, writes=[v_b])

        def load_q(h, j):
            q_ap, q_b = qT[(h * NG_ + j) % 2]
            S.dma("sp", q_ap, QT_d[h * 128:(h + 1) * 128, j * G:(j + 1) * G], q_b, reads=[Q_b[j]], writes=[q_b])

        load_kv(0)
        load_q(0, 0)
        for h in range(NH):
            k_ap, k_b = kT[h % 2]
            v_ap, v_b = vv[h % 2]
            if h + 1 < NH:
                load_kv(h + 1)
            for j in range(NG_):
                q_ap, q_b = qT[(h * NG_ + j) % 2]
                if j + 1 < NG_:
                    load_q(h, j + 1)
                elif h + 1 < NH:
                    load_q(h + 1, 0)
                npk = NPG * 4
                nkt = npk + 4 * j + 4

                def emit_s(kt, k_ap=k_ap, k_b=k_b, q_ap=q_ap, q_b=q_b, j=j):
                    sa, sb_ = (4, 5) if kt % 2 == 0 else (6, 7)
                    mm(sa, PS[sa][:, :], k_ap[0:64, kt * 128:(kt + 1) * 128], q_ap[0:64, :], True, True, [k_b, q_b])
                    mm(sb_, PS[sb_][:, :], k_ap[64:128, kt * 128:(kt + 1) * 128], q_ap[64:128, :], True, True, [k_b, q_b])
                    p1, p1b = pp[pi[0] % 4]
                    p2, p2b = pp[(pi[0] + 1) % 4]
                    pi[0] += 2
                    bias_ = fl_bias if kt < NPG * 4 else k_zero
                    S.op("act", lambda e, o=p1, i_=PS[sa][:, :], b_=bias_: e.activation(out=o, in_=i_, func=AF.Exp, scale=0.125,
                                                                                     bias=b_),
                         reads=[psb[sa], cbuf, kbuf], writes=[p1b])
                    S.op("act", lambda e, o=p2, i_=PS[sb_][:, :], b_=bias_: e.activation(out=o, in_=i_, func=AF.Exp, scale=0.125,
                                                                                      bias=b_),
                         reads=[psb[sb_], cbuf, kbuf], writes=[p2b])
                    m = kt - NPG * 4 - 4 * j
                    if m >= 0:
                        mk = maskb[:, m * 512:(m + 1) * 512]
                        S.op("dve", lambda e, o=p1, mk=mk: e.tensor_tensor(out=o, in0=o, in1=mk, op=ALU.mult),
                             reads=[p1b, c16buf], writes=[p1b])
                        S.op("dve", lambda e, o=p2, mk=mk: e.tensor_tensor(out=o, in0=o, in1=mk, op=ALU.mult),
                             reads=[p2b, c16buf], writes=[p2b])
                    return p1, p1b, p2, p2b

                pend = emit_s(0)
                for kt in range(nkt):
                    p1, p1b, p2, p2b = pend
                    if kt + 1 < nkt:
                        pend = emit_s(kt + 1)
                    first, last = kt == 0, kt == nkt - 1
                    vt = v_ap[:, kt * 128:(kt + 1) * 128]
                    mm(0, PS[0][:, :], vt, p1, first, last, [v_b, p1b])
                    mm(1, PS[1][:, :], vt, p2, first, last, [v_b, p2b])
                    mm(2, PS[2][:, :], ones_1, p1, first, last, [c16buf, p1b])
                    mm(3, PS[3][:, :], ones_1, p2, first, last, [c16buf, p2b])
                S.op("dve", lambda e: e.tensor_copy(out=o1, in_=PS[0][:, :]), reads=[psb[0]], writes=[o1b])
                S.op("dve", lambda e: e.tensor_copy(out=o2, in_=PS[1][:, :]), reads=[psb[1]], writes=[o2b])
                S.op("dve", lambda e: e.tensor_copy(out=r1, in_=PS[2][:, :]), reads=[psb[2]], writes=[r1b])
                S.op("dve", lambda e: e.tensor_copy(out=r2, in_=PS[3][:, :]), reads=[psb[3]], writes=[r2b])
                S.op("dve", lambda e: e.reciprocal(out=r1, in_=r1), reads=[r1b], writes=[r1b])
                S.op("dve", lambda e: e.reciprocal(out=r2, in_=r2), reads=[r2b], writes=[r2b])
                S.op("dve", lambda e: e.tensor_tensor(out=o1, in0=o1, in1=r1, op=ALU.mult),
                     reads=[o1b, r1b], writes=[o1b])
                S.op("dve", lambda e: e.scalar_tensor_tensor(out=o2, in0=o2, scalar=lam_col, in1=r2,
                                                             op0=ALU.mult, op1=ALU.mult),
                     reads=[o2b, r2b, lbuf], writes=[o2b])
                S.op("dve", lambda e: e.tensor_tensor(out=o1, in0=o1, in1=o2, op=ALU.subtract),
                     reads=[o1b, o2b], writes=[o1b])
                sq_ap, sq_b = sq_pair[j % 2]
                S.op("act", lambda e, o=sq_ap: e.activation(out=o, in_=o1, func=AF.Square), reads=[o1b], writes=[sq_b])
                mm(4, PS[4][:, :], ones_h, sq_ap, True, True, [sq_b, c16buf])
                rsqrt_ps(4, G, rs, rsb, 1e-5)
                os_ap, os_b = ost[j % 2]
                S.op("dve", lambda e, o=os_ap: e.scalar_tensor_tensor(out=o, in0=o1, scalar=gsub_col, in1=rs,
                                                                      op0=ALU.mult, op1=ALU.mult),
                     reads=[o1b, rsb, lbuf], writes=[os_b])
                S.dma("sp", ON_d[h * 128:(h + 1) * 128, j * G:(j + 1) * G], os_ap, os_b, reads=[os_b], writes=[ON_b[j]])
        end_phase(a32, a16)


    def phase_d():
        a32, a16 = new_phase()
        dtr, dtrb = a32.take("dtr", 128)
        dtv, dtvb = a32.take("dtv", 128)
        adt, adtb = a32.take("adt", 128)
        nega, negab = a32.take("nega", 32)
        onesf, onesfb = a32.take("onesf", 128)
        acs, acsb = a32.take("acs", 32)
        tot, totb = a32.take("tot", 32)
        dsd, dsdb = a32.take("dsd", 32)
        dab, dabb = a32.take("dab", 32)
        tmps, tmpsb = a32.take("tmps", 32)
        rhs4all, _r40 = a32.take("rhs4all", 4096)
        rhs4 = [(rhs4all[:, i * 512:(i + 1) * 512], _r40 if i == 0 else Buf("rhs4_%d" % i, a32.prior)) for i in range(8)]
        a32.bufs.extend([b_ for _, b_ in rhs4[1:]])
        acsbc = [a32.take("acsbc%d" % i, 512) for i in range(2)]
        dif = [a32.take("dif%d" % i, 512) for i in range(2)]
        St, _stb0 = a32.take("St", 2048)
        Stb = [_stb0] + [Buf("St%d" % i, a32.prior) for i in range(1, 4)]
        a32.bufs.extend(Stb[1:])
        yz, yzb = a32.take("yz", 2048)
        ss, ssb = a32.take("ss", 8)
        xin = [a16.take("xin%d" % i, 520) for i in range(4)]
        xc, xcb = a16.take("xc", 24 * G)
        zt = [a16.take("zt%d" % i, 2048) for i in range(2)]
        xst, xstb = a16.take("xst", 2048)
        btk, btkb = a16.take("btk", 512)
        MT = [a16.take("MT%d" % i, 512) for i in range(2)]
        Ce = [a16.take("Ce%d" % i, 512) for i in range(2)]
        LT = [a16.take("LT%d" % i, 512) for i in range(2)]
        ea = [a16.take("ea%d" % i, 512) for i in range(2)]
        cbm, cbmb = a16.take("cbm", 512)
        trib4, trib4b = a16.take("trib4", 512)
        xd, xdb = a16.take("xd", 2048)
        Sbf, _sbf0 = a16.take("Sbf", 2048)
        Sbfb = [_sbf0] + [Buf("Sbf%d" % i, a16.prior) for i in range(1, 4)]
        a16.bufs.extend(Sbfb[1:])
        yzn, yznb = a16.take("yzn", 2048)
        junk, junkb = yzn[:, 0:512], yznb
        sz, szb = a16.take("sz", 2048)
        ynT = [a16.take("ynT%d" % i, 16 * G) for i in range(1)]
        cdiag, cdiagb = a16.take("cdiag", 96 * 128)
        xdt, xdtb = a16.take("xdt", 2048)
        for fj in range(96):
            S.op("dve", lambda e, fj=fj: e.tensor_scalar(out=cdiag[:, fj * 128:(fj + 1) * 128], in0=ident32,
                                                          scalar1=pv[:, PV["conv_w"] + fj:PV["conv_w"] + fj + 1],
                                                          scalar2=None, op0=ALU.mult), reads=[cbuf], writes=[cdiagb])

        S.op("act", lambda e: e.activation(out=nega, in_=rowb[:, 32:64], func=AF.Exp), reads=[cbuf], writes=[negab])
        S.op("dve", lambda e: e.tensor_scalar(out=nega, in0=nega, scalar1=-1.0, scalar2=None, op0=ALU.mult),
             reads=[negab], writes=[negab])
        S.op("dve", lambda e: e.memset(onesf, 1.0), writes=[onesfb])
        S.op("dve", lambda e: e.memset(St, 0.0), writes=Stb)
        S.op("dve", lambda e: e.memset(Sbf, 0.0), writes=Sbfb)
        for a in range(4):
            S.op("dve", lambda e, a=a: e.tensor_copy(out=trib4[:, a * 128:(a + 1) * 128], in_=tri32), reads=[cbuf],
                 writes=[trib4b])
        psT = [PS[4][:, :].bitcast(BF16), PS[5][:, :].bitcast(BF16)]
        psB = PS[6][:, :].bitcast(BF16)
        xi = [0]
        pending = [None]
        for gg in range(NKG):
            pre = gg < NPG
            g = gg - NPG
            t0 = g * G
            k0 = gg * G
            if NPG > 0 and gg == NPG:
                S.op("dve", lambda e: e.tensor_scalar(out=St, in0=St, scalar1=fl_valid, scalar2=None, op0=ALU.mult),
                     reads=Stb + [cbuf], writes=Stb)
                S.op("dve", lambda e: e.tensor_scalar(out=Sbf, in0=Sbf, scalar1=fl_valid, scalar2=None, op0=ALU.mult),
                     reads=Sbfb + [cbuf], writes=Sbfb)
            for f in range(24):
                x_ap, x_b = xin[xi[0] % 4]
                xi[0] += 1
                if gg == 0:
                    S.op("dve", lambda e, o=x_ap[:, 0:3]: e.memset(o, 0.0), writes=[x_b])
                    S.dma("sp", x_ap[:, 3:3 + G], XBC_d[f * 128:(f + 1) * 128, 0:G], x_b, reads=[XBC_b[0]], writes=[x_b])
                else:
                    S.dma("sp", x_ap[:, 0:3 + G], XBC_d[f * 128:(f + 1) * 128, k0 - 3:k0 + G], x_b,
                          reads=[XBC_b[gg - 1], XBC_b[gg]], writes=[x_b])
                    if gg == NPG:
                        S.op("dve", lambda e, o=x_ap[:, 0:3]: e.tensor_scalar(out=o, in0=o, scalar1=fl_valid, scalar2=None,
                                                                              op0=ALU.mult), reads=[x_b, cbuf], writes=[x_b])
                cp = f % 4
                for j in range(4):
                    mm(cp, PS[cp][:, :], cdiag[:, (f * 4 + j) * 128:(f * 4 + j + 1) * 128], x_ap[:, j:j + G], j == 0, j == 3,
                       [cdiagb, x_b])
                S.op("act", lambda e, o=xc[:, f * G:(f + 1) * G], i_=PS[cp][:, :], b=pv[:, PV["conv_b"] + f:PV["conv_b"] + f + 1]:
                     e.activation(out=o, in_=i_, func=AF.Silu, bias=b), reads=[psb[cp], cbuf], writes=[xcb])
            S.dma("sp", dtr.rearrange("p (k h) -> p k h", h=32),
                  DT_d[k0:k0 + G, :].rearrange("(k p) h -> p k h", p=128), dtrb, reads=[DT_b[gg]], writes=[dtrb])
            for k in range(4):
                S.op("dve", lambda e, k=k: e.tensor_tensor(out=dtr[:, k * 32:(k + 1) * 32], in0=dtr[:, k * 32:(k + 1) * 32],
                                                           in1=rowb[:, 0:32], op=ALU.add), reads=[dtrb, cbuf], writes=[dtrb])
            S.op("act", lambda e: e.activation(out=dtv, in_=dtr, func=AF.Exp), reads=[dtrb], writes=[dtvb])
            S.op("act", lambda e: e.activation(out=dtv, in_=dtv, func=AF.Ln, bias=k_one), reads=[dtvb, kbuf], writes=[dtvb])
            for k in range(4):
                S.op("dve", lambda e, k=k: e.tensor_tensor(out=adt[:, k * 32:(k + 1) * 32], in0=dtv[:, k * 32:(k + 1) * 32],
                                                           in1=nega, op=ALU.mult), reads=[dtvb, negab], writes=[adtb])
            yn_ap, yn_b = ynT[0]
            for k in range(4):
                c0 = k * 128
                r0 = t0 + c0
                z_ap, z_b = zt[k % 2]
                if not pre:
                    S.dma("sp", z_ap, Z_d[r0:r0 + 128, :], z_b, reads=[Z_b[g]], writes=[z_b])
                    S.op("act", lambda e, z_ap=z_ap: e.activation(out=sz, in_=z_ap, func=AF.Silu), reads=[z_b], writes=[szb])
                adk = adt[:, k * 32:(k + 1) * 32]
                dtk = dtv[:, k * 32:(k + 1) * 32]
                if not pre:
                    for q in range(8):
                        r4, r4b = rhs4[q]
                        S.op("dve", lambda e, o=r4, q=q, adk=adk: e.tensor_tensor(
                            out=o.rearrange("p (a b) -> p a b", a=4), in0=tri32.unsqueeze(1).to_broadcast([128, 4, 128]),
                            in1=adk[:, 4 * q:4 * q + 4].unsqueeze(2).to_broadcast([128, 4, 128]), op=ALU.mult),
                            reads=[cbuf, adtb], writes=[r4b])
                for f in range(16):
                    S.op("pe", lambda e, o=psT[f // 8][:, (f % 8) * 128:(f % 8 + 1) * 128], i_=xc[:, f * G + c0:f * G + c0 + 128]:
                         e.transpose(o, i_, identb), reads=[xcb, c16buf], writes=[psb[4 + f // 8]])
                for f in range(4):
                    S.op("pe", lambda e, o=psB[:, f * 128:(f + 1) * 128], i_=xc[:, (16 + f) * G + c0:(16 + f) * G + c0 + 128]:
                         e.transpose(o, i_, identb), reads=[xcb, c16buf], writes=[psb[6]])
                S.op("act", lambda e: e.activation(out=xst[:, 0:1024], in_=psT[0], func=AF.Copy), reads=[psb[4]], writes=[xstb])
                S.op("dve", lambda e: e.tensor_copy(out=xst[:, 1024:2048], in_=psT[1]), reads=[psb[5]], writes=[xstb])
                S.op("dve", lambda e: e.tensor_copy(out=btk, in_=psB[:, 0:512]), reads=[psb[6]], writes=[btkb])
                mm(7, PS[7][:, 0:32], tri32, adk, True, True, [cbuf, adtb])
                mm(7, PS[7][:, 32:64], onesf, adk, True, True, [onesfb, adtb])
                S.op("dve", lambda e: e.tensor_copy(out=acs, in_=PS[7][:, 0:32]), reads=[psb[7]], writes=[acsb])
                S.op("dve", lambda e: e.tensor_copy(out=tot, in_=PS[7][:, 32:64]), reads=[psb[7]], writes=[totb])
                S.op("dve", lambda e: e.tensor_tensor(out=tmps, in0=tot, in1=acs, op=ALU.subtract), reads=[totb, acsb],
                     writes=[tmpsb])
                S.op("act", lambda e: e.activation(out=tmps, in_=tmps, func=AF.Exp), reads=[tmpsb], writes=[tmpsb])
                S.op("dve", lambda e, dtk=dtk: e.tensor_tensor(out=dsd, in0=tmps, in1=dtk, op=ALU.mult),
                     reads=[tmpsb, dtvb], writes=[dsdb])
                S.op("act", lambda e: e.activation(out=dab, in_=tot, func=AF.Exp), reads=[totb], writes=[dabb])
                if not pre:
                    for gr in range(4):
                        mm(6, PS[6][:, gr * 128:(gr + 1) * 128], xc[:, (16 + gr) * G + c0:(16 + gr) * G + c0 + 128],
                           xc[:, (20 + gr) * G + c0:(20 + gr) * G + c0 + 128], True, True, [xcb, btkb])
                    S.op("dve", lambda e: e.tensor_tensor(out=cbm, in0=PS[6][:, :], in1=trib4, op=ALU.mult),
                         reads=[psb[6], trib4b], writes=[cbmb])
                    S.op("dve", lambda e, dtk=dtk: e.tensor_tensor(out=xdt.rearrange("p (h d) -> p h d", d=64),
                                                                   in0=xst.rearrange("p (h d) -> p h d", d=64),
                                                                   in1=dtk.unsqueeze(2).to_broadcast([128, 32, 64]), op=ALU.mult),
                         reads=[xstb, dtvb], writes=[xdtb])
                    def front(q, adk=adk, c0=c0):
                        gr = q // 2
                        r4, r4b = rhs4[q]
                        abk = 7 if q % 2 == 0 else 6
                        mm(abk, PS[abk][:, :], onesf, r4, True, True, [onesfb, r4b])
                        ab_ap, ab_b = acsbc[q % 2]
                        S.op("act", lambda e, o=ab_ap, abk=abk: e.activation(out=o, in_=PS[abk][:, :], func=AF.Copy),
                             reads=[psb[abk]], writes=[ab_b])
                        d_ap, d_b = dif[q % 2]
                        for hh in range(4):
                            h = 4 * q + hh
                            S.op("dve", lambda e, o=d_ap[:, hh * 128:(hh + 1) * 128], i_=ab_ap[:, hh * 128:(hh + 1) * 128], h=h:
                                 e.tensor_scalar(out=o, in0=i_, scalar1=acs[:, h:h + 1], scalar2=0.0, op0=ALU.subtract, op1=ALU.min),
                                 reads=[ab_b, acsb], writes=[d_b])
                        l_ap, l_b = LT[q % 2]
                        e_ap, e_b = ea[q % 2]
                        S.op("act", lambda e, o=l_ap, i_=d_ap: e.activation(out=o, in_=i_, func=AF.Exp), reads=[d_b], writes=[l_b])
                        S.op("act", lambda e, o=e_ap, i_=ab_ap: e.activation(out=o, in_=i_, func=AF.Exp), reads=[ab_b], writes=[e_b])
                        m_ap, m_b = MT[q % 2]
                        c_ap, c_b = Ce[q % 2]
                        S.op("dve", lambda e, o=m_ap, i_=l_ap, cb_=cbm[:, gr * 128:(gr + 1) * 128]: e.tensor_tensor(
                            out=o.rearrange("p (a b) -> p a b", a=4), in0=i_.rearrange("p (a b) -> p a b", a=4),
                            in1=cb_.unsqueeze(1).to_broadcast([128, 4, 128]), op=ALU.mult),
                            reads=[l_b, cbmb], writes=[m_b])
                        S.op("dve", lambda e, o=c_ap, i_=e_ap, cc=xc[:, (20 + gr) * G + c0:(20 + gr) * G + c0 + 128]: e.tensor_tensor(
                            out=o.rearrange("p (a b) -> p a b", a=4), in0=i_.rearrange("p (a b) -> p a b", a=4),
                            in1=cc.unsqueeze(1).to_broadcast([128, 4, 128]),
                            op=ALU.mult), reads=[e_b, xcb], writes=[c_b])
                        return m_ap, m_b, c_ap, c_b

                    def back(q, m_ap, m_b, c_ap, c_b):
                        for hh in range(4):
                            h = 4 * q + hh
                            bk = h // 8
                            o = PS[bk][:, (h % 8) * 64:(h % 8 + 1) * 64]
                            mm(bk, o, m_ap[:, hh * 128:(hh + 1) * 128], xdt[:, h * 64:(h + 1) * 64], True, False, [m_b, xdtb])
                            mm(bk, o, c_ap[:, hh * 128:(hh + 1) * 128], Sbf[:, h * 64:(h + 1) * 64], False, False, [c_b, Sbfb[bk]])
                            mm(bk, o, Ddiag[:, h * 128:(h + 1) * 128], xst[:, h * 64:(h + 1) * 64], False, True, [c16buf, xstb])

                def st_part1():
                    S.op("dve", lambda e: e.tensor_tensor(out=xd.rearrange("p (h d) -> p h d", d=64),
                                                          in0=xst.rearrange("p (h d) -> p h d", d=64),
                                                          in1=dsd.unsqueeze(2).to_broadcast([128, 32, 64]), op=ALU.mult),
                         reads=[xstb, dsdb], writes=[xdb])
                    for gr in range(4):
                        S.op("dve", lambda e, gr=gr: e.tensor_tensor(
                            out=St[:, gr * 512:(gr + 1) * 512].rearrange("p (h d) -> p h d", d=64),
                            in0=St[:, gr * 512:(gr + 1) * 512].rearrange("p (h d) -> p h d", d=64),
                            in1=dab[:, gr * 8:(gr + 1) * 8].unsqueeze(2).to_broadcast([128, 8, 64]), op=ALU.mult),
                            reads=[Stb[gr], dabb], writes=[Stb[gr]])

                def st_part2(grs):
                    for gr in grs:
                        sbk = 7 if gr % 2 == 0 else 6
                        mm(sbk, PS[sbk][:, :], btk[:, gr * 128:(gr + 1) * 128], xd[:, gr * 512:(gr + 1) * 512], True, True,
                           [btkb, xdb])
                        S.op("dve", lambda e, gr=gr, sbk=sbk: e.tensor_tensor(out=St[:, gr * 512:(gr + 1) * 512],
                                                                              in0=St[:, gr * 512:(gr + 1) * 512],
                                                                              in1=PS[sbk][:, :], op=ALU.add),
                             reads=[Stb[gr], psb[sbk]], writes=[Stb[gr]])

                def sbf_refresh():
                    for gr in range(4):
                        S.op("act", lambda e, gr=gr: e.activation(out=Sbf[:, gr * 512:(gr + 1) * 512],
                                                                  in_=St[:, gr * 512:(gr + 1) * 512], func=AF.Copy),
                             reads=[Stb[gr]], writes=[Sbfb[gr]])

                if pre:
                    st_part1()
                    st_part2((0, 1, 2, 3))
                    sbf_refresh()
                else:
                    steps = pending[0] or []
                    pending[0] = None
                    extra = {4: [st_part1], 5: [lambda: st_part2((0, 1))], 6: [lambda: st_part2((2, 3))]}
                    for i_, st_ in enumerate(steps):
                        extra.setdefault(i_, []).append(st_)
                    fr = front(0)
                    for q in range(8):
                        cur = fr
                        if q + 1 < 8:
                            fr = front(q + 1)
                        back(q, *cur)
                        for fn_ in extra.get(q, []):
                            fn_()
                    sbf_refresh()
                    for gr in range(4):
                        S.op("dve", lambda e, gr=gr: e.tensor_tensor(out=yz[:, gr * 512:(gr + 1) * 512], in0=PS[gr][:, :],
                                                                     in1=sz[:, gr * 512:(gr + 1) * 512], op=ALU.mult),
                             reads=[psb[gr], szb], writes=[yzb])

                    def t1():
                        S.op("dve", lambda e: e.memset(ss[:, 0:4], 0.0), writes=[ssb])
                        for gr in range(4):
                            S.op("act", lambda e, gr=gr: e.activation(out=junk, in_=yz[:, gr * 512:(gr + 1) * 512], func=AF.Square,
                                                                      accum_out=ss[:, gr:gr + 1]), reads=[yzb], writes=[junkb, ssb])
                        S.op("act", lambda e: e.activation(out=ss[:, 4:8], in_=ss[:, 0:4], func=AF.Ln, scale=1.0 / 512.0,
                                                           bias=k_eps5), reads=[ssb, kbuf], writes=[ssb])
                        S.op("act", lambda e: e.activation(out=ss[:, 4:8], in_=ss[:, 4:8], func=AF.Exp, scale=-0.5),
                             reads=[ssb], writes=[ssb])

                    def t2():
                        for gr in range(4):
                            S.op("dve", lambda e, gr=gr: e.tensor_scalar(out=yzn[:, gr * 512:(gr + 1) * 512],
                                                                         in0=yz[:, gr * 512:(gr + 1) * 512],
                                                                         scalar1=ss[:, 4 + gr:5 + gr], scalar2=None, op0=ALU.mult),
                                 reads=[yzb, ssb], writes=[yznb])

                    def t3():
                        for f in range(16):
                            S.op("pe", lambda e, o=psT[f // 8][:, (f % 8) * 128:(f % 8 + 1) * 128], i_=yzn[:, f * 128:(f + 1) * 128]:
                                 e.transpose(o, i_, identb), reads=[yznb, c16buf], writes=[psb[4 + f // 8]])

                    def t4(c0=c0, yn_ap=yn_ap, yn_b=yn_b):
                        for f in range(16):
                            gcol = pv[:, PV["ssd_g"] + f:PV["ssd_g"] + f + 1]
                            src = psT[f // 8][:, (f % 8) * 128:(f % 8 + 1) * 128]
                            dst = yn_ap[:, f * G + c0:f * G + c0 + 128]
                            if f // 8 == 0:
                                S.op("act", lambda e, o=dst, i_=src, gc=gcol: e.activation(out=o, in_=i_, func=AF.Copy, scale=gc),
                                     reads=[psb[4], cbuf], writes=[yn_b])
                            else:
                                S.op("dve", lambda e, o=dst, i_=src, gc=gcol: e.tensor_scalar(out=o, in0=i_, scalar1=gc,
                                                                                             scalar2=None, op0=ALU.mult),
                                     reads=[psb[5], cbuf], writes=[yn_b])
                    pending[0] = [t1, t2, t3, t4]
            if pending[0] is not None:
                for st_ in pending[0]:
                    st_()
                pending[0] = None
            for f in range(16):
                if pre:
                    break
                S.dma("sp", YN_d[f * 128:(f + 1) * 128, t0:t0 + G], yn_ap[:, f * G:(f + 1) * G], yn_b, reads=[yn_b],
                      writes=[YN_b[g]])
        end_phase(a32, a16)


    def phase_e():
        NSLOT = 5
        a32, a16 = new_phase()
        xT, xTb = a32.take_chunks("xT", 8, G)
        f1, f1b = a32.take_chunks("f1", 8, G)
        rstd, rstdb = a32.take("rstd", G)
        tmp32 = [a32.take("tmp%d" % i, G) for i in range(2)]
        memx, memxb = f1[:, 0:8 * MEM], tuple(f1b)
        ws = WStream(a32, a16, 3, NSLOT, 4096)
        xn, xnb = a16.take_chunks("xn", 8, G)
        hid, hidb = a16.take("hid", NFF * G)
        sq_pair = [a16.take("sq%d" % i, G) for i in range(2)]
        aux, auxb = a16.take("aux", 8 * G)
        qx, qxb = a16.take("qx", 8 * G)
        gts = [a16.take("gts%d" % i, 2 * G) for i in range(2)]
        pp = [a16.take("ppx%d" % i, G) for i in range(2)]
        kxT, kxTb = a16.take("kxT", 8 * MEM)
        vx, vxb = a16.take("vx", 2 * D)
        memn, memnb = a16.take("memn", 8 * MEM)
        ons = xn
        onsb = xnb
        yns, ynsb = hid, hidb

        for c in range(8):
            S.dma("sp", memx[:, c * MEM:(c + 1) * MEM], memT_d[c * 128:(c + 1) * 128, :], f1b[0], writes=[memxb])
        mc = lambda c: memx[:, c * MEM:(c + 1) * MEM]
        rms_stats(mc, memxb, 8, MEM, sq_pair, ones_d, 7, rstd[:, 0:MEM], rstdb, 1e-6)
        norm_cast(mc, memxb, PV["mem_g"], rstd[:, 0:MEM], rstdb, lambda c: memn[:, c * MEM:(c + 1) * MEM], memnb)
        for dt_ in range(8):
            w, wb = ws.load(wxk_d[dt_], 1024)
            p = dt_ % 2
            for c in range(8):
                mm(p, PS[p][:, 0:MEM], w[:, c * 128:(c + 1) * 128], memn[:, c * MEM:(c + 1) * MEM], c == 0, c == 7, [wb, memnb])
            evac_copy(kxT[:, dt_ * MEM:(dt_ + 1) * MEM], PS[p][:, 0:MEM], [psb[p]], [kxTb])
        for blk in range(2):
            w, wb = ws.load(wxv_d[blk], 4096)
            for kt in range(2):
                p = 2 + kt
                for c in range(8):
                    mm(p, PS[p][:, :], memn[:, c * MEM + kt * 128:c * MEM + (kt + 1) * 128], w[:, c * 512:(c + 1) * 512],
                       c == 0, c == 7, [wb, memnb])
                evac_copy(vx[:, kt * D + blk * 512:kt * D + (blk + 1) * 512], PS[p][:, :], [psb[p]], [vxb])

        def proj_norm_resid(w_d, src, srcb, gpost, nck=8):
            for d in range(8):
                w, wb = ws.load(w_d[d], nck * 128)
                p = 4 + d % 2
                for c in range(nck):
                    mm(p, PS[p][:, :], w[:, c * 128:(c + 1) * 128], src[:, c * G:(c + 1) * G], c == 0, c == nck - 1, [wb, srcb])
                sq_ap, sq_b = sq_pair[d % 2]
                S.op("dve", lambda e, o=f1[:, d * G:(d + 1) * G], i_=PS[p][:, :]: e.tensor_copy(out=o, in_=i_),
                     reads=[psb[p]], writes=[f1b[d]])
                S.op("act", lambda e, o=sq_ap, i_=f1[:, d * G:(d + 1) * G]: e.activation(out=o, in_=i_, func=AF.Square),
                     reads=[f1b[d]], writes=[sq_b])
                mm(6, PS[6][:, :], ones_d, sq_ap, d == 0, d == 7, [sq_b, c16buf])
            rsqrt_ps(6, G, rstd, rstdb, 1e-6, 1.0)
            resid_add(xT, xTb, f1, f1b, rstd, rstdb, gpost)

        gi = [0]
        for g in range(NG_):
            t0 = g * G
            for c in range(8):
                S.dma("sp", ons[:, c * G:(c + 1) * G], ON_d[c * 128:(c + 1) * 128, t0:t0 + G], onsb[c], reads=[ON_b[g]], writes=[onsb[c]])
            for c in range(16):
                S.dma("sp", yns[:, c * G:(c + 1) * G], YN_d[c * 128:(c + 1) * 128, t0:t0 + G], ynsb, reads=[YN_b[g]], writes=[ynsb])
            for c in range(8):
                S.dma("sp", xT[:, c * G:(c + 1) * G], XS_d[c * 128:(c + 1) * 128, t0:t0 + G], xTb[c], reads=[XS_b[g]], writes=[xTb[c]])
            for d in range(8):
                wa, wab = ws.load(wa_r[d], 1024)
                wsd, wsb = ws.load(ws_r[d], 2048)
                pa, ps_ = (0, 1) if d % 2 == 0 else (2, 3)
                for c in range(8):
                    mm(pa, PS[pa][:, :], wa[:, c * 128:(c + 1) * 128], ons[:, c * G:(c + 1) * G], c == 0, c == 7, [wab, onsb[c]])
                for c in range(16):
                    mm(ps_, PS[ps_][:, :], wsd[:, c * 128:(c + 1) * 128], yns[:, c * G:(c + 1) * G], c == 0, c == 15, [wsb, ynsb])
                gt_ap, gt_b = gts[gi[0] % 2]
                gi[0] += 1
                S.dma("sp", gt_ap[:, 0:G], GT_d[d * 128:(d + 1) * 128, t0:t0 + G], gt_b, reads=[GT_b[g]], writes=[gt_b])
                S.dma("sp", gt_ap[:, G:2 * G], GT_d[D + d * 128:D + (d + 1) * 128, t0:t0 + G], gt_b, reads=[GT_b[g]], writes=[gt_b])
                ta, tab = tmp32[0]
                tb, tbb = tmp32[1]
                S.op("dve", lambda e, pa=pa, g_=gt_ap[:, 0:G]: e.tensor_tensor(out=ta, in0=PS[pa][:, :], in1=g_, op=ALU.mult),
                     reads=[psb[pa], gt_b], writes=[tab])
                S.op("dve", lambda e, ps_=ps_, g_=gt_ap[:, G:2 * G]: e.tensor_tensor(out=tb, in0=PS[ps_][:, :], in1=g_, op=ALU.mult),
                     reads=[psb[ps_], gt_b], writes=[tbb])
                S.op("dve", lambda e, o=aux[:, d * G:(d + 1) * G]: e.tensor_tensor(out=o, in0=ta, in1=tb, op=ALU.add),
                     reads=[tab, tbb], writes=[auxb])
            proj_norm_resid(wmix_r, aux, auxb, PV["mix_post"])
            xc_ = lambda c: xT[:, c * G:(c + 1) * G]
            rms_stats(xc_, xTb, 8, G, sq_pair, ones_d, 7, rstd, rstdb, 1e-6)
            norm_cast(xc_, xTb, PV["xa_pre"], rstd, rstdb, lambda c: xn[:, c * G:(c + 1) * G], xnb)
            for d in range(8):
                w, wb = ws.load(wxq_r[d], 1024)
                p = d % 2
                for c in range(8):
                    mm(p, PS[p][:, :], w[:, c * 128:(c + 1) * 128], xn[:, c * G:(c + 1) * G], c == 0, c == 7, [wb, xnb[c]])
                evac_copy(qx[:, d * G:(d + 1) * G], PS[p][:, :], [psb[p]], [qxb])
            for hd in range(4):
                for kt in range(2):
                    p = 4 + kt
                    for dd in range(2):
                        dt_ = hd * 2 + dd
                        mm(p, PS[p][:, :], kxT[:, dt_ * MEM + kt * 128:dt_ * MEM + (kt + 1) * 128], qx[:, dt_ * G:(dt_ + 1) * G],
                           dd == 0, dd == 1, [kxTb, qxb])
                    p_ap, p_b = pp[kt]
                    S.op("act", lambda e, o=p_ap, p=p: e.activation(out=o, in_=PS[p][:, :], func=AF.Exp, scale=1.0 / 16.0),
                         reads=[psb[p]], writes=[p_b])
                for kt in range(2):
                    p_ap, p_b = pp[kt]
                    for e_ in range(2):
                        mm(e_, PS[e_][:, :], vx[:, kt * D + hd * 256 + e_ * 128:kt * D + hd * 256 + (e_ + 1) * 128], p_ap,
                           kt == 0, kt == 1, [vxb, p_b])
                    mm(2, PS[2][:, :], ones_1, p_ap, kt == 0, kt == 1, [c16buf, p_b])
                ta, tab = tmp32[0]
                S.op("dve", lambda e: e.reciprocal(out=ta, in_=PS[2][:, :]), reads=[psb[2]], writes=[tab])
                for e_ in range(2):
                    S.op("dve", lambda e, e_=e_, o=aux[:, (hd * 2 + e_) * G:(hd * 2 + e_ + 1) * G]:
                         e.tensor_tensor(out=o, in0=PS[e_][:, :], in1=ta, op=ALU.mult), reads=[psb[e_], tab], writes=[auxb])
            proj_norm_resid(wxo_r, aux, auxb, PV["xa_post"])
            ffn(a32, a16, ws, xT, xTb, xn, xnb, hid, hidb, f1, f1b, sq_pair, rstd, rstdb, tmp32, wgu_r[1], wdn_r[1],
                PV["ffn2_pre"], PV["ffn2_post"])
            for c in range(8):
                S.dma("sp", out_d[c * 128:(c + 1) * 128, t0:t0 + G], xT[:, c * G:(c + 1) * G], xTb[c], reads=[xTb[c]], writes=[])
        end_phase(a32, a16)

    if "ab" in stages:
        phase_ab()
    if "d" in stages:
        phase_d()
    if "c" in stages:
        phase_c()
    if "e" in stages:
        phase_e()

    finals = []
    for lst in (XS_b, Q_b, K_b, V_b, Z_b, XBC_b, DT_b, GT_b, ON_b, YN_b):
        pass
    allb = []
    seen = set()
    for e in S.ENGS:
        for o in S.ops[e]:
            if o.dinc is not None and id(o.dinc) not in seen:
                seen.add(id(o.dinc))
                allb.append(o.dinc)
    S.emit(allb)
    return nc


def _tile_w(W):
    K, N = W.shape
    return np.ascontiguousarray(W.reshape(K // 128, 128, N // 128, 128).transpose(2, 1, 0, 3).reshape(N // 128, 128, K))


def _blk_w(W, nb=512):
    K, N = W.shape
    return np.ascontiguousarray(
        W.reshape(K // 128, 128, N // nb, nb).transpose(2, 1, 0, 3).reshape(N // nb, 128, (K // 128) * nb))


def _col(v):
    return np.ascontiguousarray(np.asarray(v).reshape(-1, 128).T)


def _consts():
    c = np.zeros((128, NCONST), np.float32)
    c[:, 0:128] = np.eye(128, dtype=np.float32)
    s = np.arange(128)
    c[:, 128:256] = (s[:, None] <= s[None, :]).astype(np.float32)
    q = np.arange(512)
    for m in range(4):
        c[:, 256 + m * 512:256 + (m + 1) * 512] = (q[None, :] >= 128 * m + s[:, None]).astype(np.float32)
    return c


def prep_shared(inp):
    f = lambda k: np.asarray(inp[k], np.float32)[0]
    sh = {}
    for i, n in ((1, "ffn1"), (2, "ffn2")):
        wgu = f(n + "_w_gu")
        sh["wgu%d" % i] = np.ascontiguousarray(
            wgu.reshape(8, 128, 2, NFF, 128).transpose(3, 1, 2, 0, 4).reshape(NFF, 128, 2048))
        sh["wdn%d" % i] = _tile_w(f(n + "_w_down"))
    win = f("w_in")
    q, k, v, z, xbc, dt, gt = np.split(win, np.cumsum([1024, 1024, 1024, 2048, 3072, 32])[:6], axis=1)
    sh["winf"] = np.concatenate([_tile_w(q), _tile_w(k), _tile_w(xbc), _tile_w(gt)], axis=0)
    sh["wint"] = np.concatenate([_blk_w(v), _blk_w(z)], axis=0)
    sh["wdt"] = np.ascontiguousarray(dt.reshape(8, 128, 32).transpose(1, 0, 2).reshape(128, 256))
    sh["wa"] = _tile_w(f("w_branch_attn"))
    sh["ws"] = _tile_w(f("w_branch_ssd"))
    sh["wmix"] = _tile_w(f("w_mix_out"))
    sh["wxq"] = _tile_w(f("xa_w_q"))
    kv = f("xa_w_kv")
    sh["wxk"] = _tile_w(kv[:, :1024])
    sh["wxv"] = _blk_w(kv[:, 1024:])
    sh["wxo"] = _tile_w(f("xa_w_o"))
    cw = f("ssd_conv_w")
    convw = np.ascontiguousarray(cw.reshape(4, 24, 128).transpose(2, 1, 0).reshape(128, 96))
    pvec = np.concatenate([_col(f("ffn1_pre_g")), _col(f("ffn1_post_g")), _col(f("mix_pre_g")), _col(f("mix_post_g")),
                           _col(f("xa_pre_g")), _col(f("xa_post_g")), _col(f("mem_norm_g")), _col(f("ffn2_pre_g")),
                           _col(f("ffn2_post_g")), _col(f("b_gate")), _col(f("da_subln_g")), convw,
                           _col(f("ssd_conv_b")), _col(f("ssd_norm_g"))], axis=1).astype(np.float32)
    assert pvec.shape == (128, NPV), pvec.shape
    sh["pvec"] = np.ascontiguousarray(pvec)
    row = np.concatenate([f("ssd_dt_bias"), f("ssd_A_log"), f("ssd_D"), f("da_lambda_q1"), f("da_lambda_k1"),
                          f("da_lambda_q2"), f("da_lambda_k2")])[None, :].astype(np.float32)
    assert row.shape == (1, NROW)
    sh["row"] = np.ascontiguousarray(row)
    sh["consts"] = _consts()
    return sh


def core_inputs(x, mem, b, r, T, TP):
    m = {}
    m["xT"] = np.ascontiguousarray(x[b, r * T:(r + 1) * T].T)
    m["xTp"] = np.ascontiguousarray(x[b, 0:TP].T)
    fl = np.zeros((128, 2), np.float32)
    fl[:, 0] = 1.0 if r else 0.0
    fl[:, 1] = 0.0 if r else NEG
    m["flags"] = fl
    m["memT"] = np.ascontiguousarray(mem[b].T)
    return m


def kernel(**inputs):
    x = np.asarray(inputs["x"], np.float32)
    mem = np.asarray(inputs["mem"], np.float32)
    B, SEQ_, _ = x.shape
    T = SEQ_ // 2
    sh = prep_shared(inputs)
    nc = build_program(T, T)
    in_maps = []
    for c in range(8):
        m = dict(sh)
        m.update(core_inputs(x, mem, c // 2, c % 2, T, T))
        in_maps.append(m)
    res = run_bass_kernel_spmd(nc, in_maps, core_ids=list(range(8)))
    out = np.empty((B, SEQ_, D), np.float32)
    for c in range(8):
        out[c // 2, (c % 2) * T:(c % 2 + 1) * T] = res.results[c]["outT"].T
    return out
```

```python
import numpy as np
import ml_dtypes
import concourse.bass as bass
import concourse.mybir as mybir
from concourse.bass_utils import run_bass_kernel_spmd

F32 = mybir.dt.float32
BF16 = mybir.dt.bfloat16
AF = mybir.ActivationFunctionType
ALU = mybir.AluOpType

D = 1024
DFF = 2816
NFF = DFF // 128
SEQ = 4096
G = 512
NH = 8
SH = 32
SG = 4
MEM = 256
NEG = -30000.0

PV = {}
_o = 0
for _n, _w in [("ffn1_pre", 8), ("ffn1_post", 8), ("mix_pre", 8), ("mix_post", 8), ("xa_pre", 8),
               ("xa_post", 8), ("mem_g", 8), ("ffn2_pre", 8), ("ffn2_post", 8), ("b_gate", 16),
               ("subln", 1), ("conv_w", 96), ("conv_b", 24), ("ssd_g", 16)]:
    PV[_n] = _o
    _o += _w
NPV = _o
NROW = 96 + 256
NCONST = 256 + 2048


class Buf:
    __slots__ = ("name", "w", "r", "dsem", "dcnt")

    def __init__(self, name, prior=None):
        self.name = name
        self.w = None
        self.r = list(prior) if prior else []
        self.dsem = None
        self.dcnt = 0

    def tokens(self):
        t = list(self.r)
        if self.w is not None:
            t.append(self.w)
        return t


class BL(list):
    pass


class Op:
    __slots__ = ("eng", "fn", "deps", "dwaits", "need", "ticket", "dinc")

    def __init__(self, eng, fn):
        self.eng = eng
        self.fn = fn
        self.deps = []
        self.dwaits = []
        self.need = False
        self.ticket = None
        self.dinc = None


class Sched:
    ENGS = ("pe", "act", "dve", "pool", "sp")

    def __init__(self, nc):
        self.nc = nc
        self.ops = {e: [] for e in self.ENGS}
        self.nsem = 0

    def _dep(self, op, tok, kind):
        if isinstance(tok, Op):
            if tok.eng == op.eng:
                if op.eng == "pe" or kind != "raw":
                    return
            tok.need = True
            op.deps.append(tok)
        else:
            sb = tok[1]
            op.dwaits.append((sb, sb.dcnt))

    @staticmethod
    def _flat(bs):
        out = []
        for b in bs:
            if isinstance(b, (list, tuple)):
                out.extend(Sched._flat(b))
            else:
                out.append(b)
        return out

    def _track(self, op, tok, reads, writes):
        reads = self._flat(reads)
        writes = self._flat(writes)
        for b in reads:
            if b.w is not None:
                self._dep(op, b.w, "raw")
        for b in writes:
            if b.w is not None:
                self._dep(op, b.w, "waw")
            for t in b.r:
                if t is not op:
                    self._dep(op, t, "war")
        key = tok.eng if isinstance(tok, Op) else id(tok[1])
        for b in reads:
            b.r = [t for t in b.r if (t.eng if isinstance(t, Op) else id(t[1])) != key]
            b.r.append(tok)
        for b in writes:
            b.w = tok
            b.r = []

    def op(self, eng, fn, reads=(), writes=()):
        o = Op(eng, fn)
        for b in self._flat(reads):
            if b.name.startswith("psb"):
                for t in b.r:
                    assert not isinstance(t, Op) or t.eng == eng, ("two engines read one PSUM bank", b.name, eng, t.eng)
        self._track(o, o, reads, writes)
        self.ops[eng].append(o)
        return o

    def dma(self, q, out, in_, sb, reads=(), writes=()):
        o = Op(q, lambda e: e.dma_start(out=out, in_=in_))
        if sb.dsem is None:
            sb.dsem = self.nc.alloc_semaphore(name="d%d_%s" % (self.nsem, sb.name))
            self.nsem += 1
        tok = ("d", sb)
        self._track(o, tok, reads, writes)
        sb.dcnt += 16
        o.dinc = sb
        self.ops[q].append(o)
        return o

    def emit(self, final_waits):
        nc = self.nc
        sems = {e: nc.alloc_semaphore(name="eng_" + e) for e in self.ENGS}
        for e in self.ENGS:
            t = 0
            for o in self.ops[e]:
                if o.need:
                    t += 1
                    o.ticket = t
        handles = {"pe": "tensor", "act": "scalar", "dve": "vector", "pool": "gpsimd", "sp": "sync"}
        with nc.Block() as block:
            for e in self.ENGS:
                ops = self.ops[e]
                fw = final_waits if e == "sp" else []

                def body(eng, ops=ops, e=e, fw=fw):
                    waited = {}
                    for o in ops:
                        need = {}
                        for d in o.deps:
                            k = ("e", d.eng)
                            if d.ticket > need.get(k, (None, 0))[1]:
                                need[k] = (sems[d.eng], d.ticket)
                        for sb, v in o.dwaits:
                            k = ("d", id(sb))
                            if v > need.get(k, (None, 0))[1]:
                                need[k] = (sb.dsem, v)
                        for k, (s, v) in need.items():
                            if waited.get(k, 0) < v:
                                eng.wait_ge(s, v)
                                waited[k] = v
                        ins = o.fn(eng)
                        if o.dinc is not None:
                            ins.then_inc(o.dinc.dsem, 16)
                        elif o.ticket is not None:
                            ins.then_inc(sems[e], 1)
                    for sb in fw:
                        eng.wait_ge(sb.dsem, sb.dcnt)

                getattr(block, handles[e])(body)


def build_program(T, TP=0, stages=("ab", "c", "d", "e"), debug=False):
    NG_ = T // G
    NPG = TP // G
    KVT = TP + T
    NKG = KVT // G
    NCH = KVT // 128
    nc = bass.Bass("TRN2", target_bir_lowering=False)
    S = Sched(nc)
    okind = "ExternalOutput" if debug else "Internal"

    def din(name, shape, dt=F32):
        return nc.dram_tensor(name, list(shape), dt, kind="ExternalInput").ap()

    def dscr(name, shape, dt):
        return nc.dram_tensor(name, list(shape), dt, kind=okind).ap()

    xT_d = din("xT", [D, T])
    xTp_d = din("xTp", [D, max(TP, G)])
    flags_d = din("flags", [128, 2])
    memT_d = din("memT", [D, MEM])
    pvec_d = din("pvec", [128, NPV])
    row_d = din("row", [1, NROW])
    const_d = din("consts", [128, NCONST])
    wgu_d = [din("wgu%d" % i, [NFF, 128, 2048]) for i in (1, 2)]
    wdn_d = [din("wdn%d" % i, [8, 128, DFF]) for i in (1, 2)]
    winf_d = din("winf", [56, 128, 1024])
    wint_d = din("wint", [6, 128, 4096])
    wdt_d = din("wdt", [128, 256])
    wa_d = din("wa", [8, 128, 1024])
    ws_d = din("ws", [8, 128, 2048])
    wmix_d = din("wmix", [8, 128, 1024])
    wxq_d = din("wxq", [8, 128, 1024])
    wxk_d = din("wxk", [8, 128, 1024])
    wxv_d = din("wxv", [2, 128, 4096])
    wxo_d = din("wxo", [8, 128, 1024])
    out_d = nc.dram_tensor("outT", [D, T], F32, kind="ExternalOutput").ap()

    XS_d = dscr("XS", [D, T], F32)
    _W = {}
    QT_d = dscr("QT", [D, T], BF16)
    KT_d = dscr("KT", [D, KVT], BF16)
    V_d = dscr("V", [KVT, D], BF16)
    Z_d = dscr("Z", [T, 2048], BF16)
    XBC_d = dscr("XBC", [3072, KVT], BF16)
    DT_d = dscr("DT", [KVT, 32], F32)
    GT_d = dscr("GT", [2048, T], BF16)
    ON_d = dscr("ON", [D, T], BF16)
    YN_d = dscr("YN", [2048, T], BF16)

    def dbufs(name, n):
        return [Buf("%s%d" % (name, i)) for i in range(n)]
    XS_b, Q_b, Z_b, GT_b, ON_b, YN_b = [dbufs(n, NG_) for n in ("XS", "Q", "Z", "GT", "ON", "YN")]
    K_b, V_b, XBC_b, DT_b = [dbufs(n, NKG) for n in ("K", "V", "XBC", "DT")]

    A32 = nc.alloc_sbuf_tensor("A32", [128, 12288], F32)
    A16 = nc.alloc_sbuf_tensor("A16", [128, 57344], BF16)
    CST = nc.alloc_sbuf_tensor("CST", [128, NPV + NROW + NCONST + 64], F32)
    C16 = nc.alloc_sbuf_tensor("C16", [128, 128 * 4 + 2048 + 32 * 128], BF16)
    PS = [nc.alloc_psum_tensor("ps%d" % i, [128, 512], F32) for i in range(8)]

    class Arena:
        def __init__(self, t, prior):
            self.t = t
            self.off = 0
            self.prior = prior
            self.bufs = []

        def take(self, name, n):
            ap = self.t[:, self.off:self.off + n]
            self.off += n
            assert self.off <= self.t.shape[1], (name, self.off)
            b = Buf(name, self.prior)
            self.bufs.append(b)
            return ap, b

        def take_chunks(self, name, nch, width):
            ap, b0 = self.take(name, nch * width)
            bl = BL([b0] + [Buf("%s_%d" % (name, i), self.prior) for i in range(1, nch)])
            self.bufs.extend(bl[1:])
            return ap, bl

    state = {"prior32": [], "prior16": [], "priorps": []}
    psb = [Buf("psb%d" % i) for i in range(8)]

    def new_phase():
        toks = []
        for b in state.get("bufs", []):
            toks.extend(b.tokens())
        seen = set()
        pr = []
        for t in toks:
            k = id(t) if isinstance(t, Op) else ("d", id(t[1]))
            if k not in seen:
                seen.add(k)
                pr.append(t)
        a32 = Arena(A32, pr)
        a16 = Arena(A16, pr)
        state["a32"], state["a16"] = a32, a16
        return a32, a16

    def end_phase(a32, a16):
        state["bufs"] = a32.bufs + a16.bufs

    pv = CST[:, 0:NPV]
    rowb = CST[:, NPV:NPV + NROW]
    cst = CST[:, NPV + NROW:NPV + NROW + NCONST]
    lamc = CST[:, NPV + NROW + NCONST:NPV + NROW + NCONST + 64]
    cbuf = Buf("consts")
    c16buf = Buf("c16")
    ident32 = cst[:, 0:128]
    tri32 = cst[:, 128:256]
    identb = C16[:, 0:128]
    ones_d = C16[:, 128:256]
    ones_1 = C16[:, 256:384]
    ones_h = C16[:, 384:512]
    maskb = C16[:, 512:512 + 2048]
    Ddiag = C16[:, 2560:2560 + 4096]

    S.dma("sp", CST[:, 0:NPV], pvec_d[:, :], cbuf, writes=[cbuf])
    S.dma("sp", rowb, row_d[0:1, :].partition_broadcast(128), cbuf, writes=[cbuf])
    S.dma("sp", cst, const_d[:, :], cbuf, writes=[cbuf])
    S.dma("sp", lamc[:, 16:18], flags_d[:, :], cbuf, writes=[cbuf])
    fl_valid = lamc[:, 16:17]
    fl_bias = lamc[:, 17:18]
    S.op("dve", lambda e: e.tensor_copy(out=identb, in_=ident32), reads=[cbuf], writes=[c16buf])
    S.op("dve", lambda e: e.memset(ones_d, 1.0 / 1024.0), writes=[c16buf])
    S.op("dve", lambda e: e.memset(ones_1, 1.0), writes=[c16buf])
    S.op("dve", lambda e: e.memset(ones_h, 1.0 / 128.0), writes=[c16buf])
    S.op("dve", lambda e: e.tensor_copy(out=maskb, in_=cst[:, 256:256 + 2048]), reads=[cbuf], writes=[c16buf])
    for h in range(SH):
        S.op("dve", lambda e, h=h: e.tensor_scalar(out=Ddiag[:, h * 128:(h + 1) * 128], in0=ident32,
                                                     scalar1=rowb[:, 64 + h:65 + h], scalar2=None, op0=ALU.mult),
             reads=[cbuf], writes=[c16buf])
    LAM_INIT = 0.8 - 0.6 * 1.0
    lbuf = Buf("lam")
    tmpl = lamc[:, 0:64]
    s1 = lamc[:, 0:1]

    def lam_ops():
        a32, a16 = new_phase()
        t1, t1b = a32.take("lt1", 64)
        t2, t2b = a32.take("lt2", 64)
        acc, accb = a32.take("lacc", 4)
        S.op("dve", lambda e: e.tensor_tensor(out=t1, in0=rowb[:, 96:160], in1=rowb[:, 160:224], op=ALU.mult),
             reads=[cbuf], writes=[t1b])
        S.op("dve", lambda e: e.reduce_sum(out=acc[:, 0:1], in_=t1, axis=mybir.AxisListType.X),
             reads=[t1b], writes=[accb])
        S.op("dve", lambda e: e.tensor_tensor(out=t2, in0=rowb[:, 224:288], in1=rowb[:, 288:352], op=ALU.mult),
             reads=[cbuf], writes=[t2b])
        S.op("dve", lambda e: e.reduce_sum(out=acc[:, 1:2], in_=t2, axis=mybir.AxisListType.X),
             reads=[t2b, accb], writes=[accb])
        S.op("act", lambda e: e.activation(out=acc[:, 2:4], in_=acc[:, 0:2], func=AF.Exp), reads=[accb], writes=[accb])
        S.op("dve", lambda e: e.tensor_tensor(out=lamc[:, 0:1], in0=acc[:, 2:3], in1=acc[:, 3:4], op=ALU.subtract),
             reads=[accb], writes=[lbuf])
        S.op("dve", lambda e: e.tensor_scalar(out=lamc[:, 0:1], in0=lamc[:, 0:1], scalar1=LAM_INIT, scalar2=None,
                                                op0=ALU.add), reads=[lbuf], writes=[lbuf])
        S.op("dve", lambda e: e.tensor_scalar(out=lamc[:, 1:2], in0=pv[:, PV["subln"]:PV["subln"] + 1],
                                                scalar1=1.0 - LAM_INIT, scalar2=None, op0=ALU.mult),
             reads=[cbuf, lbuf], writes=[lbuf])
        end_phase(a32, a16)
    lam_ops()
    lam_col = lamc[:, 0:1]
    gsub_col = lamc[:, 1:2]
    import math
    kbuf = Buf("kconst")
    k_eps6, k_eps5, k_lnhalf, k_one, k_zero = [lamc[:, 8 + i:9 + i] for i in range(5)]
    for ap_, v_ in ((k_eps6, 1e-6), (k_eps5, 1e-5), (k_lnhalf, math.log(0.5)), (k_one, 1.0), (k_zero, 0.0)):
        S.op("dve", lambda e, a=ap_, v=v_: e.memset(a, v), writes=[kbuf])

    class WRef:
        __slots__ = ("f32", "scr", "buf")

        def __init__(self, f32, scr):
            self.f32, self.scr, self.buf = f32, scr, Buf("wscr")

    def wrefs(name, d_ap):
        if d_ap.ndim == 3:
            scr = nc.dram_tensor(name + "_bf", list(d_ap.shape), BF16, kind="Internal").ap()
            return [WRef(d_ap[i], scr[i]) for i in range(d_ap.shape[0])]
        return [WRef(d_ap, None)]

    class WStream:
        def __init__(self, a32, a16, nstage, nslot, cap, scap=2048):
            self.sl = [a16.take("wbf%d" % i, cap) for i in range(nslot)]
            self.stb = [Buf("wst%d" % i) for i in range(nslot)]
            self.j = 0

        def load(self, wr, n, cast_eng=None):
            sl_ap, sl_b = self.sl[self.j % len(self.sl)]
            st_b = self.stb[self.j % len(self.sl)]
            self.j += 1
            if not isinstance(wr, WRef):
                S.dma("pool", sl_ap[:, 0:n], wr, sl_b, writes=[sl_b])
            elif wr.buf.w is None:
                S.dma("pool", sl_ap[:, 0:n], wr.f32, sl_b, writes=[sl_b])
                if wr.scr is not None:
                    S.dma("sp", wr.scr, sl_ap[:, 0:n], st_b, reads=[sl_b], writes=[wr.buf])
            else:
                S.dma("pool", sl_ap[:, 0:n], wr.scr, sl_b, reads=[wr.buf], writes=[sl_b])
            return sl_ap[:, 0:n], sl_b

    wgu_r = [wrefs("wgu%d" % (i + 1), wgu_d[i]) for i in range(2)]
    wdn_r = [wrefs("wdn%d" % (i + 1), wdn_d[i]) for i in range(2)]
    winf_r = wrefs("winf", winf_d)
    wint_r = wrefs("wint", wint_d)
    wa_r, ws_r, wmix_r, wxq_r, wxo_r = [wrefs(n, d) for n, d in
                                        (("wa", wa_d), ("ws", ws_d), ("wmix", wmix_d), ("wxq", wxq_d), ("wxo", wxo_d))]
    evac_rr = [0]

    def evac_copy(out_ap, in_ap, reads, writes):
        evac_rr[0] += 1
        if evac_rr[0] % 2:
            return S.op("act", lambda e: e.activation(out=out_ap, in_=in_ap, func=AF.Copy), reads=reads, writes=writes)
        return S.op("dve", lambda e: e.tensor_copy(out=out_ap, in_=in_ap), reads=reads, writes=writes)

    def mm(ps_i, out_ap, lhsT, rhs, start, stop, reads):
        return S.op("pe", lambda e: e.matmul(out_ap, lhsT, rhs, start=start, stop=stop), reads=reads, writes=[psb[ps_i]])

    def rsqrt_ps(ps_i, width, out_ap, out_b, eps, mul=1.0):
        eb = {1e-6: k_eps6, 1e-5: k_eps5}[eps]
        mb = {1.0: k_zero, 0.5: k_lnhalf}[mul]
        S.op("act", lambda e: e.activation(out=out_ap, in_=PS[ps_i][:, 0:width], func=AF.Ln, bias=eb),
             reads=[psb[ps_i], kbuf], writes=[out_b])
        S.op("act", lambda e: e.activation(out=out_ap, in_=out_ap, func=AF.Exp, scale=-0.5, bias=mb),
             reads=[out_b, kbuf], writes=[out_b])

    def rms_stats(xsrc, xb, nchunk, width, sq_pair, ones_ap, ps_i, rstd_ap, rstd_b, eps):
        for c in range(nchunk):
            sq_ap, sq_b = sq_pair[c % 2]
            S.op("act", lambda e, o=sq_ap[:, 0:width], i_=xsrc(c): e.activation(out=o, in_=i_, func=AF.Square),
                 reads=[xb[c] if isinstance(xb, BL) else xb], writes=[sq_b])
            mm(ps_i, PS[ps_i][:, 0:width], ones_ap, sq_ap[:, 0:width], c == 0, c == nchunk - 1, [sq_b, c16buf])
        rsqrt_ps(ps_i, width, rstd_ap, rstd_b, eps)

    def norm_cast(xsrc, xb, gcol0, rstd_ap, rstd_b, dst, dst_b, nchunk=8):
        for c in range(nchunk):
            S.op("dve", lambda e, c=c: e.scalar_tensor_tensor(out=dst(c), in0=xsrc(c), scalar=pv[:, gcol0 + c:gcol0 + c + 1],
                                                               in1=rstd_ap, op0=ALU.mult, op1=ALU.mult),
                 reads=[xb[c] if isinstance(xb, BL) else xb, rstd_b, cbuf],
                 writes=[dst_b[c] if isinstance(dst_b, BL) else dst_b])

    def ffn(a32, a16, ws, xT, xTb, xn, xnb, hid, hidb, f1, f1b, sq_pair, rstd, rstdb, tmp32, wgu, wdn, gpre, gpost):
        xc = lambda c: xT[:, c * G:(c + 1) * G]
        rms_stats(xc, xTb, 8, G, sq_pair, ones_d, 7, rstd, rstdb, 1e-6)
        norm_cast(xc, xTb, gpre, rstd, rstdb, lambda c: xn[:, c * G:(c + 1) * G], xnb)
        for i in range(NFF):
            w, wb = ws.load(wgu[i], 2048)
            pg, pu = (0, 1) if i % 2 == 0 else (2, 3)
            for c in range(8):
                mm(pg, PS[pg][:, :], w[:, c * 128:(c + 1) * 128], xn[:, c * G:(c + 1) * G], c == 0, c == 7, [wb, xnb[c]])
            for c in range(8):
                mm(pu, PS[pu][:, :], w[:, 1024 + c * 128:1024 + (c + 1) * 128], xn[:, c * G:(c + 1) * G],
                   c == 0, c == 7, [wb, xnb[c]])
            t_ap, t_b = tmp32[i % 2]
            S.op("act", lambda e, o=t_ap, i_=PS[pg][:, :]: e.activation(out=o, in_=i_, func=AF.Silu),
                 reads=[psb[pg]], writes=[t_b])
            S.op("dve", lambda e, o=hid[:, i * G:(i + 1) * G], a=t_ap, b=PS[pu][:, :]:
                 e.tensor_tensor(out=o, in0=b, in1=a, op=ALU.mult), reads=[t_b, psb[pu]], writes=[hidb])
        for d in range(8):
            w, wb = ws.load(wdn[d], DFF)
            p = 4 + d % 2
            for i in range(NFF):
                mm(p, PS[p][:, :], w[:, i * 128:(i + 1) * 128], hid[:, i * G:(i + 1) * G], i == 0, i == NFF - 1, [wb, hidb])
            sq_ap, sq_b = sq_pair[d % 2]
            S.op("dve", lambda e, o=f1[:, d * G:(d + 1) * G], i_=PS[p][:, :]: e.tensor_copy(out=o, in_=i_),
                 reads=[psb[p]], writes=[f1b[d]])
            S.op("act", lambda e, o=sq_ap, i_=f1[:, d * G:(d + 1) * G]: e.activation(out=o, in_=i_, func=AF.Square),
                 reads=[f1b[d]], writes=[sq_b])
            mm(6, PS[6][:, :], ones_d, sq_ap, d == 0, d == 7, [sq_b, c16buf])
        rsqrt_ps(6, G, rstd, rstdb, 1e-6, 0.5)
        resid_add(xT, xTb, f1, f1b, rstd, rstdb, gpost)

    def resid_add(xT, xTb, f1, f1b, rstd, rstdb, gcol0):
        for d in range(8):
            eng = "dve"
            S.op("dve", lambda e, d=d: e.scalar_tensor_tensor(out=f1[:, d * G:(d + 1) * G], in0=f1[:, d * G:(d + 1) * G],
                                                               scalar=pv[:, gcol0 + d:gcol0 + d + 1], in1=rstd,
                                                               op0=ALU.mult, op1=ALU.mult),
                 reads=[f1b[d], rstdb, cbuf], writes=[f1b[d]])
            S.op(eng, lambda e, d=d: e.tensor_tensor(out=xT[:, d * G:(d + 1) * G], in0=f1[:, d * G:(d + 1) * G],
                                                      in1=xT[:, d * G:(d + 1) * G], op=ALU.add),
                 reads=[f1b[d], xTb[d]], writes=[xTb[d]])

    def phase_ab():
        NSLOT = 8
        a32, a16 = new_phase()
        xT, xTb = a32.take_chunks("xT", 8, G)
        f1, f1b = a32.take_chunks("f1", 8, G)
        rstd, rstdb = a32.take("rstd", G)
        tmp32 = [a32.take("tmp%d" % i, G) for i in range(2)]
        xn, xnb = a16.take_chunks("xn", 8, G)
        hid, hidb = a16.take("hid", NFF * G)
        sq_pair = [a16.take("sq%d" % i, G) for i in range(2)]
        stg = [a16.take("stg%d" % i, G) for i in range(4)]
        dtst = [a32.take("dtst%d" % i, 32) for i in range(2)]
        ws = WStream(a32, a16, 3, NSLOT, 4096)
        wdt32, wdt32b = a32.take("wdt32", 256)
        wdtb, wdtbb = a16.take("wdtb", 256)
        S.dma("sp", wdt32, wdt_d[:, :], wdt32b, writes=[wdt32b])
        S.op("dve", lambda e: e.tensor_copy(out=wdtb, in_=wdt32), reads=[wdt32b], writes=[wdtbb])
        sti = [0]

        def stage():
            sti[0] += 1
            return stg[sti[0] % 4]

        for gg in range(NKG):
            pre = gg < NPG
            g = gg - NPG
            t0 = g * G
            k0 = gg * G
            src = xTp_d[:, gg * G:(gg + 1) * G] if pre else xT_d[:, t0:t0 + G]
            for c in range(8):
                S.dma("sp", xT[:, c * G:(c + 1) * G], src[c * 128:(c + 1) * 128, :], xTb[c], writes=[xTb[c]])
            ffn(a32, a16, ws, xT, xTb, xn, xnb, hid, hidb, f1, f1b, sq_pair, rstd, rstdb, tmp32, wgu_r[0], wdn_r[0],
                PV["ffn1_pre"], PV["ffn1_post"])
            if not pre:
                for c in range(8):
                    S.dma("sp", XS_d[c * 128:(c + 1) * 128, t0:t0 + G], xT[:, c * G:(c + 1) * G], xTb[c],
                          reads=[xTb[c]], writes=[XS_b[g]])
            xc = lambda c: xT[:, c * G:(c + 1) * G]
            rms_stats(xc, xTb, 8, G, sq_pair, ones_d, 7, rstd, rstdb, 1e-6)
            norm_cast(xc, xTb, PV["mix_pre"], rstd, rstdb, lambda c: xn[:, c * G:(c + 1) * G], xnb)
            for ft in range(56):
                if pre and (ft < 8 or ft >= 40):
                    continue
                w, wb = ws.load(winf_r[ft], 1024)
                p = ft % 4
                for c in range(8):
                    mm(p, PS[p][:, :], w[:, c * 128:(c + 1) * 128], xn[:, c * G:(c + 1) * G], c == 0, c == 7, [wb, xnb[c]])
                st_ap, st_b = stage()
                if ft < 8:
                    dst, db = QT_d[ft * 128:(ft + 1) * 128, t0:t0 + G], Q_b[g]
                elif ft < 16:
                    dst, db = KT_d[(ft - 8) * 128:(ft - 7) * 128, k0:k0 + G], K_b[gg]
                elif ft < 40:
                    dst, db = XBC_d[(ft - 16) * 128:(ft - 15) * 128, k0:k0 + G], XBC_b[gg]
                else:
                    dst, db = GT_d[(ft - 40) * 128:(ft - 39) * 128, t0:t0 + G], GT_b[g]
                if ft >= 40:
                    bcol = pv[:, PV["b_gate"] + ft - 40:PV["b_gate"] + ft - 39]
                    S.op("act", lambda e, o=st_ap, i_=PS[p][:, :], b=bcol: e.activation(out=o, in_=i_, func=AF.Sigmoid, bias=b),
                         reads=[psb[p], cbuf], writes=[st_b])
                else:
                    evac_copy(st_ap, PS[p][:, :], [psb[p]], [st_b])
                S.dma("sp", dst, st_ap, st_b, reads=[st_b], writes=[db])
            for blk in range(6):
                if pre and blk >= 2:
                    continue
                w, wb = ws.load(wint_r[blk], 4096)
                for tt in range(G // 128):
                    p = 4 + (blk * 4 + tt) % 2
                    for c in range(8):
                        mm(p, PS[p][:, :], xn[:, c * G + tt * 128:c * G + (tt + 1) * 128], w[:, c * 512:(c + 1) * 512],
                           c == 0, c == 7, [wb, xnb[c]])
                    st_ap, st_b = stage()
                    evac_copy(st_ap, PS[p][:, :], [psb[p]], [st_b])
                    if blk < 2:
                        r0 = k0 + tt * 128
                        S.dma("sp", V_d[r0:r0 + 128, blk * 512:(blk + 1) * 512], st_ap, st_b, reads=[st_b], writes=[V_b[gg]])
                    else:
                        r0 = t0 + tt * 128
                        S.dma("sp", Z_d[r0:r0 + 128, (blk - 2) * 512:(blk - 1) * 512], st_ap, st_b, reads=[st_b],
                              writes=[Z_b[g]])
            for tt in range(G // 128):
                for c in range(8):
                    mm(6, PS[6][:, 0:32], xn[:, c * G + tt * 128:c * G + (tt + 1) * 128], wdtb[:, c * 32:(c + 1) * 32],
                       c == 0, c == 7, [wdtbb, xnb[c]])
                d_ap, d_b = dtst[tt % 2]
                S.op("dve", lambda e, o=d_ap, i_=PS[6][:, 0:32]: e.tensor_copy(out=o, in_=i_), reads=[psb[6]], writes=[d_b])
                r0 = k0 + tt * 128
                S.dma("sp", DT_d[r0:r0 + 128, :], d_ap, d_b, reads=[d_b], writes=[DT_b[gg]])
        end_phase(a32, a16)

    def phase_c():
        a32, a16 = new_phase()
        kT = [a16.take("kT%d" % i, KVT) for i in range(2)]
        vv = [a16.take("vv%d" % i, KVT) for i in range(2)]
        qT = [a16.take("qT%d" % i, G) for i in range(2)]
        pp = [a16.take("pp%d" % i, G) for i in range(4)]
        sq_pair = [a16.take("sqc%d" % i, G) for i in range(2)]
        ost = [a16.take("ost%d" % i, G) for i in range(2)]
        r1, r1b = a32.take("r1", G)
        r2, r2b = a32.take("r2", G)
        o1, o1b = a32.take("o1", G)
        o2, o2b = a32.take("o2", G)
        rs, rsb = a32.take("rsc", G)
        pi = [0]
        pend_c = [None]

        def load_kv(h):
            k_ap, k_b = kT[h % 2]
            v_ap, v_b = vv[h % 2]
            S.dma("sp", k_ap, KT_d[h * 128:(h + 1) * 128, :], k_b, reads=K_b, writes=[k_b])
            for q4 in range(0, NCH, 8):
                S.dma("sp", v_ap[:, q4 * 128:(q4 + 8) * 128].rearrange("p (k e) -> p k e", e=128),
                      V_d[q4 * 128:(q4 + 8) * 128, h * 128:(h + 1) * 128].rearrange("(k p) e -> p k e", p=128), v_b,
                      reads=V_b, writes=[v_b])

        def load_q(h, j):
            q_ap, q_b = qT[(h * NG_ + j) % 2]
            S.dma("sp", q_ap, QT_d[h * 128:(h + 1) * 128, j * G:(j + 1) * G], q_b, reads=[Q_b[j]], writes=[q_b])

        load_kv(0)
        load_q(0, 0)
        for h in range(NH):
            k_ap, k_b = kT[h % 2]
            v_ap, v_b = vv[h % 2]
            if h + 1 < NH:
                load_kv(h + 1)
            for j in range(NG_):
                q_ap, q_b = qT[(h * NG_ + j) % 2]
                if j + 1 < NG_:
                    load_q(h, j + 1)
                elif h + 1 < NH:
                    load_q(h + 1, 0)
                npk = NPG * 4
                nkt = npk + 4 * j + 4

                def emit_s(kt, k_ap=k_ap, k_b=k_b, q_ap=q_ap, q_b=q_b, j=j):
                    sa, sb_ = (4, 5) if kt % 2 == 0 else (6, 7)
                    mm(sa, PS[sa][:, :], k_ap[0:64, kt * 128:(kt + 1) * 128], q_ap[0:64, :], True, True, [k_b, q_b])
                    mm(sb_, PS[sb_][:, :], k_ap[64:128, kt * 128:(kt + 1) * 128], q_ap[64:128, :], True, True, [k_b, q_b])
                    p1, p1b = pp[pi[0] % 4]
                    p2, p2b = pp[(pi[0] + 1) % 4]
                    pi[0] += 2
                    bias_ = fl_bias if kt < NPG * 4 else k_zero
                    S.op("act", lambda e, o=p1, i_=PS[sa][:, :], b_=bias_: e.activation(out=o, in_=i_, func=AF.Exp, scale=0.125,
                                                                                     bias=b_),
                         reads=[psb[sa], cbuf, kbuf], writes=[p1b])
                    S.op("act", lambda e, o=p2, i_=PS[sb_][:, :], b_=bias_: e.activation(out=o, in_=i_, func=AF.Exp, scale=0.125,
                                                                                      bias=b_),
                         reads=[psb[sb_], cbuf, kbuf], writes=[p2b])
                    m = kt - NPG * 4 - 4 * j
                    if m >= 0:
                        mk = maskb[:, m * 512:(m + 1) * 512]
                        S.op("dve", lambda e, o=p1, mk=mk: e.tensor_tensor(out=o, in0=o, in1=mk, op=ALU.mult),
                             reads=[p1b, c16buf], writes=[p1b])
                        S.op("dve", lambda e, o=p2, mk=mk: e.tensor_tensor(out=o, in0=o, in1=mk, op=ALU.mult),
                             reads=[p2b, c16buf], writes=[p2b])
                    return p1, p1b, p2, p2b

                pend = emit_s(0)
                for kt in range(nkt):
                    p1, p1b, p2, p2b = pend
                    if kt + 1 < nkt:
                        pend = emit_s(kt + 1)
                    first, last = kt == 0, kt == nkt - 1
                    vt = v_ap[:, kt * 128:(kt + 1) * 128]
                    mm(0, PS[0][:, :], vt, p1, first, last, [v_b, p1b])
                    mm(1, PS[1][:, :], vt, p2, first, last, [v_b, p2b])
                    mm(2, PS[2][:, :], ones_1, p1, first, last, [c16buf, p1b])
                    mm(3, PS[3][:, :], ones_1, p2, first, last, [c16buf, p2b])
                    if kt == 2 and pend_c[0] is not None:
                        pend_c[0]()
                        pend_c[0] = None
                S.op("dve", lambda e: e.reciprocal(out=r1, in_=PS[2][:, :]), reads=[psb[2]], writes=[r1b])
                S.op("dve", lambda e: e.reciprocal(out=r2, in_=PS[3][:, :]), reads=[psb[3]], writes=[r2b])
                S.op("dve", lambda e: e.tensor_tensor(out=o1, in0=PS[0][:, :], in1=r1, op=ALU.mult),
                     reads=[psb[0], r1b], writes=[o1b])
                S.op("dve", lambda e: e.scalar_tensor_tensor(out=o2, in0=PS[1][:, :], scalar=lam_col, in1=r2,
                                                             op0=ALU.mult, op1=ALU.mult),
                     reads=[psb[1], r2b, lbuf], writes=[o2b])
                S.op("dve", lambda e: e.tensor_tensor(out=o1, in0=o1, in1=o2, op=ALU.subtract),
                     reads=[o1b, o2b], writes=[o1b])

                def ep2(h=h, j=j):
                    sq_ap, sq_b = sq_pair[j % 2]
                    S.op("act", lambda e, o=sq_ap: e.activation(out=o, in_=o1, func=AF.Square), reads=[o1b], writes=[sq_b])
                    mm(4, PS[4][:, :], ones_h, sq_ap, True, True, [sq_b, c16buf])
                    rsqrt_ps(4, G, rs, rsb, 1e-5)
                    os_ap, os_b = ost[j % 2]
                    S.op("dve", lambda e, o=os_ap: e.scalar_tensor_tensor(out=o, in0=o1, scalar=gsub_col, in1=rs,
                                                                          op0=ALU.mult, op1=ALU.mult),
                         reads=[o1b, rsb, lbuf], writes=[os_b])
                    S.dma("sp", ON_d[h * 128:(h + 1) * 128, j * G:(j + 1) * G], os_ap, os_b, reads=[os_b], writes=[ON_b[j]])
                pend_c[0] = ep2
        if pend_c[0] is not None:
            pend_c[0]()
            pend_c[0] = None
        end_phase(a32, a16)


    def phase_d():
        a32, a16 = new_phase()
        dtr, dtrb = a32.take("dtr", 128)
        dtv, dtvb = a32.take("dtv", 128)
        adt, adtb = a32.take("adt", 128)
        nega, negab = a32.take("nega", 32)
        onesf, onesfb = a32.take("onesf", 128)
        acs, acsb = a32.take("acs", 32)
        tot, totb = a32.take("tot", 32)
        dsd, dsdb = a32.take("dsd", 32)
        dab, dabb = a32.take("dab", 32)
        tmps, tmpsb = a32.take("tmps", 32)
        rhs4all, _r40 = a32.take("rhs4all", 4096)
        rhs4 = [(rhs4all[:, i * 512:(i + 1) * 512], _r40 if i == 0 else Buf("rhs4_%d" % i, a32.prior)) for i in range(8)]
        a32.bufs.extend([b_ for _, b_ in rhs4[1:]])
        acsbc = [a32.take("acsbc%d" % i, 512) for i in range(2)]
        dif = [a32.take("dif%d" % i, 512) for i in range(2)]
        St, _stb0 = a32.take("St", 2048)
        Stb = [_stb0] + [Buf("St%d" % i, a32.prior) for i in range(1, 4)]
        a32.bufs.extend(Stb[1:])
        yz, yzb = a32.take("yz", 2048)
        ss, ssb = a32.take("ss", 8)
        xin = [a16.take("xin%d" % i, 520) for i in range(4)]
        xc, xcb = a16.take("xc", 24 * G)
        zt = [a16.take("zt%d" % i, 2048) for i in range(2)]
        xst, xstb = a16.take("xst", 2048)
        btk, btkb = a16.take("btk", 512)
        MT = [a16.take("MT%d" % i, 512) for i in range(2)]
        Ce = [a16.take("Ce%d" % i, 512) for i in range(2)]
        LT = [a16.take("LT%d" % i, 512) for i in range(2)]
        ea = [a16.take("ea%d" % i, 512) for i in range(2)]
        cbm, cbmb = a16.take("cbm", 512)
        trib4, trib4b = a16.take("trib4", 512)
        xd, xdb = a16.take("xd", 2048)
        Sbf, _sbf0 = a16.take("Sbf", 2048)
        Sbfb = [_sbf0] + [Buf("Sbf%d" % i, a16.prior) for i in range(1, 4)]
        a16.bufs.extend(Sbfb[1:])
        yzn, yznb = a16.take("yzn", 2048)
        junk, junkb = yzn[:, 0:512], yznb
        sz, szb = a16.take("sz", 2048)
        ynT = [a16.take("ynT%d" % i, 16 * G) for i in range(1)]
        cdiag, cdiagb = a16.take("cdiag", 96 * 128)
        xdt, xdtb = a16.take("xdt", 2048)
        for fj in range(96):
            S.op("dve", lambda e, fj=fj: e.tensor_scalar(out=cdiag[:, fj * 128:(fj + 1) * 128], in0=ident32,
                                                          scalar1=pv[:, PV["conv_w"] + fj:PV["conv_w"] + fj + 1],
                                                          scalar2=None, op0=ALU.mult), reads=[cbuf], writes=[cdiagb])

        S.op("act", lambda e: e.activation(out=nega, in_=rowb[:, 32:64], func=AF.Exp), reads=[cbuf], writes=[negab])
        S.op("dve", lambda e: e.tensor_scalar(out=nega, in0=nega, scalar1=-1.0, scalar2=None, op0=ALU.mult),
             reads=[negab], writes=[negab])
        S.op("dve", lambda e: e.memset(onesf, 1.0), writes=[onesfb])
        S.op("dve", lambda e: e.memset(St, 0.0), writes=Stb)
        S.op("dve", lambda e: e.memset(Sbf, 0.0), writes=Sbfb)
        for a in range(4):
            S.op("dve", lambda e, a=a: e.tensor_copy(out=trib4[:, a * 128:(a + 1) * 128], in_=tri32), reads=[cbuf],
                 writes=[trib4b])
        psT = [PS[4][:, :].bitcast(BF16), PS[5][:, :].bitcast(BF16)]
        psB = PS[6][:, :].bitcast(BF16)
        xi = [0]
        pending = [None]
        for gg in range(NKG):
            pre = gg < NPG
            g = gg - NPG
            t0 = g * G
            k0 = gg * G
            if NPG > 0 and gg == NPG:
                S.op("dve", lambda e: e.tensor_scalar(out=St, in0=St, scalar1=fl_valid, scalar2=None, op0=ALU.mult),
                     reads=Stb + [cbuf], writes=Stb)
                S.op("dve", lambda e: e.tensor_scalar(out=Sbf, in0=Sbf, scalar1=fl_valid, scalar2=None, op0=ALU.mult),
                     reads=Sbfb + [cbuf], writes=Sbfb)
            for f in range(24):
                x_ap, x_b = xin[xi[0] % 4]
                xi[0] += 1
                if gg == 0:
                    S.op("dve", lambda e, o=x_ap[:, 0:3]: e.memset(o, 0.0), writes=[x_b])
                    S.dma("sp", x_ap[:, 3:3 + G], XBC_d[f * 128:(f + 1) * 128, 0:G], x_b, reads=[XBC_b[0]], writes=[x_b])
                else:
                    S.dma("sp", x_ap[:, 0:3 + G], XBC_d[f * 128:(f + 1) * 128, k0 - 3:k0 + G], x_b,
                          reads=[XBC_b[gg - 1], XBC_b[gg]], writes=[x_b])
                    if gg == NPG:
                        S.op("dve", lambda e, o=x_ap[:, 0:3]: e.tensor_scalar(out=o, in0=o, scalar1=fl_valid, scalar2=None,
                                                                              op0=ALU.mult), reads=[x_b, cbuf], writes=[x_b])
                cp = f % 4
                for j in range(4):
                    mm(cp, PS[cp][:, :], cdiag[:, (f * 4 + j) * 128:(f * 4 + j + 1) * 128], x_ap[:, j:j + G], j == 0, j == 3,
                       [cdiagb, x_b])
                S.op("act", lambda e, o=xc[:, f * G:(f + 1) * G], i_=PS[cp][:, :], b=pv[:, PV["conv_b"] + f:PV["conv_b"] + f + 1]:
                     e.activation(out=o, in_=i_, func=AF.Silu, bias=b), reads=[psb[cp], cbuf], writes=[xcb])
            S.dma("sp", dtr.rearrange("p (k h) -> p k h", h=32),
                  DT_d[k0:k0 + G, :].rearrange("(k p) h -> p k h", p=128), dtrb, reads=[DT_b[gg]], writes=[dtrb])
            for k in range(4):
                S.op("dve", lambda e, k=k: e.tensor_tensor(out=dtr[:, k * 32:(k + 1) * 32], in0=dtr[:, k * 32:(k + 1) * 32],
                                                           in1=rowb[:, 0:32], op=ALU.add), reads=[dtrb, cbuf], writes=[dtrb])
            S.op("act", lambda e: e.activation(out=dtv, in_=dtr, func=AF.Exp), reads=[dtrb], writes=[dtvb])
            S.op("act", lambda e: e.activation(out=dtv, in_=dtv, func=AF.Ln, bias=k_one), reads=[dtvb, kbuf], writes=[dtvb])
            for k in range(4):
                S.op("dve", lambda e, k=k: e.tensor_tensor(out=adt[:, k * 32:(k + 1) * 32], in0=dtv[:, k * 32:(k + 1) * 32],
                                                           in1=nega, op=ALU.mult), reads=[dtvb, negab], writes=[adtb])
            yn_ap, yn_b = ynT[0]
            for k in range(4):
                c0 = k * 128
                r0 = t0 + c0
                z_ap, z_b = zt[k % 2]
                if not pre:
                    S.dma("sp", z_ap, Z_d[r0:r0 + 128, :], z_b, reads=[Z_b[g]], writes=[z_b])
                    S.op("act", lambda e, z_ap=z_ap: e.activation(out=sz, in_=z_ap, func=AF.Silu), reads=[z_b], writes=[szb])
                adk = adt[:, k * 32:(k + 1) * 32]
                dtk = dtv[:, k * 32:(k + 1) * 32]
                if not pre:
                    for q in range(8):
                        r4, r4b = rhs4[q]
                        S.op("dve", lambda e, o=r4, q=q, adk=adk: e.tensor_tensor(
                            out=o.rearrange("p (a b) -> p a b", a=4), in0=tri32.unsqueeze(1).to_broadcast([128, 4, 128]),
                            in1=adk[:, 4 * q:4 * q + 4].unsqueeze(2).to_broadcast([128, 4, 128]), op=ALU.mult),
                            reads=[cbuf, adtb], writes=[r4b])
                for f in range(16):
                    S.op("pe", lambda e, o=psT[f // 8][:, (f % 8) * 128:(f % 8 + 1) * 128], i_=xc[:, f * G + c0:f * G + c0 + 128]:
                         e.transpose(o, i_, identb), reads=[xcb, c16buf], writes=[psb[4 + f // 8]])
                for f in range(4):
                    S.op("pe", lambda e, o=psB[:, f * 128:(f + 1) * 128], i_=xc[:, (16 + f) * G + c0:(16 + f) * G + c0 + 128]:
                         e.transpose(o, i_, identb), reads=[xcb, c16buf], writes=[psb[6]])
                S.op("act", lambda e: e.activation(out=xst[:, 0:1024], in_=psT[0], func=AF.Copy), reads=[psb[4]], writes=[xstb])
                S.op("dve", lambda e: e.tensor_copy(out=xst[:, 1024:2048], in_=psT[1]), reads=[psb[5]], writes=[xstb])
                S.op("dve", lambda e: e.tensor_copy(out=btk, in_=psB[:, 0:512]), reads=[psb[6]], writes=[btkb])
                mm(7, PS[7][:, 0:32], tri32, adk, True, True, [cbuf, adtb])
                mm(7, PS[7][:, 32:64], onesf, adk, True, True, [onesfb, adtb])
                S.op("dve", lambda e: e.tensor_copy(out=acs, in_=PS[7][:, 0:32]), reads=[psb[7]], writes=[acsb])
                S.op("dve", lambda e: e.tensor_copy(out=tot, in_=PS[7][:, 32:64]), reads=[psb[7]], writes=[totb])
                S.op("dve", lambda e: e.tensor_tensor(out=tmps, in0=tot, in1=acs, op=ALU.subtract), reads=[totb, acsb],
                     writes=[tmpsb])
                S.op("act", lambda e: e.activation(out=tmps, in_=tmps, func=AF.Exp), reads=[tmpsb], writes=[tmpsb])
                S.op("dve", lambda e, dtk=dtk: e.tensor_tensor(out=dsd, in0=tmps, in1=dtk, op=ALU.mult),
                     reads=[tmpsb, dtvb], writes=[dsdb])
                S.op("act", lambda e: e.activation(out=dab, in_=tot, func=AF.Exp), reads=[totb], writes=[dabb])
                if not pre:
                    for gr in range(4):
                        mm(6, PS[6][:, gr * 128:(gr + 1) * 128], xc[:, (16 + gr) * G + c0:(16 + gr) * G + c0 + 128],
                           xc[:, (20 + gr) * G + c0:(20 + gr) * G + c0 + 128], True, True, [xcb, btkb])
                    S.op("dve", lambda e: e.tensor_tensor(out=cbm, in0=PS[6][:, :], in1=trib4, op=ALU.mult),
                         reads=[psb[6], trib4b], writes=[cbmb])
                    S.op("dve", lambda e, dtk=dtk: e.tensor_tensor(out=xdt.rearrange("p (h d) -> p h d", d=64),
                                                                   in0=xst.rearrange("p (h d) -> p h d", d=64),
                                                                   in1=dtk.unsqueeze(2).to_broadcast([128, 32, 64]), op=ALU.mult),
                         reads=[xstb, dtvb], writes=[xdtb])
                    def front(q, adk=adk, c0=c0):
                        gr = q // 2
                        r4, r4b = rhs4[q]
                        abk = 7 if q % 2 == 0 else 6
                        mm(abk, PS[abk][:, :], onesf, r4, True, True, [onesfb, r4b])
                        ab_ap, ab_b = acsbc[q % 2]
                        S.op("act", lambda e, o=ab_ap, abk=abk: e.activation(out=o, in_=PS[abk][:, :], func=AF.Copy),
                             reads=[psb[abk]], writes=[ab_b])
                        d_ap, d_b = dif[q % 2]
                        for hh in range(4):
                            h = 4 * q + hh
                            S.op("dve", lambda e, o=d_ap[:, hh * 128:(hh + 1) * 128], i_=ab_ap[:, hh * 128:(hh + 1) * 128], h=h:
                                 e.tensor_scalar(out=o, in0=i_, scalar1=acs[:, h:h + 1], scalar2=0.0, op0=ALU.subtract, op1=ALU.min),
                                 reads=[ab_b, acsb], writes=[d_b])
                        l_ap, l_b = LT[q % 2]
                        e_ap, e_b = ea[q % 2]
                        S.op("act", lambda e, o=l_ap, i_=d_ap: e.activation(out=o, in_=i_, func=AF.Exp), reads=[d_b], writes=[l_b])
                        S.op("act", lambda e, o=e_ap, i_=ab_ap: e.activation(out=o, in_=i_, func=AF.Exp), reads=[ab_b], writes=[e_b])
                        m_ap, m_b = MT[q % 2]
                        c_ap, c_b = Ce[q % 2]
                        S.op("dve", lambda e, o=m_ap, i_=l_ap, cb_=cbm[:, gr * 128:(gr + 1) * 128]: e.tensor_tensor(
                            out=o.rearrange("p (a b) -> p a b", a=4), in0=i_.rearrange("p (a b) -> p a b", a=4),
                            in1=cb_.unsqueeze(1).to_broadcast([128, 4, 128]), op=ALU.mult),
                            reads=[l_b, cbmb], writes=[m_b])
                        S.op("dve", lambda e, o=c_ap, i_=e_ap, cc=xc[:, (20 + gr) * G + c0:(20 + gr) * G + c0 + 128]: e.tensor_tensor(
                            out=o.rearrange("p (a b) -> p a b", a=4), in0=i_.rearrange("p (a b) -> p a b", a=4),
                            in1=cc.unsqueeze(1).to_broadcast([128, 4, 128]),
                            op=ALU.mult), reads=[e_b, xcb], writes=[c_b])
                        return m_ap, m_b, c_ap, c_b

                    def back(q, m_ap, m_b, c_ap, c_b):
                        for hh in range(4):
                            h = 4 * q + hh
                            bk = h // 8
                            o = PS[bk][:, (h % 8) * 64:(h % 8 + 1) * 64]
                            mm(bk, o, m_ap[:, hh * 128:(hh + 1) * 128], xdt[:, h * 64:(h + 1) * 64], True, False, [m_b, xdtb])
                            mm(bk, o, c_ap[:, hh * 128:(hh + 1) * 128], Sbf[:, h * 64:(h + 1) * 64], False, False, [c_b, Sbfb[bk]])
                            mm(bk, o, Ddiag[:, h * 128:(h + 1) * 128], xst[:, h * 64:(h + 1) * 64], False, True, [c16buf, xstb])

                def st_part1():
                    S.op("dve", lambda e: e.tensor_tensor(out=xd.rearrange("p (h d) -> p h d", d=64),
                                                          in0=xst.rearrange("p (h d) -> p h d", d=64),
                                                          in1=dsd.unsqueeze(2).to_broadcast([128, 32, 64]), op=ALU.mult),
                         reads=[xstb, dsdb], writes=[xdb])
                    for gr in range(4):
                        S.op("dve", lambda e, gr=gr: e.tensor_tensor(
                            out=St[:, gr * 512:(gr + 1) * 512].rearrange("p (h d) -> p h d", d=64),
                            in0=St[:, gr * 512:(gr + 1) * 512].rearrange("p (h d) -> p h d", d=64),
                            in1=dab[:, gr * 8:(gr + 1) * 8].unsqueeze(2).to_broadcast([128, 8, 64]), op=ALU.mult),
                            reads=[Stb[gr], dabb], writes=[Stb[gr]])

                def st_part2(grs):
                    for gr in grs:
                        sbk = 7 if gr % 2 == 0 else 6
                        mm(sbk, PS[sbk][:, :], btk[:, gr * 128:(gr + 1) * 128], xd[:, gr * 512:(gr + 1) * 512], True, True,
                           [btkb, xdb])
                        S.op("dve", lambda e, gr=gr, sbk=sbk: e.tensor_tensor(out=St[:, gr * 512:(gr + 1) * 512],
                                                                              in0=St[:, gr * 512:(gr + 1) * 512],
                                                                              in1=PS[sbk][:, :], op=ALU.add),
                             reads=[Stb[gr], psb[sbk]], writes=[Stb[gr]])

                def sbf_refresh():
                    for gr in range(4):
                        S.op("act", lambda e, gr=gr: e.activation(out=Sbf[:, gr * 512:(gr + 1) * 512],
                                                                  in_=St[:, gr * 512:(gr + 1) * 512], func=AF.Copy),
                             reads=[Stb[gr]], writes=[Sbfb[gr]])

                if pre:
                    st_part1()
                    st_part2((0, 1, 2, 3))
                    sbf_refresh()
                else:
                    steps = pending[0] or []
                    pending[0] = None
                    extra = {4: [st_part1], 5: [lambda: st_part2((0, 1))], 6: [lambda: st_part2((2, 3))]}
                    for i_, st_ in enumerate(steps):
                        extra.setdefault(i_, []).append(st_)
                    fr = front(0)
                    for q in range(8):
                        cur = fr
                        if q + 1 < 8:
                            fr = front(q + 1)
                        back(q, *cur)
                        for fn_ in extra.get(q, []):
                            fn_()
                    sbf_refresh()
                    for gr in range(4):
                        S.op("dve", lambda e, gr=gr: e.tensor_tensor(out=yz[:, gr * 512:(gr + 1) * 512], in0=PS[gr][:, :],
                                                                     in1=sz[:, gr * 512:(gr + 1) * 512], op=ALU.mult),
                             reads=[psb[gr], szb], writes=[yzb])

                    def t1():
                        S.op("dve", lambda e: e.memset(ss[:, 0:4], 0.0), writes=[ssb])
                        for gr in range(4):
                            S.op("act", lambda e, gr=gr: e.activation(out=junk, in_=yz[:, gr * 512:(gr + 1) * 512], func=AF.Square,
                                                                      accum_out=ss[:, gr:gr + 1]), reads=[yzb], writes=[junkb, ssb])
                        S.op("act", lambda e: e.activation(out=ss[:, 4:8], in_=ss[:, 0:4], func=AF.Ln, scale=1.0 / 512.0,
                                                           bias=k_eps5), reads=[ssb, kbuf], writes=[ssb])
                        S.op("act", lambda e: e.activation(out=ss[:, 4:8], in_=ss[:, 4:8], func=AF.Exp, scale=-0.5),
                             reads=[ssb], writes=[ssb])

                    def t2():
                        for gr in range(4):
                            S.op("dve", lambda e, gr=gr: e.tensor_scalar(out=yzn[:, gr * 512:(gr + 1) * 512],
                                                                         in0=yz[:, gr * 512:(gr + 1) * 512],
                                                                         scalar1=ss[:, 4 + gr:5 + gr], scalar2=None, op0=ALU.mult),
                                 reads=[yzb, ssb], writes=[yznb])

                    def t3():
                        for f in range(16):
                            S.op("pe", lambda e, o=psT[f // 8][:, (f % 8) * 128:(f % 8 + 1) * 128], i_=yzn[:, f * 128:(f + 1) * 128]:
                                 e.transpose(o, i_, identb), reads=[yznb, c16buf], writes=[psb[4 + f // 8]])

                    def t4(c0=c0, yn_ap=yn_ap, yn_b=yn_b):
                        for f in range(16):
                            gcol = pv[:, PV["ssd_g"] + f:PV["ssd_g"] + f + 1]
                            src = psT[f // 8][:, (f % 8) * 128:(f % 8 + 1) * 128]
                            dst = yn_ap[:, f * G + c0:f * G + c0 + 128]
                            if f // 8 == 0:
                                S.op("act", lambda e, o=dst, i_=src, gc=gcol: e.activation(out=o, in_=i_, func=AF.Copy, scale=gc),
                                     reads=[psb[4], cbuf], writes=[yn_b])
                            else:
                                S.op("dve", lambda e, o=dst, i_=src, gc=gcol: e.tensor_scalar(out=o, in0=i_, scalar1=gc,
                                                                                             scalar2=None, op0=ALU.mult),
                                     reads=[psb[5], cbuf], writes=[yn_b])
                    pending[0] = [t1, t2, t3, t4]
            if pending[0] is not None:
                for st_ in pending[0]:
                    st_()
                pending[0] = None
            for f in range(16):
                if pre:
                    break
                S.dma("sp", YN_d[f * 128:(f + 1) * 128, t0:t0 + G], yn_ap[:, f * G:(f + 1) * G], yn_b, reads=[yn_b],
                      writes=[YN_b[g]])
        end_phase(a32, a16)


    def phase_e():
        NSLOT = 5
        a32, a16 = new_phase()
        xT, xTb = a32.take_chunks("xT", 8, G)
        f1, f1b = a32.take_chunks("f1", 8, G)
        rstd, rstdb = a32.take("rstd", G)
        tmp32 = [a32.take("tmp%d" % i, G) for i in range(2)]
        memx, memxb = f1[:, 0:8 * MEM], tuple(f1b)
        ws = WStream(a32, a16, 3, NSLOT, 4096)
        xn, xnb = a16.take_chunks("xn", 8, G)
        hid, hidb = a16.take("hid", NFF * G)
        sq_pair = [a16.take("sq%d" % i, G) for i in range(2)]
        aux, auxb = a16.take("aux", 8 * G)
        qx, qxb = a16.take("qx", 8 * G)
        gts = [a16.take("gts%d" % i, 2 * G) for i in range(2)]
        pp = [a16.take("ppx%d" % i, G) for i in range(2)]
        kxT, kxTb = a16.take("kxT", 8 * MEM)
        vx, vxb = a16.take("vx", 2 * D)
        memn, memnb = a16.take("memn", 8 * MEM)
        ons = xn
        onsb = xnb
        yns, ynsb = hid, hidb

        for c in range(8):
            S.dma("sp", memx[:, c * MEM:(c + 1) * MEM], memT_d[c * 128:(c + 1) * 128, :], f1b[0], writes=[memxb])
        mc = lambda c: memx[:, c * MEM:(c + 1) * MEM]
        rms_stats(mc, memxb, 8, MEM, sq_pair, ones_d, 7, rstd[:, 0:MEM], rstdb, 1e-6)
        norm_cast(mc, memxb, PV["mem_g"], rstd[:, 0:MEM], rstdb, lambda c: memn[:, c * MEM:(c + 1) * MEM], memnb)
        for dt_ in range(8):
            w, wb = ws.load(wxk_d[dt_], 1024)
            p = dt_ % 2
            for c in range(8):
                mm(p, PS[p][:, 0:MEM], w[:, c * 128:(c + 1) * 128], memn[:, c * MEM:(c + 1) * MEM], c == 0, c == 7, [wb, memnb])
            evac_copy(kxT[:, dt_ * MEM:(dt_ + 1) * MEM], PS[p][:, 0:MEM], [psb[p]], [kxTb])
        for blk in range(2):
            w, wb = ws.load(wxv_d[blk], 4096)
            for kt in range(2):
                p = 2 + kt
                for c in range(8):
                    mm(p, PS[p][:, :], memn[:, c * MEM + kt * 128:c * MEM + (kt + 1) * 128], w[:, c * 512:(c + 1) * 512],
                       c == 0, c == 7, [wb, memnb])
                evac_copy(vx[:, kt * D + blk * 512:kt * D + (blk + 1) * 512], PS[p][:, :], [psb[p]], [vxb])

        def proj_norm_resid(w_d, src, srcb, gpost, nck=8):
            for d in range(8):
                w, wb = ws.load(w_d[d], nck * 128)
                p = 4 + d % 2
                for c in range(nck):
                    mm(p, PS[p][:, :], w[:, c * 128:(c + 1) * 128], src[:, c * G:(c + 1) * G], c == 0, c == nck - 1, [wb, srcb])
                sq_ap, sq_b = sq_pair[d % 2]
                S.op("dve", lambda e, o=f1[:, d * G:(d + 1) * G], i_=PS[p][:, :]: e.tensor_copy(out=o, in_=i_),
                     reads=[psb[p]], writes=[f1b[d]])
                S.op("act", lambda e, o=sq_ap, i_=f1[:, d * G:(d + 1) * G]: e.activation(out=o, in_=i_, func=AF.Square),
                     reads=[f1b[d]], writes=[sq_b])
                mm(6, PS[6][:, :], ones_d, sq_ap, d == 0, d == 7, [sq_b, c16buf])
            rsqrt_ps(6, G, rstd, rstdb, 1e-6, 1.0)
            resid_add(xT, xTb, f1, f1b, rstd, rstdb, gpost)

        gi = [0]
        for g in range(NG_):
            t0 = g * G
            for c in range(8):
                S.dma("sp", ons[:, c * G:(c + 1) * G], ON_d[c * 128:(c + 1) * 128, t0:t0 + G], onsb[c], reads=[ON_b[g]], writes=[onsb[c]])
            for c in range(16):
                S.dma("sp", yns[:, c * G:(c + 1) * G], YN_d[c * 128:(c + 1) * 128, t0:t0 + G], ynsb, reads=[YN_b[g]], writes=[ynsb])
            for c in range(8):
                S.dma("sp", xT[:, c * G:(c + 1) * G], XS_d[c * 128:(c + 1) * 128, t0:t0 + G], xTb[c], reads=[XS_b[g]], writes=[xTb[c]])
            for d in range(8):
                wa, wab = ws.load(wa_r[d], 1024)
                wsd, wsb = ws.load(ws_r[d], 2048)
                pa, ps_ = (0, 1) if d % 2 == 0 else (2, 3)
                for c in range(8):
                    mm(pa, PS[pa][:, :], wa[:, c * 128:(c + 1) * 128], ons[:, c * G:(c + 1) * G], c == 0, c == 7, [wab, onsb[c]])
                for c in range(16):
                    mm(ps_, PS[ps_][:, :], wsd[:, c * 128:(c + 1) * 128], yns[:, c * G:(c + 1) * G], c == 0, c == 15, [wsb, ynsb])
                gt_ap, gt_b = gts[gi[0] % 2]
                gi[0] += 1
                S.dma("sp", gt_ap[:, 0:G], GT_d[d * 128:(d + 1) * 128, t0:t0 + G], gt_b, reads=[GT_b[g]], writes=[gt_b])
                S.dma("sp", gt_ap[:, G:2 * G], GT_d[D + d * 128:D + (d + 1) * 128, t0:t0 + G], gt_b, reads=[GT_b[g]], writes=[gt_b])
                ta, tab = tmp32[0]
                tb, tbb = tmp32[1]
                S.op("dve", lambda e, pa=pa, g_=gt_ap[:, 0:G]: e.tensor_tensor(out=ta, in0=PS[pa][:, :], in1=g_, op=ALU.mult),
                     reads=[psb[pa], gt_b], writes=[tab])
                S.op("dve", lambda e, ps_=ps_, g_=gt_ap[:, G:2 * G]: e.tensor_tensor(out=tb, in0=PS[ps_][:, :], in1=g_, op=ALU.mult),
                     reads=[psb[ps_], gt_b], writes=[tbb])
                S.op("dve", lambda e, o=aux[:, d * G:(d + 1) * G]: e.tensor_tensor(out=o, in0=ta, in1=tb, op=ALU.add),
                     reads=[tab, tbb], writes=[auxb])
            proj_norm_resid(wmix_r, aux, auxb, PV["mix_post"])
            xc_ = lambda c: xT[:, c * G:(c + 1) * G]
            rms_stats(xc_, xTb, 8, G, sq_pair, ones_d, 7, rstd, rstdb, 1e-6)
            norm_cast(xc_, xTb, PV["xa_pre"], rstd, rstdb, lambda c: xn[:, c * G:(c + 1) * G], xnb)
            for d in range(8):
                w, wb = ws.load(wxq_r[d], 1024)
                p = d % 2
                for c in range(8):
                    mm(p, PS[p][:, :], w[:, c * 128:(c + 1) * 128], xn[:, c * G:(c + 1) * G], c == 0, c == 7, [wb, xnb[c]])
                evac_copy(qx[:, d * G:(d + 1) * G], PS[p][:, :], [psb[p]], [qxb])
            for hd in range(4):
                for kt in range(2):
                    p = 4 + kt
                    for dd in range(2):
                        dt_ = hd * 2 + dd
                        mm(p, PS[p][:, :], kxT[:, dt_ * MEM + kt * 128:dt_ * MEM + (kt + 1) * 128], qx[:, dt_ * G:(dt_ + 1) * G],
                           dd == 0, dd == 1, [kxTb, qxb])
                    p_ap, p_b = pp[kt]
                    S.op("act", lambda e, o=p_ap, p=p: e.activation(out=o, in_=PS[p][:, :], func=AF.Exp, scale=1.0 / 16.0),
                         reads=[psb[p]], writes=[p_b])
                for kt in range(2):
                    p_ap, p_b = pp[kt]
                    for e_ in range(2):
                        mm(e_, PS[e_][:, :], vx[:, kt * D + hd * 256 + e_ * 128:kt * D + hd * 256 + (e_ + 1) * 128], p_ap,
                           kt == 0, kt == 1, [vxb, p_b])
                    mm(2, PS[2][:, :], ones_1, p_ap, kt == 0, kt == 1, [c16buf, p_b])
                ta, tab = tmp32[0]
                S.op("dve", lambda e: e.reciprocal(out=ta, in_=PS[2][:, :]), reads=[psb[2]], writes=[tab])
                for e_ in range(2):
                    S.op("dve", lambda e, e_=e_, o=aux[:, (hd * 2 + e_) * G:(hd * 2 + e_ + 1) * G]:
                         e.tensor_tensor(out=o, in0=PS[e_][:, :], in1=ta, op=ALU.mult), reads=[psb[e_], tab], writes=[auxb])
            proj_norm_resid(wxo_r, aux, auxb, PV["xa_post"])
            ffn(a32, a16, ws, xT, xTb, xn, xnb, hid, hidb, f1, f1b, sq_pair, rstd, rstdb, tmp32, wgu_r[1], wdn_r[1],
                PV["ffn2_pre"], PV["ffn2_post"])
            for c in range(8):
                S.dma("sp", out_d[c * 128:(c + 1) * 128, t0:t0 + G], xT[:, c * G:(c + 1) * G], xTb[c], reads=[xTb[c]], writes=[])
        end_phase(a32, a16)

    if "ab" in stages:
        phase_ab()
    if "d" in stages:
        phase_d()
    if "c" in stages:
        phase_c()
    if "e" in stages:
        phase_e()

    finals = []
    for lst in (XS_b, Q_b, K_b, V_b, Z_b, XBC_b, DT_b, GT_b, ON_b, YN_b):
        pass
    allb = []
    seen = set()
    for e in S.ENGS:
        for o in S.ops[e]:
            if o.dinc is not None and id(o.dinc) not in seen:
                seen.add(id(o.dinc))
                allb.append(o.dinc)
    S.emit(allb)
    return nc


def _tile_w(W):
    K, N = W.shape
    return np.ascontiguousarray(W.reshape(K // 128, 128, N // 128, 128).transpose(2, 1, 0, 3).reshape(N // 128, 128, K))


def _blk_w(W, nb=512):
    K, N = W.shape
    return np.ascontiguousarray(
        W.reshape(K // 128, 128, N // nb, nb).transpose(2, 1, 0, 3).reshape(N // nb, 128, (K // 128) * nb))


def _col(v):
    return np.ascontiguousarray(np.asarray(v).reshape(-1, 128).T)


def _consts():
    c = np.zeros((128, NCONST), np.float32)
    c[:, 0:128] = np.eye(128, dtype=np.float32)
    s = np.arange(128)
    c[:, 128:256] = (s[:, None] <= s[None, :]).astype(np.float32)
    q = np.arange(512)
    for m in range(4):
        c[:, 256 + m * 512:256 + (m + 1) * 512] = (q[None, :] >= 128 * m + s[:, None]).astype(np.float32)
    return c


def prep_shared(inp):
    f = lambda k: np.asarray(inp[k], np.float32)[0]
    sh = {}
    for i, n in ((1, "ffn1"), (2, "ffn2")):
        wgu = f(n + "_w_gu")
        sh["wgu%d" % i] = np.ascontiguousarray(
            wgu.reshape(8, 128, 2, NFF, 128).transpose(3, 1, 2, 0, 4).reshape(NFF, 128, 2048))
        sh["wdn%d" % i] = _tile_w(f(n + "_w_down"))
    win = f("w_in")
    q, k, v, z, xbc, dt, gt = np.split(win, np.cumsum([1024, 1024, 1024, 2048, 3072, 32])[:6], axis=1)
    sh["winf"] = np.concatenate([_tile_w(q), _tile_w(k), _tile_w(xbc), _tile_w(gt)], axis=0)
    sh["wint"] = np.concatenate([_blk_w(v), _blk_w(z)], axis=0)
    sh["wdt"] = np.ascontiguousarray(dt.reshape(8, 128, 32).transpose(1, 0, 2).reshape(128, 256))
    sh["wa"] = _tile_w(f("w_branch_attn"))
    sh["ws"] = _tile_w(f("w_branch_ssd"))
    sh["wmix"] = _tile_w(f("w_mix_out"))
    sh["wxq"] = _tile_w(f("xa_w_q"))
    kv = f("xa_w_kv")
    sh["wxk"] = _tile_w(kv[:, :1024])
    sh["wxv"] = _blk_w(kv[:, 1024:])
    sh["wxo"] = _tile_w(f("xa_w_o"))
    cw = f("ssd_conv_w")
    convw = np.ascontiguousarray(cw.reshape(4, 24, 128).transpose(2, 1, 0).reshape(128, 96))
    pvec = np.concatenate([_col(f("ffn1_pre_g")), _col(f("ffn1_post_g")), _col(f("mix_pre_g")), _col(f("mix_post_g")),
                           _col(f("xa_pre_g")), _col(f("xa_post_g")), _col(f("mem_norm_g")), _col(f("ffn2_pre_g")),
                           _col(f("ffn2_post_g")), _col(f("b_gate")), _col(f("da_subln_g")), convw,
                           _col(f("ssd_conv_b")), _col(f("ssd_norm_g"))], axis=1).astype(np.float32)
    assert pvec.shape == (128, NPV), pvec.shape
    sh["pvec"] = np.ascontiguousarray(pvec)
    row = np.concatenate([f("ssd_dt_bias"), f("ssd_A_log"), f("ssd_D"), f("da_lambda_q1"), f("da_lambda_k1"),
                          f("da_lambda_q2"), f("da_lambda_k2")])[None, :].astype(np.float32)
    assert row.shape == (1, NROW)
    sh["row"] = np.ascontiguousarray(row)
    sh["consts"] = _consts()
    return sh


def core_inputs(x, mem, b, r, T, TP):
    m = {}
    m["xT"] = np.ascontiguousarray(x[b, r * T:(r + 1) * T].T)
    m["xTp"] = np.ascontiguousarray(x[b, 0:TP].T)
    fl = np.zeros((128, 2), np.float32)
    fl[:, 0] = 1.0 if r else 0.0
    fl[:, 1] = 0.0 if r else NEG
    m["flags"] = fl
    m["memT"] = np.ascontiguousarray(mem[b].T)
    return m


def kernel(**inputs):
    x = np.asarray(inputs["x"], np.float32)
    mem = np.asarray(inputs["mem"], np.float32)
    B, SEQ_, _ = x.shape
    T = SEQ_ // 2
    sh = prep_shared(inputs)
    nc = build_program(T, T)
    in_maps = []
    for c in range(8):
        m = dict(sh)
        m.update(core_inputs(x, mem, c // 2, c % 2, T, T))
        in_maps.append(m)
    res = run_bass_kernel_spmd(nc, in_maps, core_ids=list(range(8)))
    out = np.empty((B, SEQ_, D), np.float32)
    for c in range(8):
        out[c // 2, (c % 2) * T:(c % 2 + 1) * T] = res.results[c]["outT"].T
    return out
```

```python
import numpy as np
import ml_dtypes
import concourse.bass as bass
import concourse.mybir as mybir
from concourse.bass_utils import run_bass_kernel_spmd

F32 = mybir.dt.float32
BF16 = mybir.dt.bfloat16
AF = mybir.ActivationFunctionType
ALU = mybir.AluOpType

D = 1024
DFF = 2816
NFF = DFF // 128
SEQ = 4096
G = 512
NH = 8
SH = 32
SG = 4
MEM = 256
NEG = -30000.0

PV = {}
_o = 0
for _n, _w in [("ffn1_pre", 8), ("ffn1_post", 8), ("mix_pre", 8), ("mix_post", 8), ("xa_pre", 8),
               ("xa_post", 8), ("mem_g", 8), ("ffn2_pre", 8), ("ffn2_post", 8), ("b_gate", 16),
               ("subln", 1), ("conv_w", 96), ("conv_b", 24), ("ssd_g", 16)]:
    PV[_n] = _o
    _o += _w
NPV = _o
NROW = 96 + 256
NCONST = 256 + 2048


class Buf:
    __slots__ = ("name", "w", "r", "dsem", "dcnt")

    def __init__(self, name, prior=None):
        self.name = name
        self.w = None
        self.r = list(prior) if prior else []
        self.dsem = None
        self.dcnt = 0

    def tokens(self):
        t = list(self.r)
        if self.w is not None:
            t.append(self.w)
        return t


class BL(list):
    pass


class Op:
    __slots__ = ("eng", "fn", "deps", "dwaits", "need", "ticket", "dinc")

    def __init__(self, eng, fn):
        self.eng = eng
        self.fn = fn
        self.deps = []
        self.dwaits = []
        self.need = False
        self.ticket = None
        self.dinc = None


class Sched:
    ENGS = ("pe", "act", "dve", "pool", "sp")

    def __init__(self, nc):
        self.nc = nc
        self.ops = {e: [] for e in self.ENGS}
        self.nsem = 0

    def _dep(self, op, tok, kind):
        if isinstance(tok, Op):
            if tok.eng == op.eng:
                if op.eng == "pe" or kind != "raw":
                    return
            tok.need = True
            op.deps.append(tok)
        else:
            sb = tok[1]
            op.dwaits.append((sb, sb.dcnt))

    @staticmethod
    def _flat(bs):
        out = []
        for b in bs:
            if isinstance(b, (list, tuple)):
                out.extend(Sched._flat(b))
            else:
                out.append(b)
        return out

    def _track(self, op, tok, reads, writes):
        reads = self._flat(reads)
        writes = self._flat(writes)
        for b in reads:
            if b.w is not None:
                self._dep(op, b.w, "raw")
        for b in writes:
            if b.w is not None:
                self._dep(op, b.w, "waw")
            for t in b.r:
                if t is not op:
                    self._dep(op, t, "war")
        key = tok.eng if isinstance(tok, Op) else id(tok[1])
        for b in reads:
            b.r = [t for t in b.r if (t.eng if isinstance(t, Op) else id(t[1])) != key]
            b.r.append(tok)
        for b in writes:
            b.w = tok
            b.r = []

    def op(self, eng, fn, reads=(), writes=()):
        o = Op(eng, fn)
        for b in self._flat(reads):
            if b.name.startswith("psb"):
                for t in b.r:
                    assert not isinstance(t, Op) or t.eng == eng, ("two engines read one PSUM bank", b.name, eng, t.eng)
        self._track(o, o, reads, writes)
        self.ops[eng].append(o)
        return o

    def dma(self, q, out, in_, sb, reads=(), writes=()):
        o = Op(q, lambda e: e.dma_start(out=out, in_=in_))
        if sb.dsem is None:
            sb.dsem = self.nc.alloc_semaphore(name="d%d_%s" % (self.nsem, sb.name))
            self.nsem += 1
        tok = ("d", sb)
        self._track(o, tok, reads, writes)
        sb.dcnt += 16
        o.dinc = sb
        self.ops[q].append(o)
        return o

    def emit(self, final_waits):
        nc = self.nc
        sems = {e: nc.alloc_semaphore(name="eng_" + e) for e in self.ENGS}
        for e in self.ENGS:
            t = 0
            for o in self.ops[e]:
                if o.need:
                    t += 1
                    o.ticket = t
        handles = {"pe": "tensor", "act": "scalar", "dve": "vector", "pool": "gpsimd", "sp": "sync"}
        with nc.Block() as block:
            for e in self.ENGS:
                ops = self.ops[e]
                fw = final_waits if e == "sp" else []

                def body(eng, ops=ops, e=e, fw=fw):
                    waited = {}
                    for o in ops:
                        need = {}
                        for d in o.deps:
                            k = ("e", d.eng)
                            if d.ticket > need.get(k, (None, 0))[1]:
                                need[k] = (sems[d.eng], d.ticket)
                        for sb, v in o.dwaits:
                            k = ("d", id(sb))
                            if v > need.get(k, (None, 0))[1]:
                                need[k] = (sb.dsem, v)
                        for k, (s, v) in need.items():
                            if waited.get(k, 0) < v:
                                eng.wait_ge(s, v)
                                waited[k] = v
                        ins = o.fn(eng)
                        if o.dinc is not None:
                            ins.then_inc(o.dinc.dsem, 16)
                        elif o.ticket is not None:
                            ins.then_inc(sems[e], 1)
                    for sb in fw:
                        eng.wait_ge(sb.dsem, sb.dcnt)

                getattr(block, handles[e])(body)


def build_program(T, TP=0, stages=("ab", "c", "d", "e"), debug=False):
    NG_ = T // G
    NPG = TP // G
    KVT = TP + T
    NKG = KVT // G
    NCH = KVT // 128
    nc = bass.Bass("TRN2", target_bir_lowering=False)
    S = Sched(nc)
    okind = "ExternalOutput" if debug else "Internal"

    def din(name, shape, dt=F32):
        return nc.dram_tensor(name, list(shape), dt, kind="ExternalInput").ap()

    def dscr(name, shape, dt):
        return nc.dram_tensor(name, list(shape), dt, kind=okind).ap()

    xT_d = din("xT", [D, T])
    xTp_d = din("xTp", [D, max(TP, G)])
    flags_d = din("flags", [128, 2])
    memT_d = din("memT", [D, MEM])
    pvec_d = din("pvec", [128, NPV])
    row_d = din("row", [1, NROW])
    const_d = din("consts", [128, NCONST])
    wgu_d = [din("wgu%d" % i, [NFF, 128, 2048]) for i in (1, 2)]
    wdn_d = [din("wdn%d" % i, [8, 128, DFF]) for i in (1, 2)]
    winf_d = din("winf", [56, 128, 1024])
    wint_d = din("wint", [6, 128, 4096])
    wdt_d = din("wdt", [128, 256])
    wa_d = din("wa", [8, 128, 1024])
    ws_d = din("ws", [8, 128, 2048])
    wmix_d = din("wmix", [8, 128, 1024])
    wxq_d = din("wxq", [8, 128, 1024])
    wxk_d = din("wxk", [8, 128, 1024])
    wxv_d = din("wxv", [2, 128, 4096])
    wxo_d = din("wxo", [8, 128, 1024])
    out_d = nc.dram_tensor("outT", [D, T], F32, kind="ExternalOutput").ap()

    XS_d = dscr("XS", [D, T], F32)
    _W = {}
    QT_d = dscr("QT", [D, T], BF16)
    KT_d = dscr("KT", [D, KVT], BF16)
    V_d = dscr("V", [KVT, D], BF16)
    Z_d = dscr("Z", [T, 2048], BF16)
    XBC_d = dscr("XBC", [3072, KVT], BF16)
    DT_d = dscr("DT", [KVT, 32], F32)
    GT_d = dscr("GT", [2048, T], BF16)
    ON_d = dscr("ON", [D, T], BF16)
    YN_d = dscr("YN", [2048, T], BF16)

    def dbufs(name, n):
        return [Buf("%s%d" % (name, i)) for i in range(n)]
    XS_b, Q_b, Z_b, GT_b, ON_b, YN_b = [dbufs(n, NG_) for n in ("XS", "Q", "Z", "GT", "ON", "YN")]
    K_b, V_b, XBC_b, DT_b = [dbufs(n, NKG) for n in ("K", "V", "XBC", "DT")]

    A32 = nc.alloc_sbuf_tensor("A32", [128, 12288], F32)
    A16 = nc.alloc_sbuf_tensor("A16", [128, 57344], BF16)
    CST = nc.alloc_sbuf_tensor("CST", [128, NPV + NROW + NCONST + 64], F32)
    C16 = nc.alloc_sbuf_tensor("C16", [128, 128 * 4 + 2048 + 32 * 128], BF16)
    PS = [nc.alloc_psum_tensor("ps%d" % i, [128, 512], F32) for i in range(8)]

    class Arena:
        def __init__(self, t, prior):
            self.t = t
            self.off = 0
            self.prior = prior
            self.bufs = []

        def take(self, name, n):
            ap = self.t[:, self.off:self.off + n]
            self.off += n
            assert self.off <= self.t.shape[1], (name, self.off)
            b = Buf(name, self.prior)
            self.bufs.append(b)
            return ap, b

        def take_chunks(self, name, nch, width):
            ap, b0 = self.take(name, nch * width)
            bl = BL([b0] + [Buf("%s_%d" % (name, i), self.prior) for i in range(1, nch)])
            self.bufs.extend(bl[1:])
            return ap, bl

    state = {"prior32": [], "prior16": [], "priorps": []}
    psb = [Buf("psb%d" % i) for i in range(8)]

    def new_phase():
        toks = []
        for b in state.get("bufs", []):
            toks.extend(b.tokens())
        seen = set()
        pr = []
        for t in toks:
            k = id(t) if isinstance(t, Op) else ("d", id(t[1]))
            if k not in seen:
                seen.add(k)
                pr.append(t)
        a32 = Arena(A32, pr)
        a16 = Arena(A16, pr)
        state["a32"], state["a16"] = a32, a16
        return a32, a16

    def end_phase(a32, a16):
        state["bufs"] = a32.bufs + a16.bufs

    pv = CST[:, 0:NPV]
    rowb = CST[:, NPV:NPV + NROW]
    cst = CST[:, NPV + NROW:NPV + NROW + NCONST]
    lamc = CST[:, NPV + NROW + NCONST:NPV + NROW + NCONST + 64]
    cbuf = Buf("consts")
    c16buf = Buf("c16")
    ident32 = cst[:, 0:128]
    tri32 = cst[:, 128:256]
    identb = C16[:, 0:128]
    ones_d = C16[:, 128:256]
    ones_1 = C16[:, 256:384]
    ones_h = C16[:, 384:512]
    maskb = C16[:, 512:512 + 2048]
    Ddiag = C16[:, 2560:2560 + 4096]

    S.dma("sp", CST[:, 0:NPV], pvec_d[:, :], cbuf, writes=[cbuf])
    S.dma("sp", rowb, row_d[0:1, :].partition_broadcast(128), cbuf, writes=[cbuf])
    S.dma("sp", cst, const_d[:, :], cbuf, writes=[cbuf])
    S.dma("sp", lamc[:, 16:18], flags_d[:, :], cbuf, writes=[cbuf])
    fl_valid = lamc[:, 16:17]
    fl_bias = lamc[:, 17:18]
    S.op("dve", lambda e: e.tensor_copy(out=identb, in_=ident32), reads=[cbuf], writes=[c16buf])
    S.op("dve", lambda e: e.memset(ones_d, 1.0 / 1024.0), writes=[c16buf])
    S.op("dve", lambda e: e.memset(ones_1, 1.0), writes=[c16buf])
    S.op("dve", lambda e: e.memset(ones_h, 1.0 / 128.0), writes=[c16buf])
    S.op("dve", lambda e: e.tensor_copy(out=maskb, in_=cst[:, 256:256 + 2048]), reads=[cbuf], writes=[c16buf])
    for h in range(SH):
        S.op("dve", lambda e, h=h: e.tensor_scalar(out=Ddiag[:, h * 128:(h + 1) * 128], in0=ident32,
                                                     scalar1=rowb[:, 64 + h:65 + h], scalar2=None, op0=ALU.mult),
             reads=[cbuf], writes=[c16buf])
    LAM_INIT = 0.8 - 0.6 * 1.0
    lbuf = Buf("lam")
    tmpl = lamc[:, 0:64]
    s1 = lamc[:, 0:1]

    def lam_ops():
        a32, a16 = new_phase()
        t1, t1b = a32.take("lt1", 64)
        t2, t2b = a32.take("lt2", 64)
        acc, accb = a32.take("lacc", 4)
        S.op("dve", lambda e: e.tensor_tensor(out=t1, in0=rowb[:, 96:160], in1=rowb[:, 160:224], op=ALU.mult),
             reads=[cbuf], writes=[t1b])
        S.op("dve", lambda e: e.reduce_sum(out=acc[:, 0:1], in_=t1, axis=mybir.AxisListType.X),
             reads=[t1b], writes=[accb])
        S.op("dve", lambda e: e.tensor_tensor(out=t2, in0=rowb[:, 224:288], in1=rowb[:, 288:352], op=ALU.mult),
             reads=[cbuf], writes=[t2b])
        S.op("dve", lambda e: e.reduce_sum(out=acc[:, 1:2], in_=t2, axis=mybir.AxisListType.X),
             reads=[t2b, accb], writes=[accb])
        S.op("act", lambda e: e.activation(out=acc[:, 2:4], in_=acc[:, 0:2], func=AF.Exp), reads=[accb], writes=[accb])
        S.op("dve", lambda e: e.tensor_tensor(out=lamc[:, 0:1], in0=acc[:, 2:3], in1=acc[:, 3:4], op=ALU.subtract),
             reads=[accb], writes=[lbuf])
        S.op("dve", lambda e: e.tensor_scalar(out=lamc[:, 0:1], in0=lamc[:, 0:1], scalar1=LAM_INIT, scalar2=None,
                                                op0=ALU.add), reads=[lbuf], writes=[lbuf])
        S.op("dve", lambda e: e.tensor_scalar(out=lamc[:, 1:2], in0=pv[:, PV["subln"]:PV["subln"] + 1],
                                                scalar1=1.0 - LAM_INIT, scalar2=None, op0=ALU.mult),
             reads=[cbuf, lbuf], writes=[lbuf])
        end_phase(a32, a16)
    lam_ops()
    lam_col = lamc[:, 0:1]
    gsub_col = lamc[:, 1:2]
    import math
    kbuf = Buf("kconst")
    k_eps6, k_eps5, k_lnhalf, k_one, k_zero = [lamc[:, 8 + i:9 + i] for i in range(5)]
    for ap_, v_ in ((k_eps6, 1e-6), (k_eps5, 1e-5), (k_lnhalf, math.log(0.5)), (k_one, 1.0), (k_zero, 0.0)):
        S.op("dve", lambda e, a=ap_, v=v_: e.memset(a, v), writes=[kbuf])

    class WRef:
        __slots__ = ("f32", "scr", "buf")

        def __init__(self, f32, scr):
            self.f32, self.scr, self.buf = f32, scr, Buf("wscr")

    def wrefs(name, d_ap):
        if d_ap.ndim == 3:
            scr = nc.dram_tensor(name + "_bf", list(d_ap.shape), BF16, kind="Internal").ap()
            return [WRef(d_ap[i], scr[i]) for i in range(d_ap.shape[0])]
        return [WRef(d_ap, None)]

    class WStream:
        def __init__(self, a32, a16, nstage, nslot, cap, scap=2048):
            self.sl = [a16.take("wbf%d" % i, cap) for i in range(nslot)]
            self.stb = [Buf("wst%d" % i) for i in range(nslot)]
            self.j = 0

        def load(self, wr, n, cast_eng=None):
            sl_ap, sl_b = self.sl[self.j % len(self.sl)]
            st_b = self.stb[self.j % len(self.sl)]
            self.j += 1
            if not isinstance(wr, WRef):
                S.dma("pool", sl_ap[:, 0:n], wr, sl_b, writes=[sl_b])
            elif wr.buf.w is None:
                S.dma("pool", sl_ap[:, 0:n], wr.f32, sl_b, writes=[sl_b])
                if wr.scr is not None:
                    S.dma("sp", wr.scr, sl_ap[:, 0:n], st_b, reads=[sl_b], writes=[wr.buf])
            else:
                S.dma("pool", sl_ap[:, 0:n], wr.scr, sl_b, reads=[wr.buf], writes=[sl_b])
            return sl_ap[:, 0:n], sl_b

    wgu_r = [wrefs("wgu%d" % (i + 1), wgu_d[i]) for i in range(2)]
    wdn_r = [wrefs("wdn%d" % (i + 1), wdn_d[i]) for i in range(2)]
    winf_r = wrefs("winf", winf_d)
    wint_r = wrefs("wint", wint_d)
    wa_r, ws_r, wmix_r, wxq_r, wxo_r = [wrefs(n, d) for n, d in
                                        (("wa", wa_d), ("ws", ws_d), ("wmix", wmix_d), ("wxq", wxq_d), ("wxo", wxo_d))]
    evac_rr = [0]

    def evac_copy(out_ap, in_ap, reads, writes):
        evac_rr[0] += 1
        if evac_rr[0] % 2:
            return S.op("act", lambda e: e.activation(out=out_ap, in_=in_ap, func=AF.Copy), reads=reads, writes=writes)
        return S.op("dve", lambda e: e.tensor_copy(out=out_ap, in_=in_ap), reads=reads, writes=writes)

    def mm(ps_i, out_ap, lhsT, rhs, start, stop, reads):
        return S.op("pe", lambda e: e.matmul(out_ap, lhsT, rhs, start=start, stop=stop), reads=reads, writes=[psb[ps_i]])

    def rsqrt_ps(ps_i, width, out_ap, out_b, eps, mul=1.0):
        eb = {1e-6: k_eps6, 1e-5: k_eps5}[eps]
        mb = {1.0: k_zero, 0.5: k_lnhalf}[mul]
        S.op("act", lambda e: e.activation(out=out_ap, in_=PS[ps_i][:, 0:width], func=AF.Ln, bias=eb),
             reads=[psb[ps_i], kbuf], writes=[out_b])
        S.op("act", lambda e: e.activation(out=out_ap, in_=out_ap, func=AF.Exp, scale=-0.5, bias=mb),
             reads=[out_b, kbuf], writes=[out_b])

    def rms_stats(xsrc, xb, nchunk, width, sq_pair, ones_ap, ps_i, rstd_ap, rstd_b, eps):
        for c in range(nchunk):
            sq_ap, sq_b = sq_pair[c % 2]
            S.op("act", lambda e, o=sq_ap[:, 0:width], i_=xsrc(c): e.activation(out=o, in_=i_, func=AF.Square),
                 reads=[xb[c] if isinstance(xb, BL) else xb], writes=[sq_b])
            mm(ps_i, PS[ps_i][:, 0:width], ones_ap, sq_ap[:, 0:width], c == 0, c == nchunk - 1, [sq_b, c16buf])
        rsqrt_ps(ps_i, width, rstd_ap, rstd_b, eps)

    def norm_cast(xsrc, xb, gcol0, rstd_ap, rstd_b, dst, dst_b, nchunk=8):
        for c in range(nchunk):
            S.op("dve", lambda e, c=c: e.scalar_tensor_tensor(out=dst(c), in0=xsrc(c), scalar=pv[:, gcol0 + c:gcol0 + c + 1],
                                                               in1=rstd_ap, op0=ALU.mult, op1=ALU.mult),
                 reads=[xb[c] if isinstance(xb, BL) else xb, rstd_b, cbuf],
                 writes=[dst_b[c] if isinstance(dst_b, BL) else dst_b])

    def ffn(a32, a16, ws, xT, xTb, xn, xnb, hid, hidb, f1, f1b, sq_pair, rstd, rstdb, tmp32, wgu, wdn, gpre, gpost):
        xc = lambda c: xT[:, c * G:(c + 1) * G]
        rms_stats(xc, xTb, 8, G, sq_pair, ones_d, 7, rstd, rstdb, 1e-6)
        norm_cast(xc, xTb, gpre, rstd, rstdb, lambda c: xn[:, c * G:(c + 1) * G], xnb)
        for i in range(NFF):
            w, wb = ws.load(wgu[i], 2048)
            pg, pu = (0, 1) if i % 2 == 0 else (2, 3)
            for c in range(8):
                mm(pg, PS[pg][:, :], w[:, c * 128:(c + 1) * 128], xn[:, c * G:(c + 1) * G], c == 0, c == 7, [wb, xnb[c]])
            for c in range(8):
                mm(pu, PS[pu][:, :], w[:, 1024 + c * 128:1024 + (c + 1) * 128], xn[:, c * G:(c + 1) * G],
                   c == 0, c == 7, [wb, xnb[c]])
            t_ap, t_b = tmp32[i % 2]
            S.op("act", lambda e, o=t_ap, i_=PS[pg][:, :]: e.activation(out=o, in_=i_, func=AF.Silu),
                 reads=[psb[pg]], writes=[t_b])
            S.op("dve", lambda e, o=hid[:, i * G:(i + 1) * G], a=t_ap, b=PS[pu][:, :]:
                 e.tensor_tensor(out=o, in0=b, in1=a, op=ALU.mult), reads=[t_b, psb[pu]], writes=[hidb])
        for d in range(8):
            w, wb = ws.load(wdn[d], DFF)
            p = 4 + d % 2
            for i in range(NFF):
                mm(p, PS[p][:, :], w[:, i * 128:(i + 1) * 128], hid[:, i * G:(i + 1) * G], i == 0, i == NFF - 1, [wb, hidb])
            sq_ap, sq_b = sq_pair[d % 2]
            S.op("dve", lambda e, o=f1[:, d * G:(d + 1) * G], i_=PS[p][:, :]: e.tensor_copy(out=o, in_=i_),
                 reads=[psb[p]], writes=[f1b[d]])
            S.op("act", lambda e, o=sq_ap, i_=f1[:, d * G:(d + 1) * G]: e.activation(out=o, in_=i_, func=AF.Square),
                 reads=[f1b[d]], writes=[sq_b])
            mm(6, PS[6][:, :], ones_d, sq_ap, d == 0, d == 7, [sq_b, c16buf])
        rsqrt_ps(6, G, rstd, rstdb, 1e-6, 0.5)
        resid_add(xT, xTb, f1, f1b, rstd, rstdb, gpost)

    def resid_add(xT, xTb, f1, f1b, rstd, rstdb, gcol0):
        for d in range(8):
            eng = "dve"
            S.op("dve", lambda e, d=d: e.scalar_tensor_tensor(out=f1[:, d * G:(d + 1) * G], in0=f1[:, d * G:(d + 1) * G],
                                                               scalar=pv[:, gcol0 + d:gcol0 + d + 1], in1=rstd,
                                                               op0=ALU.mult, op1=ALU.mult),
                 reads=[f1b[d], rstdb, cbuf], writes=[f1b[d]])
            S.op(eng, lambda e, d=d: e.tensor_tensor(out=xT[:, d * G:(d + 1) * G], in0=f1[:, d * G:(d + 1) * G],
                                                      in1=xT[:, d * G:(d + 1) * G], op=ALU.add),
                 reads=[f1b[d], xTb[d]], writes=[xTb[d]])

    def phase_ab():
        NSLOT = 8
        a32, a16 = new_phase()
        xT, xTb = a32.take_chunks("xT", 8, G)
        f1, f1b = a32.take_chunks("f1", 8, G)
        rstd, rstdb = a32.take("rstd", G)
        tmp32 = [a32.take("tmp%d" % i, G) for i in range(2)]
        xn, xnb = a16.take_chunks("xn", 8, G)
        hid, hidb = a16.take("hid", NFF * G)
        sq_pair = [a16.take("sq%d" % i, G) for i in range(2)]
        stg = [a16.take("stg%d" % i, G) for i in range(4)]
        dtst = [a32.take("dtst%d" % i, 32) for i in range(2)]
        ws = WStream(a32, a16, 3, NSLOT, 4096)
        wdt32, wdt32b = a32.take("wdt32", 256)
        wdtb, wdtbb = a16.take("wdtb", 256)
        S.dma("sp", wdt32, wdt_d[:, :], wdt32b, writes=[wdt32b])
        S.op("dve", lambda e: e.tensor_copy(out=wdtb, in_=wdt32), reads=[wdt32b], writes=[wdtbb])
        sti = [0]

        def stage():
            sti[0] += 1
            return stg[sti[0] % 4]

        for gg in range(NKG):
            pre = gg < NPG
            g = gg - NPG
            t0 = g * G
            k0 = gg * G
            src = xTp_d[:, gg * G:(gg + 1) * G] if pre else xT_d[:, t0:t0 + G]
            for c in range(8):
                S.dma("sp", xT[:, c * G:(c + 1) * G], src[c * 128:(c + 1) * 128, :], xTb[c], writes=[xTb[c]])
            ffn(a32, a16, ws, xT, xTb, xn, xnb, hid, hidb, f1, f1b, sq_pair, rstd, rstdb, tmp32, wgu_r[0], wdn_r[0],
                PV["ffn1_pre"], PV["ffn1_post"])
            if not pre:
                for c in range(8):
                    S.dma("sp", XS_d[c * 128:(c + 1) * 128, t0:t0 + G], xT[:, c * G:(c + 1) * G], xTb[c],
                          reads=[xTb[c]], writes=[XS_b[g]])
            xc = lambda c: xT[:, c * G:(c + 1) * G]
            rms_stats(xc, xTb, 8, G, sq_pair, ones_d, 7, rstd, rstdb, 1e-6)
            norm_cast(xc, xTb, PV["mix_pre"], rstd, rstdb, lambda c: xn[:, c * G:(c + 1) * G], xnb)
            for ft in range(56):
                if pre and (ft < 8 or ft >= 40):
                    continue
                w, wb = ws.load(winf_r[ft], 1024)
                p = ft % 4
                for c in range(8):
                    mm(p, PS[p][:, :], w[:, c * 128:(c + 1) * 128], xn[:, c * G:(c + 1) * G], c == 0, c == 7, [wb, xnb[c]])
                st_ap, st_b = stage()
                if ft < 8:
                    dst, db = QT_d[ft * 128:(ft + 1) * 128, t0:t0 + G], Q_b[g]
                elif ft < 16:
                    dst, db = KT_d[(ft - 8) * 128:(ft - 7) * 128, k0:k0 + G], K_b[gg]
                elif ft < 40:
                    dst, db = XBC_d[(ft - 16) * 128:(ft - 15) * 128, k0:k0 + G], XBC_b[gg]
                else:
                    dst, db = GT_d[(ft - 40) * 128:(ft - 39) * 128, t0:t0 + G], GT_b[g]
                if ft >= 40:
                    bcol = pv[:, PV["b_gate"] + ft - 40:PV["b_gate"] + ft - 39]
                    S.op("act", lambda e, o=st_ap, i_=PS[p][:, :], b=bcol: e.activation(out=o, in_=i_, func=AF.Sigmoid, bias=b),
                         reads=[psb[p], cbuf], writes=[st_b])
                else:
                    evac_copy(st_ap, PS[p][:, :], [psb[p]], [st_b])
                S.dma("sp", dst, st_ap, st_b, reads=[st_b], writes=[db])
            for blk in range(6):
                if pre and blk >= 2:
                    continue
                w, wb = ws.load(wint_r[blk], 4096)
                for tt in range(G // 128):
                    p = 4 + (blk * 4 + tt) % 2
                    for c in range(8):
                        mm(p, PS[p][:, :], xn[:, c * G + tt * 128:c * G + (tt + 1) * 128], w[:, c * 512:(c + 1) * 512],
                           c == 0, c == 7, [wb, xnb[c]])
                    st_ap, st_b = stage()
                    evac_copy(st_ap, PS[p][:, :], [psb[p]], [st_b])
                    if blk < 2:
                        r0 = k0 + tt * 128
                        S.dma("sp", V_d[r0:r0 + 128, blk * 512:(blk + 1) * 512], st_ap, st_b, reads=[st_b], writes=[V_b[gg]])
                    else:
                        r0 = t0 + tt * 128
                        S.dma("sp", Z_d[r0:r0 + 128, (blk - 2) * 512:(blk - 1) * 512], st_ap, st_b, reads=[st_b],
                              writes=[Z_b[g]])
            for tt in range(G // 128):
                for c in range(8):
                    mm(6, PS[6][:, 0:32], xn[:, c * G + tt * 128:c * G + (tt + 1) * 128], wdtb[:, c * 32:(c + 1) * 32],
                       c == 0, c == 7, [wdtbb, xnb[c]])
                d_ap, d_b = dtst[tt % 2]
                S.op("dve", lambda e, o=d_ap, i_=PS[6][:, 0:32]: e.tensor_copy(out=o, in_=i_), reads=[psb[6]], writes=[d_b])
                r0 = k0 + tt * 128
                S.dma("sp", DT_d[r0:r0 + 128, :], d_ap, d_b, reads=[d_b], writes=[DT_b[gg]])
        end_phase(a32, a16)

    def phase_c():
        a32, a16 = new_phase()
        kT = [a16.take("kT%d" % i, KVT) for i in range(2)]
        vv = [a16.take("vv%d" % i, KVT) for i in range(2)]
        qT = [a16.take("qT%d" % i, G) for i in range(2)]
        pp = [a16.take("pp%d" % i, G) for i in range(4)]
        sq_pair = [a16.take("sqc%d" % i, G) for i in range(2)]
        ost = [a16.take("ost%d" % i, G) for i in range(2)]
        r1, r1b = a32.take("r1", G)
        r2, r2b = a32.take("r2", G)
        o1, o1b = a32.take("o1", G)
        o2, o2b = a32.take("o2", G)
        rs, rsb = a32.take("rsc", G)
        pi = [0]
        pend_c = [None]

        def load_kv(h):
            k_ap, k_b = kT[h % 2]
            v_ap, v_b = vv[h % 2]
            S.dma("sp", k_ap, KT_d[h * 128:(h + 1) * 128, :], k_b, reads=K_b, writes=[k_b])
            for q4 in range(0, NCH, 8):
                S.dma("sp", v_ap[:, q4 * 128:(q4 + 8) * 128].rearrange("p (k e) -> p k e", e=128),
                      V_d[q4 * 128:(q4 + 8) * 128, h * 128:(h + 1) * 128].rearrange("(k p) e -> p k e", p=128), v_b,
                      reads=V_b, writes=[v_b])

        def load_q(h, j):
            q_ap, q_b = qT[(h * NG_ + j) % 2]
            S.dma("sp", q_ap, QT_d[h * 128:(h + 1) * 128, j * G:(j + 1) * G], q_b, reads=[Q_b[j]], writes=[q_b])

        load_kv(0)
        load_q(0, 0)
        for h in range(NH):
            k_ap, k_b = kT[h % 2]
            v_ap, v_b = vv[h % 2]
            if h + 1 < NH:
                load_kv(h + 1)
            for j in range(NG_):
                q_ap, q_b = qT[(h * NG_ + j) % 2]
                if j + 1 < NG_:
                    load_q(h, j + 1)
                elif h + 1 < NH:
                    load_q(h + 1, 0)
                npk = NPG * 4
                nkt = npk + 4 * j + 4

                def emit_s(kt, k_ap=k_ap, k_b=k_b, q_ap=q_ap, q_b=q_b, j=j):
                    sa, sb_ = (4, 5) if kt % 2 == 0 else (6, 7)
                    mm(sa, PS[sa][:, :], k_ap[0:64, kt * 128:(kt + 1) * 128], q_ap[0:64, :], True, True, [k_b, q_b])
                    mm(sb_, PS[sb_][:, :], k_ap[64:128, kt * 128:(kt + 1) * 128], q_ap[64:128, :], True, True, [k_b, q_b])
                    p1, p1b = pp[pi[0] % 4]
                    p2, p2b = pp[(pi[0] + 1) % 4]
                    pi[0] += 2
                    bias_ = fl_bias if kt < NPG * 4 else k_zero
                    S.op("act", lambda e, o=p1, i_=PS[sa][:, :], b_=bias_: e.activation(out=o, in_=i_, func=AF.Exp, scale=0.125,
                                                                                     bias=b_),
                         reads=[psb[sa], cbuf, kbuf], writes=[p1b])
                    S.op("act", lambda e, o=p2, i_=PS[sb_][:, :], b_=bias_: e.activation(out=o, in_=i_, func=AF.Exp, scale=0.125,
                                                                                      bias=b_),
                         reads=[psb[sb_], cbuf, kbuf], writes=[p2b])
                    m = kt - NPG * 4 - 4 * j
                    if m >= 0:
                        mk = maskb[:, m * 512:(m + 1) * 512]
                        S.op("dve", lambda e, o=p1, mk=mk: e.tensor_tensor(out=o, in0=o, in1=mk, op=ALU.mult),
                             reads=[p1b, c16buf], writes=[p1b])
                        S.op("dve", lambda e, o=p2, mk=mk: e.tensor_tensor(out=o, in0=o, in1=mk, op=ALU.mult),
                             reads=[p2b, c16buf], writes=[p2b])
                    return p1, p1b, p2, p2b

                pend = emit_s(0)
                for kt in range(nkt):
                    p1, p1b, p2, p2b = pend
                    if kt + 1 < nkt:
                        pend = emit_s(kt + 1)
                    first, last = kt == 0, kt == nkt - 1
                    vt = v_ap[:, kt * 128:(kt + 1) * 128]
                    mm(0, PS[0][:, :], vt, p1, first, last, [v_b, p1b])
                    mm(1, PS[1][:, :], vt, p2, first, last, [v_b, p2b])
                    mm(2, PS[2][:, :], ones_1, p1, first, last, [c16buf, p1b])
                    mm(3, PS[3][:, :], ones_1, p2, first, last, [c16buf, p2b])
                    if kt == 2 and pend_c[0] is not None:
                        pend_c[0]()
                        pend_c[0] = None
                S.op("dve", lambda e: e.tensor_copy(out=o1, in_=PS[0][:, :]), reads=[psb[0]], writes=[o1b])
                S.op("dve", lambda e: e.tensor_copy(out=o2, in_=PS[1][:, :]), reads=[psb[1]], writes=[o2b])
                S.op("dve", lambda e: e.tensor_copy(out=r1, in_=PS[2][:, :]), reads=[psb[2]], writes=[r1b])
                S.op("dve", lambda e: e.tensor_copy(out=r2, in_=PS[3][:, :]), reads=[psb[3]], writes=[r2b])
                S.op("dve", lambda e: e.reciprocal(out=r1, in_=r1), reads=[r1b], writes=[r1b])
                S.op("dve", lambda e: e.reciprocal(out=r2, in_=r2), reads=[r2b], writes=[r2b])
                S.op("dve", lambda e: e.tensor_tensor(out=o1, in0=o1, in1=r1, op=ALU.mult),
                     reads=[o1b, r1b], writes=[o1b])
                S.op("dve", lambda e: e.scalar_tensor_tensor(out=o2, in0=o2, scalar=lam_col, in1=r2,
                                                             op0=ALU.mult, op1=ALU.mult),
                     reads=[o2b, r2b, lbuf], writes=[o2b])
                S.op("dve", lambda e: e.tensor_tensor(out=o1, in0=o1, in1=o2, op=ALU.subtract),
                     reads=[o1b, o2b], writes=[o1b])

                def ep2(h=h, j=j):
                    sq_ap, sq_b = sq_pair[j % 2]
                    S.op("act", lambda e, o=sq_ap: e.activation(out=o, in_=o1, func=AF.Square), reads=[o1b], writes=[sq_b])
                    mm(4, PS[4][:, :], ones_h, sq_ap, True, True, [sq_b, c16buf])
                    rsqrt_ps(4, G, rs, rsb, 1e-5)
                    os_ap, os_b = ost[j % 2]
                    S.op("dve", lambda e, o=os_ap: e.scalar_tensor_tensor(out=o, in0=o1, scalar=gsub_col, in1=rs,
                                                                          op0=ALU.mult, op1=ALU.mult),
                         reads=[o1b, rsb, lbuf], writes=[os_b])
                    S.dma("sp", ON_d[h * 128:(h + 1) * 128, j * G:(j + 1) * G], os_ap, os_b, reads=[os_b], writes=[ON_b[j]])
                pend_c[0] = ep2
        if pend_c[0] is not None:
            pend_c[0]()
            pend_c[0] = None
        end_phase(a32, a16)


    def phase_d():
        a32, a16 = new_phase()
        dtr, dtrb = a32.take("dtr", 128)
        dtv, dtvb = a32.take("dtv", 128)
        adt, adtb = a32.take("adt", 128)
        nega, negab = a32.take("nega", 32)
        onesf, onesfb = a32.take("onesf", 128)
        acs, acsb = a32.take("acs", 32)
        tot, totb = a32.take("tot", 32)
        dsd, dsdb = a32.take("dsd", 32)
        dab, dabb = a32.take("dab", 32)
        tmps, tmpsb = a32.take("tmps", 32)
        rhs4all, _r40 = a32.take("rhs4all", 4096)
        rhs4 = [(rhs4all[:, i * 512:(i + 1) * 512], _r40 if i == 0 else Buf("rhs4_%d" % i, a32.prior)) for i in range(8)]
        a32.bufs.extend([b_ for _, b_ in rhs4[1:]])
        acsbc = [a32.take("acsbc%d" % i, 512) for i in range(2)]
        dif = [a32.take("dif%d" % i, 512) for i in range(2)]
        St, _stb0 = a32.take("St", 2048)
        Stb = [_stb0] + [Buf("St%d" % i, a32.prior) for i in range(1, 4)]
        a32.bufs.extend(Stb[1:])
        yz, yzb = a32.take("yz", 2048)
        ss, ssb = a32.take("ss", 8)
        xin = [a16.take("xin%d" % i, 520) for i in range(4)]
        xc, xcb = a16.take("xc", 24 * G)
        zt = [a16.take("zt%d" % i, 2048) for i in range(2)]
        xst, xstb = a16.take("xst", 2048)
        btk, btkb = a16.take("btk", 512)
        MT = [a16.take("MT%d" % i, 512) for i in range(2)]
        Ce = [a16.take("Ce%d" % i, 512) for i in range(2)]
        LT = [a16.take("LT%d" % i, 512) for i in range(2)]
        ea = [a16.take("ea%d" % i, 512) for i in range(2)]
        cbm, cbmb = a16.take("cbm", 512)
        trib4, trib4b = a16.take("trib4", 512)
        xd, xdb = a16.take("xd", 2048)
        Sbf, _sbf0 = a16.take("Sbf", 2048)
        Sbfb = [_sbf0] + [Buf("Sbf%d" % i, a16.prior) for i in range(1, 4)]
        a16.bufs.extend(Sbfb[1:])
        yzn, yznb = a16.take("yzn", 2048)
        junk, junkb = yzn[:, 0:512], yznb
        sz, szb = a16.take("sz", 2048)
        ynT = [a16.take("ynT%d" % i, 16 * G) for i in range(1)]
        cdiag, cdiagb = a16.take("cdiag", 96 * 128)
        xdt, xdtb = a16.take("xdt", 2048)
        for fj in range(96):
            S.op("dve", lambda e, fj=fj: e.tensor_scalar(out=cdiag[:, fj * 128:(fj + 1) * 128], in0=ident32,
                                                          scalar1=pv[:, PV["conv_w"] + fj:PV["conv_w"] + fj + 1],
                                                          scalar2=None, op0=ALU.mult), reads=[cbuf], writes=[cdiagb])

        S.op("act", lambda e: e.activation(out=nega, in_=rowb[:, 32:64], func=AF.Exp), reads=[cbuf], writes=[negab])
        S.op("dve", lambda e: e.tensor_scalar(out=nega, in0=nega, scalar1=-1.0, scalar2=None, op0=ALU.mult),
             reads=[negab], writes=[negab])
        S.op("dve", lambda e: e.memset(onesf, 1.0), writes=[onesfb])
        S.op("dve", lambda e: e.memset(St, 0.0), writes=Stb)
        S.op("dve", lambda e: e.memset(Sbf, 0.0), writes=Sbfb)
        for a in range(4):
            S.op("dve", lambda e, a=a: e.tensor_copy(out=trib4[:, a * 128:(a + 1) * 128], in_=tri32), reads=[cbuf],
                 writes=[trib4b])
        psT = [PS[4][:, :].bitcast(BF16), PS[5][:, :].bitcast(BF16)]
        psB = PS[6][:, :].bitcast(BF16)
        xi = [0]
        pending = [None]
        for gg in range(NKG):
            pre = gg < NPG
            g = gg - NPG
            t0 = g * G
            k0 = gg * G
            if NPG > 0 and gg == NPG:
                S.op("dve", lambda e: e.tensor_scalar(out=St, in0=St, scalar1=fl_valid, scalar2=None, op0=ALU.mult),
                     reads=Stb + [cbuf], writes=Stb)
                S.op("dve", lambda e: e.tensor_scalar(out=Sbf, in0=Sbf, scalar1=fl_valid, scalar2=None, op0=ALU.mult),
                     reads=Sbfb + [cbuf], writes=Sbfb)
            for f in range(24):
                x_ap, x_b = xin[xi[0] % 4]
                xi[0] += 1
                if gg == 0:
                    S.op("dve", lambda e, o=x_ap[:, 0:3]: e.memset(o, 0.0), writes=[x_b])
                    S.dma("sp", x_ap[:, 3:3 + G], XBC_d[f * 128:(f + 1) * 128, 0:G], x_b, reads=[XBC_b[0]], writes=[x_b])
                else:
                    S.dma("sp", x_ap[:, 0:3 + G], XBC_d[f * 128:(f + 1) * 128, k0 - 3:k0 + G], x_b,
                          reads=[XBC_b[gg - 1], XBC_b[gg]], writes=[x_b])
                    if gg == NPG:
                        S.op("dve", lambda e, o=x_ap[:, 0:3]: e.tensor_scalar(out=o, in0=o, scalar1=fl_valid, scalar2=None,
                                                                              op0=ALU.mult), reads=[x_b, cbuf], writes=[x_b])
                cp = f % 4
                for j in range(4):
                    mm(cp, PS[cp][:, :], cdiag[:, (f * 4 + j) * 128:(f * 4 + j + 1) * 128], x_ap[:, j:j + G], j == 0, j == 3,
                       [cdiagb, x_b])
                S.op("act", lambda e, o=xc[:, f * G:(f + 1) * G], i_=PS[cp][:, :], b=pv[:, PV["conv_b"] + f:PV["conv_b"] + f + 1]:
                     e.activation(out=o, in_=i_, func=AF.Silu, bias=b), reads=[psb[cp], cbuf], writes=[xcb])
            S.dma("sp", dtr.rearrange("p (k h) -> p k h", h=32),
                  DT_d[k0:k0 + G, :].rearrange("(k p) h -> p k h", p=128), dtrb, reads=[DT_b[gg]], writes=[dtrb])
            for k in range(4):
                S.op("dve", lambda e, k=k: e.tensor_tensor(out=dtr[:, k * 32:(k + 1) * 32], in0=dtr[:, k * 32:(k + 1) * 32],
                                                           in1=rowb[:, 0:32], op=ALU.add), reads=[dtrb, cbuf], writes=[dtrb])
            S.op("act", lambda e: e.activation(out=dtv, in_=dtr, func=AF.Exp), reads=[dtrb], writes=[dtvb])
            S.op("act", lambda e: e.activation(out=dtv, in_=dtv, func=AF.Ln, bias=k_one), reads=[dtvb, kbuf], writes=[dtvb])
            for k in range(4):
                S.op("dve", lambda e, k=k: e.tensor_tensor(out=adt[:, k * 32:(k + 1) * 32], in0=dtv[:, k * 32:(k + 1) * 32],
                                                           in1=nega, op=ALU.mult), reads=[dtvb, negab], writes=[adtb])
            yn_ap, yn_b = ynT[0]
            for k in range(4):
                c0 = k * 128
                r0 = t0 + c0
                z_ap, z_b = zt[k % 2]
                if not pre:
                    S.dma("sp", z_ap, Z_d[r0:r0 + 128, :], z_b, reads=[Z_b[g]], writes=[z_b])
                    S.op("act", lambda e, z_ap=z_ap: e.activation(out=sz, in_=z_ap, func=AF.Silu), reads=[z_b], writes=[szb])
                adk = adt[:, k * 32:(k + 1) * 32]
                dtk = dtv[:, k * 32:(k + 1) * 32]
                if not pre:
                    for q in range(8):
                        r4, r4b = rhs4[q]
                        S.op("dve", lambda e, o=r4, q=q, adk=adk: e.tensor_tensor(
                            out=o.rearrange("p (a b) -> p a b", a=4), in0=tri32.unsqueeze(1).to_broadcast([128, 4, 128]),
                            in1=adk[:, 4 * q:4 * q + 4].unsqueeze(2).to_broadcast([128, 4, 128]), op=ALU.mult),
                            reads=[cbuf, adtb], writes=[r4b])
                for f in range(16):
                    S.op("pe", lambda e, o=psT[f // 8][:, (f % 8) * 128:(f % 8 + 1) * 128], i_=xc[:, f * G + c0:f * G + c0 + 128]:
                         e.transpose(o, i_, identb), reads=[xcb, c16buf], writes=[psb[4 + f // 8]])
                for f in range(4):
                    S.op("pe", lambda e, o=psB[:, f * 128:(f + 1) * 128], i_=xc[:, (16 + f) * G + c0:(16 + f) * G + c0 + 128]:
                         e.transpose(o, i_, identb), reads=[xcb, c16buf], writes=[psb[6]])
                S.op("act", lambda e: e.activation(out=xst[:, 0:1024], in_=psT[0], func=AF.Copy), reads=[psb[4]], writes=[xstb])
                S.op("dve", lambda e: e.tensor_copy(out=xst[:, 1024:2048], in_=psT[1]), reads=[psb[5]], writes=[xstb])
                S.op("dve", lambda e: e.tensor_copy(out=btk, in_=psB[:, 0:512]), reads=[psb[6]], writes=[btkb])
                mm(7, PS[7][:, 0:32], tri32, adk, True, True, [cbuf, adtb])
                mm(7, PS[7][:, 32:64], onesf, adk, True, True, [onesfb, adtb])
                S.op("dve", lambda e: e.tensor_copy(out=acs, in_=PS[7][:, 0:32]), reads=[psb[7]], writes=[acsb])
                S.op("dve", lambda e: e.tensor_copy(out=tot, in_=PS[7][:, 32:64]), reads=[psb[7]], writes=[totb])
                S.op("dve", lambda e: e.tensor_tensor(out=tmps, in0=tot, in1=acs, op=ALU.subtract), reads=[totb, acsb],
                     writes=[tmpsb])
                S.op("act", lambda e: e.activation(out=tmps, in_=tmps, func=AF.Exp), reads=[tmpsb], writes=[tmpsb])
                S.op("dve", lambda e, dtk=dtk: e.tensor_tensor(out=dsd, in0=tmps, in1=dtk, op=ALU.mult),
                     reads=[tmpsb, dtvb], writes=[dsdb])
                S.op("act", lambda e: e.activation(out=dab, in_=tot, func=AF.Exp), reads=[totb], writes=[dabb])
                if not pre:
                    for gr in range(4):
                        mm(6, PS[6][:, gr * 128:(gr + 1) * 128], xc[:, (16 + gr) * G + c0:(16 + gr) * G + c0 + 128],
                           xc[:, (20 + gr) * G + c0:(20 + gr) * G + c0 + 128], True, True, [xcb, btkb])
                    S.op("dve", lambda e: e.tensor_tensor(out=cbm, in0=PS[6][:, :], in1=trib4, op=ALU.mult),
                         reads=[psb[6], trib4b], writes=[cbmb])
                    S.op("dve", lambda e, dtk=dtk: e.tensor_tensor(out=xdt.rearrange("p (h d) -> p h d", d=64),
                                                                   in0=xst.rearrange("p (h d) -> p h d", d=64),
                                                                   in1=dtk.unsqueeze(2).to_broadcast([128, 32, 64]), op=ALU.mult),
                         reads=[xstb, dtvb], writes=[xdtb])
                    def front(q, adk=adk, c0=c0):
                        gr = q // 2
                        r4, r4b = rhs4[q]
                        abk = 7 if q % 2 == 0 else 6
                        mm(abk, PS[abk][:, :], onesf, r4, True, True, [onesfb, r4b])
                        ab_ap, ab_b = acsbc[q % 2]
                        S.op("act", lambda e, o=ab_ap, abk=abk: e.activation(out=o, in_=PS[abk][:, :], func=AF.Copy),
                             reads=[psb[abk]], writes=[ab_b])
                        d_ap, d_b = dif[q % 2]
                        for hh in range(4):
                            h = 4 * q + hh
                            S.op("dve", lambda e, o=d_ap[:, hh * 128:(hh + 1) * 128], i_=ab_ap[:, hh * 128:(hh + 1) * 128], h=h:
                                 e.tensor_scalar(out=o, in0=i_, scalar1=acs[:, h:h + 1], scalar2=0.0, op0=ALU.subtract, op1=ALU.min),
                                 reads=[ab_b, acsb], writes=[d_b])
                        l_ap, l_b = LT[q % 2]
                        e_ap, e_b = ea[q % 2]
                        S.op("act", lambda e, o=l_ap, i_=d_ap: e.activation(out=o, in_=i_, func=AF.Exp), reads=[d_b], writes=[l_b])
                        S.op("act", lambda e, o=e_ap, i_=ab_ap: e.activation(out=o, in_=i_, func=AF.Exp), reads=[ab_b], writes=[e_b])
                        m_ap, m_b = MT[q % 2]
                        c_ap, c_b = Ce[q % 2]
                        S.op("dve", lambda e, o=m_ap, i_=l_ap, cb_=cbm[:, gr * 128:(gr + 1) * 128]: e.tensor_tensor(
                            out=o.rearrange("p (a b) -> p a b", a=4), in0=i_.rearrange("p (a b) -> p a b", a=4),
                            in1=cb_.unsqueeze(1).to_broadcast([128, 4, 128]), op=ALU.mult),
                            reads=[l_b, cbmb], writes=[m_b])
                        S.op("dve", lambda e, o=c_ap, i_=e_ap, cc=xc[:, (20 + gr) * G + c0:(20 + gr) * G + c0 + 128]: e.tensor_tensor(
                            out=o.rearrange("p (a b) -> p a b", a=4), in0=i_.rearrange("p (a b) -> p a b", a=4),
                            in1=cc.unsqueeze(1).to_broadcast([128, 4, 128]),
                            op=ALU.mult), reads=[e_b, xcb], writes=[c_b])
                        return m_ap, m_b, c_ap, c_b

                    def back(q, m_ap, m_b, c_ap, c_b):
                        for hh in range(4):
                            h = 4 * q + hh
                            bk = h // 8
                            o = PS[bk][:, (h % 8) * 64:(h % 8 + 1) * 64]
                            mm(bk, o, m_ap[:, hh * 128:(hh + 1) * 128], xdt[:, h * 64:(h + 1) * 64], True, False, [m_b, xdtb])
                            mm(bk, o, c_ap[:, hh * 128:(hh + 1) * 128], Sbf[:, h * 64:(h + 1) * 64], False, False, [c_b, Sbfb[bk]])
                            mm(bk, o, Ddiag[:, h * 128:(h + 1) * 128], xst[:, h * 64:(h + 1) * 64], False, True, [c16buf, xstb])

                def st_part1():
                    S.op("dve", lambda e: e.tensor_tensor(out=xd.rearrange("p (h d) -> p h d", d=64),
                                                          in0=xst.rearrange("p (h d) -> p h d", d=64),
                                                          in1=dsd.unsqueeze(2).to_broadcast([128, 32, 64]), op=ALU.mult),
                         reads=[xstb, dsdb], writes=[xdb])
                    for gr in range(4):
                        S.op("dve", lambda e, gr=gr: e.tensor_tensor(
                            out=St[:, gr * 512:(gr + 1) * 512].rearrange("p (h d) -> p h d", d=64),
                            in0=St[:, gr * 512:(gr + 1) * 512].rearrange("p (h d) -> p h d", d=64),
                            in1=dab[:, gr * 8:(gr + 1) * 8].unsqueeze(2).to_broadcast([128, 8, 64]), op=ALU.mult),
                            reads=[Stb[gr], dabb], writes=[Stb[gr]])

                def st_part2(grs):
                    for gr in grs:
                        sbk = 7 if gr % 2 == 0 else 6
                        mm(sbk, PS[sbk][:, :], btk[:, gr * 128:(gr + 1) * 128], xd[:, gr * 512:(gr + 1) * 512], True, True,
                           [btkb, xdb])
                        S.op("dve", lambda e, gr=gr, sbk=sbk: e.tensor_tensor(out=St[:, gr * 512:(gr + 1) * 512],
                                                                              in0=St[:, gr * 512:(gr + 1) * 512],
                                                                              in1=PS[sbk][:, :], op=ALU.add),
                             reads=[Stb[gr], psb[sbk]], writes=[Stb[gr]])

                def sbf_refresh():
                    for gr in range(4):
                        S.op("act", lambda e, gr=gr: e.activation(out=Sbf[:, gr * 512:(gr + 1) * 512],
                                                                  in_=St[:, gr * 512:(gr + 1) * 512], func=AF.Copy),
                             reads=[Stb[gr]], writes=[Sbfb[gr]])

                if pre:
                    st_part1()
                    st_part2((0, 1, 2, 3))
                    sbf_refresh()
                else:
                    steps = pending[0] or []
                    pending[0] = None
                    extra = {4: [st_part1], 5: [lambda: st_part2((0, 1))], 6: [lambda: st_part2((2, 3))]}
                    for i_, st_ in enumerate(steps):
                        extra.setdefault(i_, []).append(st_)
                    fr = front(0)
                    for q in range(8):
                        cur = fr
                        if q + 1 < 8:
                            fr = front(q + 1)
                        back(q, *cur)
                        for fn_ in extra.get(q, []):
                            fn_()
                    sbf_refresh()
                    for gr in range(4):
                        S.op("dve", lambda e, gr=gr: e.tensor_tensor(out=yz[:, gr * 512:(gr + 1) * 512], in0=PS[gr][:, :],
                                                                     in1=sz[:, gr * 512:(gr + 1) * 512], op=ALU.mult),
                             reads=[psb[gr], szb], writes=[yzb])

                    def t1():
                        S.op("dve", lambda e: e.memset(ss[:, 0:4], 0.0), writes=[ssb])
                        for gr in range(4):
                            S.op("act", lambda e, gr=gr: e.activation(out=junk, in_=yz[:, gr * 512:(gr + 1) * 512], func=AF.Square,
                                                                      accum_out=ss[:, gr:gr + 1]), reads=[yzb], writes=[junkb, ssb])
                        S.op("act", lambda e: e.activation(out=ss[:, 4:8], in_=ss[:, 0:4], func=AF.Ln, scale=1.0 / 512.0,
                                                           bias=k_eps5), reads=[ssb, kbuf], writes=[ssb])
                        S.op("act", lambda e: e.activation(out=ss[:, 4:8], in_=ss[:, 4:8], func=AF.Exp, scale=-0.5),
                             reads=[ssb], writes=[ssb])

                    def t2():
                        for gr in range(4):
                            S.op("dve", lambda e, gr=gr: e.tensor_scalar(out=yzn[:, gr * 512:(gr + 1) * 512],
                                                                         in0=yz[:, gr * 512:(gr + 1) * 512],
                                                                         scalar1=ss[:, 4 + gr:5 + gr], scalar2=None, op0=ALU.mult),
                                 reads=[yzb, ssb], writes=[yznb])

                    def t3():
                        for f in range(16):
                            S.op("pe", lambda e, o=psT[f // 8][:, (f % 8) * 128:(f % 8 + 1) * 128], i_=yzn[:, f * 128:(f + 1) * 128]:
                                 e.transpose(o, i_, identb), reads=[yznb, c16buf], writes=[psb[4 + f // 8]])

                    def t4(c0=c0, yn_ap=yn_ap, yn_b=yn_b):
                        for f in range(16):
                            gcol = pv[:, PV["ssd_g"] + f:PV["ssd_g"] + f + 1]
                            src = psT[f // 8][:, (f % 8) * 128:(f % 8 + 1) * 128]
                            dst = yn_ap[:, f * G + c0:f * G + c0 + 128]
                            if f // 8 == 0:
                                S.op("act", lambda e, o=dst, i_=src, gc=gcol: e.activation(out=o, in_=i_, func=AF.Copy, scale=gc),
                                     reads=[psb[4], cbuf], writes=[yn_b])
                            else:
                                S.op("dve", lambda e, o=dst, i_=src, gc=gcol: e.tensor_scalar(out=o, in0=i_, scalar1=gc,
                                                                                             scalar2=None, op0=ALU.mult),
                                     reads=[psb[5], cbuf], writes=[yn_b])
                    pending[0] = [t1, t2, t3, t4]
            if pending[0] is not None:
                for st_ in pending[0]:
                    st_()
                pending[0] = None
            for f in range(16):
                if pre:
                    break
                S.dma("sp", YN_d[f * 128:(f + 1) * 128, t0:t0 + G], yn_ap[:, f * G:(f + 1) * G], yn_b, reads=[yn_b],
                      writes=[YN_b[g]])
        end_phase(a32, a16)


    def phase_e():
        NSLOT = 5
        a32, a16 = new_phase()
        xT, xTb = a32.take_chunks("xT", 8, G)
        f1, f1b = a32.take_chunks("f1", 8, G)
        rstd, rstdb = a32.take("rstd", G)
        tmp32 = [a32.take("tmp%d" % i, G) for i in range(2)]
        memx, memxb = f1[:, 0:8 * MEM], tuple(f1b)
        ws = WStream(a32, a16, 3, NSLOT, 4096)
        xn, xnb = a16.take_chunks("xn", 8, G)
        hid, hidb = a16.take("hid", NFF * G)
        sq_pair = [a16.take("sq%d" % i, G) for i in range(2)]
        aux, auxb = a16.take("aux", 8 * G)
        qx, qxb = a16.take("qx", 8 * G)
        gts = [a16.take("gts%d" % i, 2 * G) for i in range(2)]
        pp = [a16.take("ppx%d" % i, G) for i in range(2)]
        kxT, kxTb = a16.take("kxT", 8 * MEM)
        vx, vxb = a16.take("vx", 2 * D)
        memn, memnb = a16.take("memn", 8 * MEM)
        ons = xn
        onsb = xnb
        yns, ynsb = hid, hidb

        for c in range(8):
            S.dma("sp", memx[:, c * MEM:(c + 1) * MEM], memT_d[c * 128:(c + 1) * 128, :], f1b[0], writes=[memxb])
        mc = lambda c: memx[:, c * MEM:(c + 1) * MEM]
        rms_stats(mc, memxb, 8, MEM, sq_pair, ones_d, 7, rstd[:, 0:MEM], rstdb, 1e-6)
        norm_cast(mc, memxb, PV["mem_g"], rstd[:, 0:MEM], rstdb, lambda c: memn[:, c * MEM:(c + 1) * MEM], memnb)
        for dt_ in range(8):
            w, wb = ws.load(wxk_d[dt_], 1024)
            p = dt_ % 2
            for c in range(8):
                mm(p, PS[p][:, 0:MEM], w[:, c * 128:(c + 1) * 128], memn[:, c * MEM:(c + 1) * MEM], c == 0, c == 7, [wb, memnb])
            evac_copy(kxT[:, dt_ * MEM:(dt_ + 1) * MEM], PS[p][:, 0:MEM], [psb[p]], [kxTb])
        for blk in range(2):
            w, wb = ws.load(wxv_d[blk], 4096)
            for kt in range(2):
                p = 2 + kt
                for c in range(8):
                    mm(p, PS[p][:, :], memn[:, c * MEM + kt * 128:c * MEM + (kt + 1) * 128], w[:, c * 512:(c + 1) * 512],
                       c == 0, c == 7, [wb, memnb])
                evac_copy(vx[:, kt * D + blk * 512:kt * D + (blk + 1) * 512], PS[p][:, :], [psb[p]], [vxb])

        def proj_norm_resid(w_d, src, srcb, gpost, nck=8):
            for d in range(8):
                w, wb = ws.load(w_d[d], nck * 128)
                p = 4 + d % 2
                for c in range(nck):
                    mm(p, PS[p][:, :], w[:, c * 128:(c + 1) * 128], src[:, c * G:(c + 1) * G], c == 0, c == nck - 1, [wb, srcb])
                sq_ap, sq_b = sq_pair[d % 2]
                S.op("dve", lambda e, o=f1[:, d * G:(d + 1) * G], i_=PS[p][:, :]: e.tensor_copy(out=o, in_=i_),
                     reads=[psb[p]], writes=[f1b[d]])
                S.op("act", lambda e, o=sq_ap, i_=f1[:, d * G:(d + 1) * G]: e.activation(out=o, in_=i_, func=AF.Square),
                     reads=[f1b[d]], writes=[sq_b])
                mm(6, PS[6][:, :], ones_d, sq_ap, d == 0, d == 7, [sq_b, c16buf])
            rsqrt_ps(6, G, rstd, rstdb, 1e-6, 1.0)
            resid_add(xT, xTb, f1, f1b, rstd, rstdb, gpost)

        gi = [0]
        for g in range(NG_):
            t0 = g * G
            for c in range(8):
                S.dma("sp", ons[:, c * G:(c + 1) * G], ON_d[c * 128:(c + 1) * 128, t0:t0 + G], onsb[c], reads=[ON_b[g]], writes=[onsb[c]])
            for c in range(16):
                S.dma("sp", yns[:, c * G:(c + 1) * G], YN_d[c * 128:(c + 1) * 128, t0:t0 + G], ynsb, reads=[YN_b[g]], writes=[ynsb])
            for c in range(8):
                S.dma("sp", xT[:, c * G:(c + 1) * G], XS_d[c * 128:(c + 1) * 128, t0:t0 + G], xTb[c], reads=[XS_b[g]], writes=[xTb[c]])
            for d in range(8):
                wa, wab = ws.load(wa_r[d], 1024)
                wsd, wsb = ws.load(ws_r[d], 2048)
                pa, ps_ = (0, 1) if d % 2 == 0 else (2, 3)
                for c in range(8):
                    mm(pa, PS[pa][:, :], wa[:, c * 128:(c + 1) * 128], ons[:, c * G:(c + 1) * G], c == 0, c == 7, [wab, onsb[c]])
                for c in range(16):
                    mm(ps_, PS[ps_][:, :], wsd[:, c * 128:(c + 1) * 128], yns[:, c * G:(c + 1) * G], c == 0, c == 15, [wsb, ynsb])
                gt_ap, gt_b = gts[gi[0] % 2]
                gi[0] += 1
                S.dma("sp", gt_ap[:, 0:G], GT_d[d * 128:(d + 1) * 128, t0:t0 + G], gt_b, reads=[GT_b[g]], writes=[gt_b])
                S.dma("sp", gt_ap[:, G:2 * G], GT_d[D + d * 128:D + (d + 1) * 128, t0:t0 + G], gt_b, reads=[GT_b[g]], writes=[gt_b])
                ta, tab = tmp32[0]
                tb, tbb = tmp32[1]
                S.op("dve", lambda e, pa=pa, g_=gt_ap[:, 0:G]: e.tensor_tensor(out=ta, in0=PS[pa][:, :], in1=g_, op=ALU.mult),
                     reads=[psb[pa], gt_b], writes=[tab])
                S.op("dve", lambda e, ps_=ps_, g_=gt_ap[:, G:2 * G]: e.tensor_tensor(out=tb, in0=PS[ps_][:, :], in1=g_, op=ALU.mult),
                     reads=[psb[ps_], gt_b], writes=[tbb])
                S.op("dve", lambda e, o=aux[:, d * G:(d + 1) * G]: e.tensor_tensor(out=o, in0=ta, in1=tb, op=ALU.add),
                     reads=[tab, tbb], writes=[auxb])
            proj_norm_resid(wmix_r, aux, auxb, PV["mix_post"])
            xc_ = lambda c: xT[:, c * G:(c + 1) * G]
            rms_stats(xc_, xTb, 8, G, sq_pair, ones_d, 7, rstd, rstdb, 1e-6)
            norm_cast(xc_, xTb, PV["xa_pre"], rstd, rstdb, lambda c: xn[:, c * G:(c + 1) * G], xnb)
            for d in range(8):
                w, wb = ws.load(wxq_r[d], 1024)
                p = d % 2
                for c in range(8):
                    mm(p, PS[p][:, :], w[:, c * 128:(c + 1) * 128], xn[:, c * G:(c + 1) * G], c == 0, c == 7, [wb, xnb[c]])
                evac_copy(qx[:, d * G:(d + 1) * G], PS[p][:, :], [psb[p]], [qxb])
            for hd in range(4):
                for kt in range(2):
                    p = 4 + kt
                    for dd in range(2):
                        dt_ = hd * 2 + dd
                        mm(p, PS[p][:, :], kxT[:, dt_ * MEM + kt * 128:dt_ * MEM + (kt + 1) * 128], qx[:, dt_ * G:(dt_ + 1) * G],
                           dd == 0, dd == 1, [kxTb, qxb])
                    p_ap, p_b = pp[kt]
                    S.op("act", lambda e, o=p_ap, p=p: e.activation(out=o, in_=PS[p][:, :], func=AF.Exp, scale=1.0 / 16.0),
                         reads=[psb[p]], writes=[p_b])
                for kt in range(2):
                    p_ap, p_b = pp[kt]
                    for e_ in range(2):
                        mm(e_, PS[e_][:, :], vx[:, kt * D + hd * 256 + e_ * 128:kt * D + hd * 256 + (e_ + 1) * 128], p_ap,
                           kt == 0, kt == 1, [vxb, p_b])
                    mm(2, PS[2][:, :], ones_1, p_ap, kt == 0, kt == 1, [c16buf, p_b])
                ta, tab = tmp32[0]
                S.op("dve", lambda e: e.reciprocal(out=ta, in_=PS[2][:, :]), reads=[psb[2]], writes=[tab])
                for e_ in range(2):
                    S.op("dve", lambda e, e_=e_, o=aux[:, (hd * 2 + e_) * G:(hd * 2 + e_ + 1) * G]:
                         e.tensor_tensor(out=o, in0=PS[e_][:, :], in1=ta, op=ALU.mult), reads=[psb[e_], tab], writes=[auxb])
            proj_norm_resid(wxo_r, aux, auxb, PV["xa_post"])
            ffn(a32, a16, ws, xT, xTb, xn, xnb, hid, hidb, f1, f1b, sq_pair, rstd, rstdb, tmp32, wgu_r[1], wdn_r[1],
                PV["ffn2_pre"], PV["ffn2_post"])
            for c in range(8):
                S.dma("sp", out_d[c * 128:(c + 1) * 128, t0:t0 + G], xT[:, c * G:(c + 1) * G], xTb[c], reads=[xTb[c]], writes=[])
        end_phase(a32, a16)

    if "ab" in stages:
        phase_ab()
    if "d" in stages:
        phase_d()
    if "c" in stages:
        phase_c()
    if "e" in stages:
        phase_e()

    finals = []
    for lst in (XS_b, Q_b, K_b, V_b, Z_b, XBC_b, DT_b, GT_b, ON_b, YN_b):
        pass
    allb = []
    seen = set()
    for e in S.ENGS:
        for o in S.ops[e]:
            if o.dinc is not None and id(o.dinc) not in seen:
                seen.add(id(o.dinc))
                allb.append(o.dinc)
    S.emit(allb)
    return nc


def _tile_w(W):
    K, N = W.shape
    return np.ascontiguousarray(W.reshape(K // 128, 128, N // 128, 128).transpose(2, 1, 0, 3).reshape(N // 128, 128, K))


def _blk_w(W, nb=512):
    K, N = W.shape
    return np.ascontiguousarray(
        W.reshape(K // 128, 128, N // nb, nb).transpose(2, 1, 0, 3).reshape(N // nb, 128, (K // 128) * nb))


def _col(v):
    return np.ascontiguousarray(np.asarray(v).reshape(-1, 128).T)


def _consts():
    c = np.zeros((128, NCONST), np.float32)
    c[:, 0:128] = np.eye(128, dtype=np.float32)
    s = np.arange(128)
    c[:, 128:256] = (s[:, None] <= s[None, :]).astype(np.float32)
    q = np.arange(512)
    for m in range(4):
        c[:, 256 + m * 512:256 + (m + 1) * 512] = (q[None, :] >= 128 * m + s[:, None]).astype(np.float32)
    return c


def prep_shared(inp):
    f = lambda k: np.asarray(inp[k], np.float32)[0]
    sh = {}
    for i, n in ((1, "ffn1"), (2, "ffn2")):
        wgu = f(n + "_w_gu")
        sh["wgu%d" % i] = np.ascontiguousarray(
            wgu.reshape(8, 128, 2, NFF, 128).transpose(3, 1, 2, 0, 4).reshape(NFF, 128, 2048))
        sh["wdn%d" % i] = _tile_w(f(n + "_w_down"))
    win = f("w_in")
    q, k, v, z, xbc, dt, gt = np.split(win, np.cumsum([1024, 1024, 1024, 2048, 3072, 32])[:6], axis=1)
    sh["winf"] = np.concatenate([_tile_w(q), _tile_w(k), _tile_w(xbc), _tile_w(gt)], axis=0)
    sh["wint"] = np.concatenate([_blk_w(v), _blk_w(z)], axis=0)
    sh["wdt"] = np.ascontiguousarray(dt.reshape(8, 128, 32).transpose(1, 0, 2).reshape(128, 256))
    sh["wa"] = _tile_w(f("w_branch_attn"))
    sh["ws"] = _tile_w(f("w_branch_ssd"))
    sh["wmix"] = _tile_w(f("w_mix_out"))
    sh["wxq"] = _tile_w(f("xa_w_q"))
    kv = f("xa_w_kv")
    sh["wxk"] = _tile_w(kv[:, :1024])
    sh["wxv"] = _blk_w(kv[:, 1024:])
    sh["wxo"] = _tile_w(f("xa_w_o"))
    cw = f("ssd_conv_w")
    convw = np.ascontiguousarray(cw.reshape(4, 24, 128).transpose(2, 1, 0).reshape(128, 96))
    pvec = np.concatenate([_col(f("ffn1_pre_g")), _col(f("ffn1_post_g")), _col(f("mix_pre_g")), _col(f("mix_post_g")),
                           _col(f("xa_pre_g")), _col(f("xa_post_g")), _col(f("mem_norm_g")), _col(f("ffn2_pre_g")),
                           _col(f("ffn2_post_g")), _col(f("b_gate")), _col(f("da_subln_g")), convw,
                           _col(f("ssd_conv_b")), _col(f("ssd_norm_g"))], axis=1).astype(np.float32)
    assert pvec.shape == (128, NPV), pvec.shape
    sh["pvec"] = np.ascontiguousarray(pvec)
    row = np.concatenate([f("ssd_dt_bias"), f("ssd_A_log"), f("ssd_D"), f("da_lambda_q1"), f("da_lambda_k1"),
                          f("da_lambda_q2"), f("da_lambda_k2")])[None, :].astype(np.float32)
    assert row.shape == (1, NROW)
    sh["row"] = np.ascontiguousarray(row)
    sh["consts"] = _consts()
    return sh


def core_inputs(x, mem, b, r, T, TP):
    m = {}
    m["xT"] = np.ascontiguousarray(x[b, r * T:(r + 1) * T].T)
    m["xTp"] = np.ascontiguousarray(x[b, 0:TP].T)
    fl = np.zeros((128, 2), np.float32)
    fl[:, 0] = 1.0 if r else 0.0
    fl[:, 1] = 0.0 if r else NEG
    m["flags"] = fl
    m["memT"] = np.ascontiguousarray(mem[b].T)
    return m


def kernel(**inputs):
    x = np.asarray(inputs["x"], np.float32)
    mem = np.asarray(inputs["mem"], np.float32)
    B, SEQ_, _ = x.shape
    T = SEQ_ // 2
    sh = prep_shared(inputs)
    nc = build_program(T, T)
    in_maps = []
    for c in range(8):
        m = dict(sh)
        m.update(core_inputs(x, mem, c // 2, c % 2, T, T))
        in_maps.append(m)
    res = run_bass_kernel_spmd(nc, in_maps, core_ids=list(range(8)))
    out = np.empty((B, SEQ_, D), np.float32)
    for c in range(8):
        out[c // 2, (c % 2) * T:(c % 2 + 1) * T] = res.results[c]["outT"].T
    return out
```
